# Optimizing a Trainium2 kernel written in Bass

```python
import math
import jax, jax.numpy as jnp
from jax import lax
import numpy as np

D_MODEL = 2048
BATCH = 4
SEQ = 4096
DEPTH = 4

HEAD_DIM = 128
ROPE_THETA = 10000.0
NORM_EPS = 1e-6

A_Q_HEADS = 8
A_KV_HEADS = 2
A_GROUP = A_Q_HEADS // A_KV_HEADS
A_RADIUS = 128
A_BLOCK = 128

B_PATTERNS = ((128, 1), (512, 4), (2048, 16))
B_GROUPS = len(B_PATTERNS)
B_HEADS_PER_GROUP = 4
B_HEADS = B_GROUPS * B_HEADS_PER_GROUP
B_BLOCK = 64

C_HEADS = 8
GRID_W = 64
C_WIN_ROWS = 8
C_WIN_COLS = 16

N_BRANCH = 3
A_Q_W = A_Q_HEADS * HEAD_DIM
A_KV_W = A_KV_HEADS * HEAD_DIM
B_W = B_HEADS * HEAD_DIM
B_OUT_W = B_HEADS_PER_GROUP * HEAD_DIM
C_W = C_HEADS * HEAD_DIM
IN_WIDTHS = (A_Q_W, A_KV_W, A_KV_W, B_W, B_W, B_W, C_W, C_W, C_W, N_BRANCH * D_MODEL)
N_IN = sum(IN_WIDTHS)

D_FF = ((8 * D_MODEL + 3 * 256 - 1) // (3 * 256)) * 256

kernel_name = 'hybrid_gated_local_dilated_grid_attention_encoder'


def rmsnorm(x, g):
    x32 = x.astype(jnp.float32)
    y = x32 * lax.rsqrt(jnp.mean(x32 * x32, axis=-1, keepdims=True) + NORM_EPS)
    return (y * g.astype(jnp.float32)).astype(x.dtype)


def rope_tables(n):
    half = HEAD_DIM // 2
    inv_freq = ROPE_THETA ** (-jnp.arange(half, dtype=jnp.float32) * 2.0 / HEAD_DIM)
    ang = jnp.arange(n, dtype=jnp.float32)[:, None] * inv_freq[None, :]
    return jnp.cos(ang), jnp.sin(ang)


def apply_rope(x, cos, sin):
    half = HEAD_DIM // 2
    x32 = x.astype(jnp.float32)
    x1, x2 = x32[..., :half], x32[..., half:]
    return jnp.concatenate([x1 * cos - x2 * sin, x2 * cos + x1 * sin], axis=-1).astype(x.dtype)


def banded_attention(q, k, v, radius, block, sink=None):
    bsz, hk, grp, n, dh = q.shape
    nb = n // block
    width = block + 2 * radius
    pad = ((0, 0), (0, 0), (radius, radius), (0, 0))
    idx = jnp.arange(nb)[:, None] * block + jnp.arange(width)[None, :]
    kb = jnp.pad(k, pad)[:, :, idx]
    vb = jnp.pad(v, pad)[:, :, idx]
    qb = q.reshape(bsz, hk, grp, nb, block, dh)
    s = jnp.einsum('bhgnqd,bhnkd->bhgnqk', qb, kb, preferred_element_type=jnp.float32) * (dh ** -0.5)
    qpos = (jnp.arange(nb)[:, None] * block + jnp.arange(block)[None, :])[:, :, None]
    kpos = (idx - radius)[:, None, :]
    valid = (jnp.abs(kpos - qpos) <= radius) & (kpos >= 0) & (kpos < n)
    s = jnp.where(valid, s, -jnp.inf)
    m = jnp.max(s, axis=-1, keepdims=True)
    if sink is not None:
        sk = sink.astype(jnp.float32).reshape(1, hk, grp, 1, 1, 1)
        m = jnp.maximum(m, sk)
    p = jnp.exp(s - m)
    denom = jnp.sum(p, axis=-1, keepdims=True)
    if sink is not None:
        denom = denom + jnp.exp(sk - m)
    o = jnp.einsum('bhgnqk,bhnkd->bhgnqd', (p / denom).astype(v.dtype), vb)
    lse = (m + jnp.log(denom))[..., 0]
    return o.reshape(bsz, hk, grp, n, dh), lse.reshape(bsz, hk, grp, n)


def mixer_a(qa, ka, va, cos, sin, gq, gk, sink):
    bsz, n, _ = qa.shape
    q = qa.reshape(bsz, n, A_KV_HEADS, A_GROUP, HEAD_DIM).transpose(0, 2, 3, 1, 4)
    k = ka.reshape(bsz, n, A_KV_HEADS, HEAD_DIM).transpose(0, 2, 1, 3)
    v = va.reshape(bsz, n, A_KV_HEADS, HEAD_DIM).transpose(0, 2, 1, 3)
    q = apply_rope(rmsnorm(q, gq), cos, sin)
    k = apply_rope(rmsnorm(k, gk), cos, sin)
    o, _ = banded_attention(q, k, v, A_RADIUS, math.gcd(n, A_BLOCK),
                            sink.reshape(A_KV_HEADS, A_GROUP))
    return o.transpose(0, 3, 1, 2, 4).reshape(bsz, n, A_Q_W)


def to_residue_classes(t, dil):
    bsz, h, n, dh = t.shape
    return t.reshape(bsz, h, n // dil, dil, dh).transpose(0, 1, 3, 2, 4).reshape(bsz, h * dil, n // dil, dh)


def mixer_b(qb, kb, vb, cos, sin, gq, gk):
    bsz, n, _ = qb.shape
    hg = B_HEADS_PER_GROUP

    def heads(t):
        return t.reshape(bsz, n, B_GROUPS, hg, HEAD_DIM).transpose(0, 2, 3, 1, 4)

    q = apply_rope(rmsnorm(heads(qb), gq), cos, sin)
    k = apply_rope(rmsnorm(heads(kb), gk), cos, sin)
    v = heads(vb)
    outs, lses = [], []
    for g, (window, dil) in enumerate(B_PATTERNS):
        m = n // dil
        o, lse = banded_attention(to_residue_classes(q[:, g], dil)[:, :, None],
                                  to_residue_classes(k[:, g], dil),
                                  to_residue_classes(v[:, g], dil),
                                  window // (2 * dil), math.gcd(m, B_BLOCK))
        o = o[:, :, 0].reshape(bsz, hg, dil, m, HEAD_DIM).transpose(0, 1, 3, 2, 4).reshape(bsz, hg, n, HEAD_DIM)
        lse = lse[:, :, 0].reshape(bsz, hg, dil, m).transpose(0, 1, 3, 2).reshape(bsz, hg, n)
        outs.append(o)
        lses.append(lse)
    o = jnp.stack(outs, axis=1)
    lse = jnp.stack(lses, axis=1)
    w = jax.nn.softmax(lse, axis=1)
    out = jnp.einsum('bghl,bghld->bhld', w.astype(o.dtype), o)
    return out.transpose(0, 2, 1, 3).reshape(bsz, n, B_OUT_W)


def mixer_c(qc, kc, vc, gq, gk, rpb):
    bsz, n, _ = qc.shape
    rows = n // GRID_W
    wr = min(C_WIN_ROWS, rows)

    def grid(t):
        return t.reshape(bsz, rows, GRID_W, C_HEADS, HEAD_DIM).transpose(0, 3, 1, 2, 4)

    q = rmsnorm(grid(qc), gq)
    k = rmsnorm(grid(kc), gk)
    v = grid(vc)
    r = jnp.arange(rows)
    row_start = jnp.clip(r - wr // 2, 0, rows - wr)
    krow = row_start[:, None] + jnp.arange(wr)[None, :]
    kg = k[:, :, krow]
    vg = v[:, :, krow]
    s = jnp.einsum('bhrcd,bhrwkd->bhrcwk', q, kg, preferred_element_type=jnp.float32) * (HEAD_DIM ** -0.5)
    cq = jnp.arange(GRID_W)
    col_start = jnp.clip(cq - C_WIN_COLS // 2, 0, GRID_W - C_WIN_COLS)
    col_ok = (cq[None, :] >= col_start[:, None]) & (cq[None, :] < col_start[:, None] + C_WIN_COLS)
    drow = krow - r[:, None]
    dcol = jnp.clip(cq[None, :] - cq[:, None], -(C_WIN_COLS - 1), C_WIN_COLS - 1)
    bias = rpb[:, drow[:, None, :, None] + (C_WIN_ROWS - 1), dcol[None, :, None, :] + (C_WIN_COLS - 1)]
    s = s + bias[None].astype(jnp.float32)
    s = jnp.where(col_ok[:, None, :], s, -jnp.inf)
    p = jax.nn.softmax(s.reshape(bsz, C_HEADS, rows, GRID_W, wr * GRID_W), axis=-1).reshape(s.shape)
    o = jnp.einsum('bhrcwk,bhrwkd->bhrcd', p.astype(v.dtype), vg)
    return o.transpose(0, 2, 3, 1, 4).reshape(bsz, n, C_W)


def setup_inputs(seed: int = 0) -> dict:
    key = jax.random.key(seed)
    ks = jax.random.split(key, 14)
    f32 = jnp.float32

    def w(k, shape, fan_in):
        return jax.random.normal(k, shape, f32) * (fan_in ** -0.5)

    return {
        'x': jax.random.normal(ks[0], (BATCH, SEQ, D_MODEL), f32),
        'norm1_g': 1.0 + 0.02 * jax.random.normal(ks[1], (DEPTH, D_MODEL), f32),
        'w_in': w(ks[2], (DEPTH, D_MODEL, N_IN), D_MODEL),
        'qk_norm_g': 1.0 + 0.02 * jax.random.normal(ks[3], (DEPTH, 6, HEAD_DIM), f32),
        'sink_a': jax.random.normal(ks[4], (DEPTH, A_Q_HEADS), f32),
        'rpb_c': 0.1 * jax.random.normal(ks[5], (DEPTH, C_HEADS, 2 * C_WIN_ROWS - 1, 2 * C_WIN_COLS - 1), f32),
        'w_br_a': w(ks[6], (DEPTH, A_Q_W, D_MODEL), A_Q_W),
        'w_br_b': w(ks[7], (DEPTH, B_OUT_W, D_MODEL), B_OUT_W),
        'w_br_c': w(ks[8], (DEPTH, C_W, D_MODEL), C_W),
        'w_o': w(ks[9], (DEPTH, D_MODEL, D_MODEL), D_MODEL),
        'norm2_g': 1.0 + 0.02 * jax.random.normal(ks[10], (DEPTH, D_MODEL), f32),
        'w_gate_up': w(ks[11], (DEPTH, D_MODEL, 2 * D_FF), D_MODEL),
        'w_down': w(ks[12], (DEPTH, D_FF, D_MODEL), D_FF),
    }


def reference(x, norm1_g, w_in, qk_norm_g, sink_a, rpb_c, w_br_a, w_br_b, w_br_c, w_o,
              norm2_g, w_gate_up, w_down):
    bsz, n, _ = x.shape
    cos, sin = rope_tables(n)
    split_points = []
    acc = 0
    for wdt in IN_WIDTHS[:-1]:
        acc += wdt
        split_points.append(acc)
    for i in range(DEPTH):
        h = rmsnorm(x, norm1_g[i])
        proj = h @ w_in[i]
        qa, ka, va, qb, kb, vb, qc, kc, vc, gl = jnp.split(proj, split_points, axis=-1)
        g = qk_norm_g[i]
        oa = mixer_a(qa, ka, va, cos, sin, g[0], g[1], sink_a[i])
        ob = mixer_b(qb, kb, vb, cos, sin, g[2], g[3])
        oc = mixer_c(qc, kc, vc, g[4], g[5], rpb_c[i])
        gates = jax.nn.sigmoid(gl.astype(jnp.float32)).astype(x.dtype).reshape(bsz, n, N_BRANCH, D_MODEL)
        merged = (gates[:, :, 0] * (oa @ w_br_a[i])
                  + gates[:, :, 1] * (ob @ w_br_b[i])
                  + gates[:, :, 2] * (oc @ w_br_c[i]))
        x = x + merged @ w_o[i]
        h2 = rmsnorm(x, norm2_g[i])
        gt, up = jnp.split(h2 @ w_gate_up[i], 2, axis=-1)
        x = x + (jax.nn.silu(gt) * up) @ w_down[i]
    return x
```

```python
import contextlib
import numpy as np
import concourse.bass as bass
import concourse.mybir as mybir
from concourse.bass_utils import run_bass_kernel_spmd

F32 = mybir.dt.float32
BF16 = mybir.dt.bfloat16
AF = mybir.ActivationFunctionType
ALU = mybir.AluOpType

D = 2048
T = 2048
SEQ = 4096
NL = 4
NIN = 15360
DFF = 5632
NFFC = 44
QUARTERS = ((0, 12), (12, 12), (24, 10), (34, 10))
EPS = 1e-6
SCALE = 128.0 ** -0.5
NEG = -30000.0

C_QA, C_KA, C_VA, C_QB, C_KB, C_VB, C_QC, C_KC, C_VC, C_G = 0, 1024, 1280, 1536, 3072, 4608, 6144, 7168, 8192, 9216

O_KA = 0
O_VA = 524288
O_KB = 1048576
O_VB = 4194304
O_KC = 7340032
O_VC = 9437184
NKV = 11534336

ENGS = ("pe", "act", "dve", "pool", "sp")


class Tile:
    __slots__ = ("w", "r")

    def __init__(self):
        self.w = None
        self.r = []


class Op:
    __slots__ = ("eng", "fn", "deps", "need_inc", "semval", "kind", "dsem", "dval", "pos")

    def __init__(self, eng, fn, kind):
        self.eng = eng
        self.fn = fn
        self.deps = set()
        self.need_inc = False
        self.semval = None
        self.kind = kind
        self.dsem = None
        self.dval = None


class Prog:
    N_DMA_SEM = {"sp": 12, "act": 16, "pool": 16}
    EPOCH = 30000

    def __init__(self, nc):
        self.nc = nc
        self.ops = {e: [] for e in ENGS}
        self.dma_ops = {e: [] for e in ENGS}
        self.cc_ops = []

    def op(self, eng, fn, reads=(), writes=(), kind=None):
        o = Op(eng, fn, kind)
        deps = o.deps
        for t in reads:
            if t.w is not None:
                deps.add(t.w)
        for t in writes:
            if t.w is not None:
                deps.add(t.w)
            if t.r:
                deps.update(t.r)
        for t in reads:
            t.r.append(o)
        for t in writes:
            t.w = o
            t.r = []
        if kind == "dma":
            lst = self.dma_ops[eng]
            n = self.N_DMA_SEM[eng]
            k = len(lst)
            o.dsem = k % n
            o.dval = 16 * (k // n + 1)
            if k >= n:
                deps.add(lst[k - n])
            lst.append(o)
        elif kind == "cc":
            o.dsem = len(self.cc_ops)
            o.dval = 1
            self.cc_ops.append(o)
        deps.discard(o)
        o.pos = len(self.ops[eng])
        self.ops[eng].append(o)
        return o

    def finish(self):
        nc = self.nc
        for e in ENGS:
            for o in self.ops[e]:
                last = {}
                for d in o.deps:
                    if d.kind is not None:
                        continue
                    if d.eng == "pe" and e == "pe":
                        continue
                    if d.eng not in last or last[d.eng].pos < d.pos:
                        last[d.eng] = d
                for d in last.values():
                    d.need_inc = True
        n_ep = {}
        for e in ENGS:
            c = 0
            for o in self.ops[e]:
                if o.kind is None and o.need_inc:
                    c += 1
                    o.semval = c
            n_ep[e] = c // self.EPOCH + 1
        sems = {}
        for e in ENGS:
            if not self.ops[e]:
                continue
            for ep in range(n_ep[e]):
                sems[(e, ep)] = nc.alloc_semaphore(name=f"s_{e}_{ep}")
        dsems = {}
        for e, n in self.N_DMA_SEM.items():
            if self.dma_ops[e]:
                for i in range(n):
                    dsems[(e, i)] = nc.alloc_semaphore(name=f"d_{e}_{i}")
        csems = [nc.alloc_semaphore(name=f"cc_{i}") for i in range(len(self.cc_ops))]
        EP = self.EPOCH

        def semof(o):
            if o.kind == "dma":
                return dsems[(o.eng, o.dsem)], o.dval
            if o.kind == "cc":
                return csems[o.dsem], 1
            ep = (o.semval - 1) // EP
            return sems[(o.eng, ep)], o.semval - ep * EP

        plans = {}
        for e in ENGS:
            seen_eng = {f: 0 for f in ENGS}
            seen_x = {}
            plan = []
            for o in self.ops[e]:
                best = {}
                xw = {}
                for d in o.deps:
                    if d.kind is not None:
                        key = (d.kind, d.eng, d.dsem)
                        if seen_x.get(key, 0) < d.dval:
                            if key not in xw or xw[key].dval < d.dval:
                                xw[key] = d
                    else:
                        if d.eng == "pe" and e == "pe":
                            continue
                        if d.eng not in best or best[d.eng].pos < d.pos:
                            best[d.eng] = d
                waits = []
                for f, d in best.items():
                    if d.semval > seen_eng[f]:
                        seen_eng[f] = d.semval
                        waits.append(d)
                for key, d in xw.items():
                    seen_x[key] = d.dval
                    waits.append(d)
                plan.append((o, waits))
            plans[e] = plan

        with nc.Block() as block:
            def make(e):
                def body(eng):
                    for o, waits in plans[e]:
                        for d in waits:
                            s, v = semof(d)
                            eng.wait_ge(s, v)
                        ins = o.fn(eng)
                        if o.kind == "dma":
                            s, v = semof(o)
                            ins.then_inc(s, 16)
                        elif o.kind == "cc":
                            s, v = semof(o)
                            ins.then_inc(s)
                        elif o.need_inc:
                            s, v = semof(o)
                            ins.then_inc(s, 1)
                return body
            if plans["sp"]:
                block.sync(make("sp"))
            if plans["pe"]:
                block.tensor(make("pe"))
            if plans["act"]:
                block.scalar(make("act"))
            if plans["dve"]:
                block.vector(make("dve"))
            if plans["pool"]:
                block.gpsimd(make("pool"))


class Cyc:
    def __init__(self, items):
        self.items = list(items)
        self.i = 0

    def next(self):
        v = self.items[self.i % len(self.items)]
        self.i += 1
        return v


class Buf:
    __slots__ = ("ap", "tiles")

    def __init__(self, ap, tiles):
        self.ap = ap
        self.tiles = tiles


class Work:
    def __init__(self, tensor, nbytes):
        self.t = tensor
        self.n = nbytes
        self.tiles = [Tile() for _ in range(nbytes // 1024)]
        self.off = 0

    def reset(self):
        self.off = 0

    def alloc(self, nbytes, dtype=BF16):
        sz = (nbytes + 1023) // 1024 * 1024
        off = self.off
        assert off + sz <= self.n, ("work overflow", off, sz, self.n)
        self.off += sz
        ap = self.t[:, off // 2:(off + nbytes) // 2]
        if dtype == F32:
            ap = ap.bitcast(F32)
        return Buf(ap, self.tiles[off // 1024:(off + sz) // 1024])


def _rope_tables():
    half = 64
    inv = (10000.0 ** (-np.arange(half, dtype=np.float32) * 2.0 / 128.0)).astype(np.float32)
    ang = np.arange(SEQ, dtype=np.float32)[:, None] * inv[None, :]
    return np.cos(ang).astype(np.float32), np.sin(ang).astype(np.float32)


def _rope_core(par, cos, sin):
    out = np.zeros((3, 2, 128, T), np.float32)
    lt = np.arange(T)
    orders = [lt,
              (np.arange(T) % 512) * 4 + np.arange(T) // 512,
              (np.arange(T) % 128) * 16 + np.arange(T) // 128]
    for v, o in enumerate(orders):
        g = par * T + o
        c = cos[g].T
        s = sin[g].T
        out[v, 0] = np.concatenate([c, c], 0)
        out[v, 1] = np.concatenate([s, s], 0)
    return out


def _mask_tables():
    a = np.arange(128)
    ab_tiles = [[], []]
    ab_sig = {}

    def add_ab(tiles2):
        key = b"".join(t.tobytes() for p in range(2) for t in tiles2[p])
        if key not in ab_sig:
            ab_sig[key] = len(ab_tiles[0])
            for p in range(2):
                ab_tiles[p].extend(tiles2[p])
        return ab_sig[key]

    sigA = []
    for j in range(16):
        t2 = [[], []]
        for p in range(2):
            for dl in (-1, 0, 1):
                q = p * T + 128 * j + a
                k = p * T + 128 * (j + dl) + a
                v = (np.abs(q[None, :] - k[:, None]) <= 128) & (k[:, None] >= 0) & (k[:, None] < SEQ)
                t2[p].append(v.astype(np.float32))
        sigA.append(add_ab(t2))
    sigB = []
    for g, dil in enumerate((1, 4, 16)):
        nb = T // dil // 128
        mtot = SEQ // dil
        row = []
        for j in range(nb):
            t2 = [[], []]
            for p in range(2):
                for dl in (-1, 0, 1):
                    q = p * (T // dil) + 128 * j + a
                    k = p * (T // dil) + 128 * (j + dl) + a
                    v = (np.abs(q[None, :] - k[:, None]) <= 64) & (k[:, None] >= 0) & (k[:, None] < mtot)
                    t2[p].append(v.astype(np.float32))
            row.append(add_ab(t2))
        sigB.append(row)
    maskAB = [np.stack(ab_tiles[p]) for p in range(2)]

    c_valid = [[], []]
    c_dr = [[], []]
    c_dc = [[], []]
    c_sig = {}
    sigC = []
    for j in range(16):
        per = []
        for dl in range(-3, 4):
            c = j + dl
            if c < -2 or c > 17:
                continue
            vs, drs, dcs = [], [], []
            for p in range(2):
                q = p * T + 128 * j + a
                k = p * T + 128 * c + a
                qr, qc = q // 64, q % 64
                kr, kc = k // 64, k % 64
                rs = np.clip(qr - 4, 0, 56)
                cs = np.clip(qc - 8, 0, 48)
                v = ((k[:, None] >= 0) & (k[:, None] < SEQ)
                     & (kr[:, None] >= rs[None, :]) & (kr[:, None] < rs[None, :] + 8)
                     & (kc[:, None] >= cs[None, :]) & (kc[:, None] < cs[None, :] + 16))
                dr = np.clip(kr[:, None] - qr[None, :] + 7, 0, 14)
                dc = np.clip(kc[:, None] - qc[None, :], -15, 15) + 15
                vs.append(v)
                drs.append(np.where(v, dr, 0))
                dcs.append(np.where(v, dc, 0))
            if not (vs[0].any() or vs[1].any()):
                continue
            per.append((dl, vs, drs, dcs))
        key = (tuple(x[0] for x in per),
               b"".join(x[1][p].tobytes() + x[2][p].tobytes() + x[3][p].tobytes() for x in per for p in range(2)))
        if key not in c_sig:
            c_sig[key] = len(c_valid[0])
            for x in per:
                for p in range(2):
                    c_valid[p].append(x[1][p])
                    c_dr[p].append(x[2][p])
                    c_dc[p].append(x[3][p])
        sigC.append((c_sig[key], tuple(x[0] for x in per)))
    cmask = [np.stack(c_valid[p]) for p in range(2)]
    cdr = [np.stack(c_dr[p]) for p in range(2)]
    cdc = [np.stack(c_dc[p]) for p in range(2)]
    return maskAB, sigA, sigB, cmask, cdr, cdc, sigC


_TABLES = None


def _tables():
    global _TABLES
    if _TABLES is None:
        _TABLES = _mask_tables()
    return _TABLES


def build_program(nl, pairs, debug=False, stop=None):
    maskAB, sigA, sigB, cmask, cdr, cdc, sigC = _tables()
    n_ab = maskAB[0].shape[0]
    n_ct = cmask[0].shape[0]

    nc = bass.Bass("TRN2", target_bir_lowering=False)
    P = Prog(nc)

    def din(name, shape, dt=F32):
        return nc.dram_tensor(name, list(shape), dt, kind="ExternalInput").ap()

    def dscr(name, shape, dt, dbg=False):
        kind = "ExternalOutput" if (debug and dbg) else "Internal"
        return nc.dram_tensor(name, list(shape), dt, kind=kind).ap()

    x_in = din("x", [T, D])
    w_in = din("w_in", [nl, D, NIN])
    w_bra = din("w_br_a", [nl, 1024, D])
    w_brb = din("w_br_b", [nl, 512, D])
    w_brc = din("w_br_c", [nl, 1024, D])
    w_o = din("w_o", [nl, D, D])
    w_gu = din("w_gate_up", [nl, D, 2 * DFF])
    w_dn = din("w_down", [nl, DFF, D])
    g1_in = din("g1", [128, nl * 16])
    g2_in = din("g2", [128, nl * 16])
    qkg_in = din("qkg", [128, nl * 6])
    sink_in = din("sink", [128, nl * 8])
    rope_in = din("rope", [3, 2, 128, T])
    mab_in = din("maskab", [128, n_ab * 128])
    cb_in = din("cbias", [nl, 8, 128, n_ct * 128])
    ident_in = din("ident", [128, 128])
    rot_in = din("rot", [128, 128])
    out = nc.dram_tensor("out", [T, D], F32, kind="ExternalOutput").ap()

    xres = dscr("xres", [16, 128, T], F32, dbg=True)
    qsc = dscr("qsc", [28, 128, T], BF16, dbg=True)
    gsc = dscr("gsc", [16, 128, 3, T], BF16, dbg=True)
    osc = dscr("osc", [20, 128, T], BF16, dbg=True)
    kv_own = nc.dram_tensor("kv_own", [NKV], BF16, kind="Internal").ap()
    MSG = {"a": 786432, "b": 1048576, "c": 655360, "d": 1048576, "e": 1048576}
    msg_src = {m: nc.dram_tensor("msg_src_" + m, [n], BF16, kind="Internal").ap() for m, n in MSG.items()}
    msg_dst = {m: nc.dram_tensor("msg_dst_" + m, [2 * n], BF16, addr_space="Local", kind="Internal").ap()
               for m, n in MSG.items()}

    def blk(flat, off, rows, cols):
        return flat[off:off + rows * cols].rearrange("(p t) -> p t", t=cols)

    def ms(m, off, rows, cols):
        return blk(msg_src[m], off, rows, cols)

    def md(m, rank, off, rows, cols):
        return blk(msg_dst[m], rank * MSG[m] + off, rows, cols)

    def kv_fm(base, off, ncols=T):
        return base[off:off + 128 * ncols].rearrange("(p t) -> p t", t=ncols)

    def v_hd(off, h):
        return kv_own[off + h * 262144:off + (h + 1) * 262144].rearrange("(p c d) -> p c d", c=16, d=128)

    def kv_tm(base, off, nrows, ncols):
        return base[off:off + nrows * ncols].rearrange("(r n) -> r n", n=ncols)

    es = contextlib.ExitStack()
    with es:
        def sb(name, shape, dt):
            return es.enter_context(nc.sbuf_tensor("sb_" + name, list(shape), dt))

        Hs = sb("H", [128, 16, T], BF16)
        H_t = [[Tile() for _ in range(4)] for _ in range(16)]
        NB = 3
        ring = [sb(f"ring{i}", [128, 8192], BF16) for i in range(NB)]
        ring_t = [Tile() for _ in range(NB)]
        WORK_BYTES = 84 * 1024
        work_s = sb("work", [128, WORK_BYTES // 2], BF16)
        work = Work(work_s, WORK_BYTES)
        ident = sb("ident_sb", [128, 128], F32)
        ones = sb("ones", [128, 128], BF16)
        rotm = sb("rotm", [128, 128], BF16)
        g1s = sb("g1s", [128, nl * 16], F32)
        g2s = sb("g2s", [128, nl * 16], F32)
        qkgs = sb("qkgs", [128, nl * 6], F32)
        sinke = sb("sinke", [128, nl * 8], F32)
        mab = sb("mab", [128, n_ab * 128], BF16)
        T_const = Tile()
        ps = [es.enter_context(nc.psum_tensor(f"ps{i}", [128, 512], F32)) for i in range(8)]
        ps_t = [Tile() for _ in range(8)]
        acc_pool = Cyc([0, 1, 2, 3])
        aux_pool = Cyc([4, 5, 6, 7])

        xres_t = [[Tile() for _ in range(4)] for _ in range(16)]
        qsc_t = [[Tile() for _ in range(4)] for _ in range(28)]
        gsc_t = [[[Tile() for _ in range(4)] for _ in range(3)] for _ in range(16)]
        osc_t = [[Tile() for _ in range(4)] for _ in range(20)]
        kvo_t = {}
        msg_t = {m: {} for m in MSG}
        msgd_t = {m: Tile() for m in MSG}

        def mt(m, key):
            if key not in msg_t[m]:
                msg_t[m][key] = Tile()
            return msg_t[m][key]
        out_t = []

        def kvt(key):
            if key not in kvo_t:
                kvo_t[key] = Tile()
            return kvo_t[key]

        def mm(o, lhsT, rhs, start, stop, rd, wr):
            P.op("pe", lambda e: e.matmul(o, lhsT, rhs, start=start, stop=stop), reads=rd, writes=wr)

        def tr(o, in_, rd, wr):
            P.op("pe", lambda e: e.transpose(o, in_, ident[:]), reads=rd + [T_const], writes=wr)

        def act(o, in_, func, rd, wr, scale=None, bias=None):
            if scale is None:
                P.op("act", lambda e: e.activation(o, in_, func), reads=rd, writes=wr)
            elif bias is None:
                P.op("act", lambda e: e.activation(o, in_, func, scale=scale), reads=rd, writes=wr)
            else:
                P.op("act", lambda e: e.activation(o, in_, func, bias=bias, scale=scale), reads=rd, writes=wr)

        def rstd(rs, src_ps, src_tile, n):
            act(rs.ap, src_ps, AF.Ln, [src_tile], rs.tiles, scale=1.0 / n, bias=float(EPS))
            act(rs.ap, rs.ap, AF.Exp, rs.tiles, rs.tiles, scale=-0.5)

        def recip_act(o_ap, o_tiles, in_ap, in_tiles, bias=None):
            if bias is None:
                act(o_ap, in_ap, AF.Ln, in_tiles, o_tiles)
            else:
                act(o_ap, in_ap, AF.Ln, in_tiles + [T_const], o_tiles, scale=1.0, bias=bias)
            act(o_ap, o_ap, AF.Exp, o_tiles, o_tiles, scale=-1.0)

        def tt(eng, o, a, b, op, rd, wr):
            P.op(eng, lambda e: e.tensor_tensor(o, a, b, op), reads=rd, writes=wr)

        def ts(eng, o, a, s1, s2, op0, op1, rd, wr):
            if op1 is None:
                P.op(eng, lambda e: e.tensor_scalar(o, a, s1, None, op0), reads=rd, writes=wr)
            else:
                P.op(eng, lambda e: e.tensor_scalar(o, a, s1, s2, op0, op1), reads=rd, writes=wr)

        def stt(o, a, s, b, op0, op1, rd, wr):
            P.op("dve", lambda e: e.scalar_tensor_tensor(o, a, s, b, op0, op1), reads=rd, writes=wr)

        def dma(eng, o, in_, rd, wr):
            P.op(eng, lambda e: e.dma_start(out=o, in_=in_), reads=rd, writes=wr, kind="dma")

        wspecs = []

        def wv(src2d, p=128):
            return src2d.rearrange("(k p) n -> p k n", p=p)

        IN_ORDER = ([("kv_a", C_KA)] + [("kb", C_KB + 512 * g) for g in range(3)]
                    + [("vb", C_VB + 512 * g) for g in range(3)]
                    + [("kc", C_KC), ("kc", C_KC + 512), ("vc", C_VC), ("vc", C_VC + 512)]
                    + [("qa", C_QA), ("qa", C_QA + 512)] + [("qb", C_QB + 512 * g) for g in range(3)]
                    + [("qc", C_QC), ("qc", C_QC + 512)] + [("gate", C_G + 512 * i) for i in range(12)])
        for l in range(nl):
            for tag, c0 in IN_ORDER:
                wspecs.append(("in", [((0, 16, 512), wv(w_in[l])[:, :, c0:c0 + 512])]))
            for wbr, nk in ((w_bra, 8), (w_brb, 4), (w_brc, 8)):
                for cg in range(4):
                    wspecs.append(("br", [((0, nk, 512), wv(wbr[l])[:, :, cg * 512:(cg + 1) * 512])]))
            for cg in range(4):
                wspecs.append(("wo", [((0, 16, 512), wv(w_o[l])[:, :, cg * 512:(cg + 1) * 512])]))
            for (j0, nq) in QUARTERS:
                for jj in range(0, nq, 2):
                    j = j0 + jj
                    wspecs.append(("gu", [((0, 16, 256), wv(w_gu[l])[:, :, j * 128:j * 128 + 256]),
                                          ((4096, 16, 256), wv(w_gu[l])[:, :, DFF + j * 128:DFF + j * 128 + 256])]))
                for cg in range(4):
                    wspecs.append(("dn", [((0, nq, 512), wv(w_dn[l])[:, j0:j0 + nq, cg * 512:(cg + 1) * 512])]))
        ws_state = {"issued": 0, "cons": 0}

        def ws_issue(n):
            tag, dmas = wspecs[n]
            slot = n % NB
            for (lo, k, ncol), src in dmas:
                dst = ring[slot][:, lo:lo + k * ncol].rearrange("p (k n) -> p k n", n=ncol)
                dma("pool", dst, src, [], [ring_t[slot]])

        def ws_next(tag):
            n = ws_state["cons"]
            assert wspecs[n][0] == tag, (wspecs[n][0], tag)
            while ws_state["issued"] < min(n + NB, len(wspecs)):
                ws_issue(ws_state["issued"])
                ws_state["issued"] += 1
            ws_state["cons"] += 1
            return n % NB

        work.reset()
        c_tmp = work.alloc(nl * 16 * 4, F32)
        dma("sp", ident[:], ident_in[:, :], [], [T_const])
        dma("pool", rotm[:], rot_in[:, :], [], [T_const])
        dma("pool", mab[:], mab_in[:, :], [], [T_const])
        P.op("dve", lambda e: e.memset(ones[:], 1.0), writes=[T_const])
        dma("sp", c_tmp.ap[:, 0:nl * 16], g1_in[:, :], [], c_tmp.tiles)
        ts("dve", g1s[:], c_tmp.ap[:, 0:nl * 16], 1.0, None, ALU.mult, None, c_tmp.tiles, [T_const])
        dma("sp", c_tmp.ap[:, 0:nl * 16], g2_in[:, :], [], c_tmp.tiles)
        ts("dve", g2s[:], c_tmp.ap[:, 0:nl * 16], 1.0, None, ALU.mult, None, c_tmp.tiles, [T_const])
        dma("sp", c_tmp.ap[:, 0:nl * 6], qkg_in[:, :], [], c_tmp.tiles)
        ts("dve", qkgs[:], c_tmp.ap[:, 0:nl * 6], 1.0, None, ALU.mult, None, c_tmp.tiles, [T_const])
        dma("sp", c_tmp.ap[:, 0:nl * 8], sink_in[:, :], [], c_tmp.tiles)
        act(sinke[:], c_tmp.ap[:, 0:nl * 8], AF.Exp, c_tmp.tiles, [T_const])

        SS = [4, 5, 6, 7]

        def normalize(gs, gcol0):
            work.reset()
            rsb = [work.alloc(2048, F32) for _ in range(4)]
            xsb = Cyc([work.alloc(2048, F32) for _ in range(14)])
            for tg in range(4):
                rstd(rsb[tg], ps[SS[tg]][:], ps_t[SS[tg]], float(D))
            for tg in range(4):
                for fc in range(16):
                    xb = xsb.next()
                    dma("act", xb.ap, xres[fc][:, tg * 512:(tg + 1) * 512], [xres_t[fc][tg]], xb.tiles)
                    stt(Hs[:, fc, tg * 512:(tg + 1) * 512], xb.ap, gs[:, gcol0 + fc:gcol0 + fc + 1], rsb[tg].ap,
                        ALU.mult, ALU.mult, xb.tiles + rsb[tg].tiles + [T_const], [H_t[fc][tg]])

        def phase0():
            work.reset()
            xin = Cyc([work.alloc(8192, F32) for _ in range(3)])
            xts = Cyc([work.alloc(16384, F32) for _ in range(2)])
            sqb = Cyc([work.alloc(8192, BF16) for _ in range(2)])
            pend0 = []
            for u in range(8):
                xt = xts.next()
                sq = sqb.next()
                xt3 = xt.ap.rearrange("p (c t) -> p c t", t=256)
                sq3 = sq.ap.rearrange("p (c t) -> p c t", t=256)
                for half in range(2):
                    tt_ = 2 * u + half
                    tg = tt_ // 4
                    xi = xin.next()
                    dma("act", xi.ap, x_in[tt_ * 128:(tt_ + 1) * 128, :], [], xi.tiles)
                    for b_ in range(4):
                        bank = acc_pool.next()
                        for i in range(4):
                            fc = b_ * 4 + i
                            tr(ps[bank][:, i * 128:(i + 1) * 128], xi.ap[:, fc * 128:(fc + 1) * 128], xi.tiles, [ps_t[bank]])
                        dstv = xt3[:, b_ * 4:(b_ + 1) * 4, half * 128:(half + 1) * 128]
                        srcv = ps[bank][:].rearrange("p (c t) -> p c t", t=128)
                        P.op("dve", lambda e, dstv=dstv, srcv=srcv: e.tensor_copy(dstv, srcv),
                             reads=[ps_t[bank]], writes=xt.tiles)
                        act(sq3[:, b_ * 4:(b_ + 1) * 4, half * 128:(half + 1) * 128], dstv, AF.Square, xt.tiles, sq.tiles)

                    def stats(tt_=tt_, tg=tg, sq3=sq3, sq=sq, half=half):
                        for fc in range(16):
                            mm(ps[SS[tg]][:, (tt_ % 4) * 128:(tt_ % 4 + 1) * 128], ones[:], sq3[:, fc, half * 128:(half + 1) * 128],
                               fc == 0, fc == 15, sq.tiles + [T_const], [ps_t[SS[tg]]])
                    pend0.append(stats)
                    if len(pend0) > 1:
                        pend0.pop(0)()
                tg = (2 * u) // 4
                dst = xres.rearrange("c p t -> p c t")[:, :, u * 256:(u + 1) * 256]
                dma("sp", dst, xt3, xt.tiles, [xres_t[fc][tg] for fc in range(16)])
            while pend0:
                pend0.pop(0)()

        def tokview(kc, variant, tg):
            base = Hs[:, kc, :]
            if variant == 0:
                return base[:, tg * 512:(tg + 1) * 512]
            if variant == 1:
                return base.rearrange("p (m r) -> p r m", r=4)[:, tg, :]
            return base.rearrange("p (m r) -> p r m", r=16)[:, 4 * tg:4 * tg + 4, :]

        def tokview128(kc, variant, tt_):
            base = Hs[:, kc, :]
            if variant == 0:
                return base[:, tt_ * 128:(tt_ + 1) * 128]
            if variant == 1:
                r, m0 = tt_ // 4, (tt_ % 4) * 128
                return base.rearrange("p (m r) -> p r m", r=4)[:, r, m0:m0 + 128]
            return base.rearrange("p (m r) -> p r m", r=16)[:, tt_, :]

        def Hall(kc):
            return [H_t[kc][0], H_t[kc][1], H_t[kc][2], H_t[kc][3]]

        def phase1(l):
            work.reset()
            sqb = Cyc([work.alloc(1024, BF16) for _ in range(6)])
            rsb = Cyc([work.alloc(2048, F32) for _ in range(3)])
            qnb = Cyc([work.alloc(1024, BF16) for _ in range(6)])
            t1b = Cyc([work.alloc(2048, F32) for _ in range(5)])
            t2b = Cyc([work.alloc(2048, F32) for _ in range(3)])
            ostb = Cyc([work.alloc(1024, BF16) for _ in range(6)])
            csb = Cyc([work.alloc(4096, F32) for _ in range(5)])
            vstb = Cyc([work.alloc(1024, BF16) for _ in range(6)])
            accb = Cyc([work.alloc(2048, F32) for _ in range(6)])

            def fm_block(slot, blk, kind, variant, gcol, rope, dsts_fn):
                wt = ring[slot][:, :].rearrange("p (k n) -> p k n", n=512)
                for tgp in range(2):
                    banks = [acc_pool.next(), acc_pool.next()]
                    for kc in range(16):
                        for t_ in range(2):
                            tg = tgp * 2 + t_
                            o = ps[banks[t_]][:]
                            rhs = tokview(kc, variant, tg)
                            if variant == 2:
                                o = o.rearrange("p (r m) -> p r m", r=4)
                            mm(o, wt[:, kc, blk * 128:(blk + 1) * 128], rhs, kc == 0, kc == 15,
                               [ring_t[slot]] + (Hall(kc) if variant else [H_t[kc][tg]]), [ps_t[banks[t_]]])
                    for t_ in range(2):
                        pend.append([tile_stages(kind, variant, gcol, rope, dsts_fn, tgp * 2 + t_, banks[t_]), 0])
                    step()

            pend = []

            def step():
                for ent in list(pend):
                    ent[0][ent[1]]()
                    ent[1] += 1
                    if ent[1] >= len(ent[0]):
                        pend.remove(ent)

            def flush():
                while pend:
                    step()

            def tile_stages(kind, variant, gcol, rope, dsts_fn, tg, bank):
                accp = ps[bank][:]
                stt_ = {}

                def store(ost):
                    for (d_ap, lo, hi, d_tiles) in dsts_fn(tg):
                        dma("sp", d_ap, ost.ap[:, lo:hi], ost.tiles, d_tiles)

                if kind == "gate":
                    def g0():
                        ost = ostb.next()
                        act(ost.ap, accp, AF.Sigmoid, [ps_t[bank]], ost.tiles)
                        store(ost)
                    return [g0]

                def s0():
                    sq = sqb.next()
                    act(sq.ap, accp, AF.Square, [ps_t[bank]], sq.tiles)
                    acs = accb.next()
                    act(acs.ap, accp, AF.Copy, [ps_t[bank]], acs.tiles)
                    stt_["sq"], stt_["acs"] = sq, acs

                def s1():
                    sq, acs = stt_["sq"], stt_["acs"]
                    ab = aux_pool.next()
                    mm(ps[ab][:], ones[:], sq.ap, True, True, sq.tiles + [T_const], [ps_t[ab]])
                    rs = rsb.next()
                    rstd(rs, ps[ab][:], ps_t[ab], 128.0)
                    if not rope:
                        ost = ostb.next()
                        stt(ost.ap, acs.ap, qkgs[:, gcol:gcol + 1], rs.ap, ALU.mult, ALU.mult,
                            acs.tiles + [T_const] + rs.tiles, ost.tiles)
                        store(ost)
                    else:
                        qn = qnb.next()
                        stt(qn.ap, acs.ap, qkgs[:, gcol:gcol + 1], rs.ap, ALU.mult, ALU.mult,
                            acs.tiles + [T_const] + rs.tiles, qn.tiles)
                        cs = csb.next()
                        csv = cs.ap.rearrange("p (c t) -> p c t", c=2)
                        dma("act", csv, rope_in[variant].rearrange("c p t -> p c t")[:, :, tg * 512:(tg + 1) * 512],
                            [], cs.tiles)
                        t1 = t1b.next()
                        tt("pool", t1.ap, qn.ap, csv[:, 0, :], ALU.mult, qn.tiles + cs.tiles, t1.tiles)
                        stt_["qn"], stt_["cs"], stt_["csv"], stt_["t1"] = qn, cs, csv, t1

                def s2():
                    qn, cs, csv, t1 = stt_["qn"], stt_["cs"], stt_["csv"], stt_["t1"]
                    rb = aux_pool.next()
                    mm(ps[rb][:], rotm[:], qn.ap, True, True, qn.tiles + [T_const], [ps_t[rb]])
                    t2 = t2b.next()
                    tt("dve", t2.ap, ps[rb][:], csv[:, 1, :], ALU.mult, [ps_t[rb]] + cs.tiles, t2.tiles)
                    ost = ostb.next()
                    tt("pool", ost.ap, t1.ap, t2.ap, ALU.add, t1.tiles + t2.tiles, ost.tiles)
                    store(ost)
                return [s0, s1, s2] if rope else [s0, s1]

            def tm_block(slot, c_lo, ncols, variant, dsts_fn):
                wt = ring[slot][:, :].rearrange("p (k n) -> p k n", n=512)
                for tt_ in range(16):
                    bank = acc_pool.next()
                    for kc in range(16):
                        mm(ps[bank][:, 0:ncols], tokview128(kc, variant, tt_), wt[:, kc, c_lo:c_lo + ncols], kc == 0, kc == 15,
                           [ring_t[slot]] + Hall(kc), [ps_t[bank]])
                    step()
                    vs = vstb.next()
                    act(vs.ap[:, 0:ncols], ps[bank][:, 0:ncols], AF.Copy, [ps_t[bank]], vs.tiles)
                    for (d_ap, lo, hi, d_tiles) in dsts_fn(tt_):
                        dma("sp", d_ap, vs.ap[:, lo:hi], vs.tiles, d_tiles)

            def kown(off, key, tg):
                return (kv_fm(kv_own, off)[:, tg * 512:(tg + 1) * 512], 0, 512, [kvt((key, tg))])

            def q_dst(idx):
                return lambda tg: [(qsc[idx][:, tg * 512:(tg + 1) * 512], 0, 512, [qsc_t[idx][tg]])]

            def cc(m):
                flush()
                P.op("pool", lambda e: e.collective_compute(
                    "AllGather", ALU.bypass, replica_groups=pairs,
                    ins=[msg_src[m].rearrange("(a b) -> a b", b=1024)],
                    outs=[msg_dst[m].rearrange("(a b) -> a b", b=1024)]),
                    reads=list(msg_t[m].values()), writes=[msgd_t[m]], kind="cc")

            gq = l * 6
            slot = ws_next("in")
            for h in range(2):
                def d_ka(tg, h=h):
                    r = [kown(O_KA + h * 128 * T, ("ka", h), tg)]
                    if tg == 0:
                        r.append((ms("a", (0 * 2 + h) * 16384, 128, 128), 0, 128, [mt("a", ("ka", 0, h))]))
                    if tg == 3:
                        r.append((ms("a", (1 * 2 + h) * 16384, 128, 128), 384, 512, [mt("a", ("ka", 1, h))]))
                    return r
                fm_block(slot, h, "k", 0, gq + 1, True, d_ka)

            def d_va(tt_):
                r = [(v_hd(O_VA, h)[:, tt_, :], h * 128, (h + 1) * 128, [kvt(("va", h, tt_))]) for h in range(2)]
                if tt_ == 0:
                    r.append((ms("a", 65536, 128, 256), 0, 256, [mt("a", ("va", 0))]))
                if tt_ == 15:
                    r.append((ms("a", 65536 + 32768, 128, 256), 0, 256, [mt("a", ("va", 1))]))
                return r
            tm_block(slot, 256, 256, 0, d_va)
            for g in range(3):
                slot = ws_next("in")
                for h in range(4):
                    def d_kb(tg, g=g, h=h):
                        if g == 2:
                            return [(ms("b", h * 128 * T, 128, T)[:, tg * 512:(tg + 1) * 512], 0, 512, [mt("b", (h, tg))])]
                        r = [kown(O_KB + (g * 4 + h) * 128 * T, ("kb", g, h), tg)]
                        if g == 0:
                            if tg == 0:
                                r.append((ms("a", 131072 + (0 * 4 + h) * 16384, 128, 128), 0, 128, [mt("a", ("kb0", 0, h))]))
                            if tg == 3:
                                r.append((ms("a", 131072 + (1 * 4 + h) * 16384, 128, 128), 384, 512, [mt("a", ("kb0", 1, h))]))
                        else:
                            for sd, lo in ((0, 0), (1, 384)):
                                v = ms("a", 262144 + (sd * 4 + h) * 65536, 128, 512)[:, tg * 128:(tg + 1) * 128]
                                r.append((v, lo, lo + 128, [mt("a", ("kb1", sd, h, tg))]))
                        return r
                    fm_block(slot, h, "k", g, gq + 3, True, d_kb)
                if g == 1:
                    cc("a")
                if g == 2:
                    cc("b")
            for g in range(3):
                slot = ws_next("in")

                def d_vb(tt_, g=g):
                    if g == 2:
                        return [(ms("d", 0, T, 512)[tt_ * 128:(tt_ + 1) * 128, :], 0, 512, [mt("d", tt_)])]
                    r = [(v_hd(O_VB + g * T * 512, h)[:, tt_, :], h * 128, (h + 1) * 128, [kvt(("vb", g, h, tt_))]) for h in range(4)]
                    if g == 0:
                        if tt_ == 0:
                            r.append((ms("c", 0, 128, 512), 0, 512, [mt("c", ("vb0", 0))]))
                        if tt_ == 15:
                            r.append((ms("c", 65536, 128, 512), 0, 512, [mt("c", ("vb0", 1))]))
                    else:
                        rr, cc_ = tt_ // 4, tt_ % 4
                        if cc_ == 0:
                            r.append((ms("c", 131072 + (0 * 4 + rr) * 65536, 128, 512), 0, 512, [mt("c", ("vb1", 0, rr))]))
                        if cc_ == 3:
                            r.append((ms("c", 131072 + (1 * 4 + rr) * 65536, 128, 512), 0, 512, [mt("c", ("vb1", 1, rr))]))
                    return r
                tm_block(slot, 0, 512, g, d_vb)
                if g == 1:
                    cc("c")
                if g == 2:
                    cc("d")
            for half in range(2):
                slot = ws_next("in")
                for hh in range(4):
                    h = half * 4 + hh

                    def d_kc(tg, h=h):
                        r = [kown(O_KC + h * 128 * T, ("kc", h), tg)]
                        if tg == 0:
                            r.append((ms("e", (0 * 8 + h) * 32768, 128, 256), 0, 256, [mt("e", ("kc", 0, h))]))
                        if tg == 3:
                            r.append((ms("e", (1 * 8 + h) * 32768, 128, 256), 256, 512, [mt("e", ("kc", 1, h))]))
                        return r
                    fm_block(slot, hh, "k", 0, gq + 5, False, d_kc)
            for half in range(2):
                slot = ws_next("in")

                def d_vc(tt_, half=half):
                    r = [(v_hd(O_VC, half * 4 + hh)[:, tt_, :], hh * 128, (hh + 1) * 128, [kvt(("vc", half * 4 + hh, tt_))])
                         for hh in range(4)]
                    if tt_ < 2:
                        r.append((ms("e", 524288, 256, 1024)[tt_ * 128:(tt_ + 1) * 128, half * 512:(half + 1) * 512], 0, 512,
                                  [mt("e", ("vc", 0, half, tt_))]))
                    if tt_ >= 14:
                        r.append((ms("e", 524288 + 262144, 256, 1024)[(tt_ - 14) * 128:(tt_ - 13) * 128, half * 512:(half + 1) * 512], 0, 512,
                                  [mt("e", ("vc", 1, half, tt_))]))
                    return r
                tm_block(slot, 0, 512, 0, d_vc)
            cc("e")
            for half in range(2):
                slot = ws_next("in")
                for hh in range(4):
                    fm_block(slot, hh, "q", 0, gq + 0, True, q_dst(half * 4 + hh))
            for g in range(3):
                slot = ws_next("in")
                for h in range(4):
                    fm_block(slot, h, "q", g, gq + 2, True, q_dst(8 + g * 4 + h))
            for half in range(2):
                slot = ws_next("in")
                for hh in range(4):
                    fm_block(slot, hh, "q", 0, gq + 4, False, q_dst(20 + half * 4 + hh))
            for gi in range(12):
                slot = ws_next("in")
                for hh in range(4):
                    blk = gi * 4 + hh
                    i, fc = blk // 16, blk % 16
                    fm_block(slot, hh, "gate", 0, 0, False,
                             lambda tg, i=i, fc=fc: [(gsc[fc][:, i, tg * 512:(tg + 1) * 512], 0, 512, [gsc_t[fc][i][tg]])])
            flush()

        def attn_setup(esz=1024, ne=4, nrd=2, nost=6):
            work.reset()
            st = {}
            st["E"] = Cyc([work.alloc(esz, BF16) for _ in range(ne)])
            st["rd"] = Cyc([work.alloc(2048, F32) for _ in range(nrd)])
            st["ost"] = Cyc([work.alloc(1024, BF16) for _ in range(nost)])
            st["S1"] = Cyc([0, 1, 2, 3])
            st["S2"] = Cyc([(0, 1), (2, 3)])
            st["OT"] = Cyc([4, 5])
            st["D"] = Cyc([6, 7])
            return st

        def unit_scores(st, u):
            chunks, mask = u["chunks"], u["mask"]
            n = len(chunks)
            if n <= 4:
                sb_ = (st["S1"].next(),)
            else:
                sb_ = st["S2"].next()
            used = sorted(set(i // 4 for i in range(n)))
            for i, (k_ap, v_ap) in enumerate(chunks):
                b = sb_[i // 4]
                mm(ps[b][:, (i % 4) * 128:(i % 4 + 1) * 128], k_ap, u["q_ap"], True, True, u["kvt"] + u["q_tiles"], [ps_t[b]])
            e = st["E"].next()
            u["e"] = e
            for bi in used:
                b = sb_[bi]
                ncol = min(n - bi * 4, 4) * 128
                if mask[0] == "ab":
                    act(e.ap[:, bi * 512:bi * 512 + ncol], ps[b][:, 0:ncol], AF.Exp, [ps_t[b]], e.tiles, scale=SCALE)
                else:
                    tb = mask[3].next()
                    boff = (mask[2] + bi * 4) * 128
                    stt(tb.ap[:, 0:ncol], ps[b][:, 0:ncol], SCALE, mask[1].ap[:, boff:boff + ncol], ALU.mult, ALU.add,
                        [ps_t[b]] + mask[1].tiles, tb.tiles)
                    act(e.ap[:, bi * 512:bi * 512 + ncol], tb.ap[:, 0:ncol], AF.Exp, tb.tiles, e.tiles)
            if mask[0] == "ab":
                moff = mask[1] * 128
                tt("dve", e.ap[:, 0:n * 128], e.ap[:, 0:n * 128], mab[:, moff:moff + n * 128], ALU.mult,
                   e.tiles + [T_const], e.tiles)

        def unit_pv(st, u):
            chunks, e, col = u["chunks"], u["e"], u["col"]
            n = len(chunks)
            ob, db = st["cur_ot"], st["cur_d"]
            for i, (k_ap, v_ap) in enumerate(chunks):
                mm(ps[ob][:, col * 128:(col + 1) * 128], v_ap, e.ap[:, i * 128:(i + 1) * 128], i == 0, i == n - 1,
                   u["kvt"] + e.tiles, [ps_t[ob]])
            for i in range(n):
                mm(ps[db][:, col * 128:(col + 1) * 128], ones[:], e.ap[:, i * 128:(i + 1) * 128], i == 0, i == n - 1,
                   e.tiles + [T_const], [ps_t[db]])

        def run_units(st, units, finalize, lag):
            n = len(units)
            for i in range(n + lag):
                if i < n:
                    unit_scores(st, units[i])
                j = i - lag
                if j >= 0:
                    if j % 4 == 0:
                        begin_batch(st)
                    unit_pv(st, units[j])
                    if j % 4 == 3:
                        finalize(j // 4)

        def drive(gens):
            gens = list(gens)
            next(gens[0])
            for i, g_ in enumerate(gens):
                if i + 1 < len(gens):
                    next(gens[i + 1])
                for _ in g_:
                    pass

        def begin_batch(st):
            st["cur_ot"] = st["OT"].next()
            st["cur_d"] = st["D"].next()

        def load_kv(K, V, own, halos):
            k_src, k_tiles, v_src, v_tiles = own
            dma("pool", K["own"].ap, k_src, k_tiles, K["own"].tiles)
            dma("pool", V["own"].ap.rearrange("p (c d) -> p c d", d=128), v_src, v_tiles, V["own"].tiles)
            kL, kR, vL, vR, h_tiles = halos
            ncol = kL.shape[1] if len(kL.shape) == 2 else kL.shape[1] * kL.shape[2]
            dma("pool", K["L"].ap[:, 0:ncol], kL, h_tiles, K["L"].tiles)
            dma("pool", K["R"].ap[:, 0:ncol], kR, h_tiles, K["R"].tiles)
            nch = vL.shape[1]
            dma("pool", V["L"].ap[:, 0:nch * 128].rearrange("p (c d) -> p c d", d=128), vL, h_tiles, V["L"].tiles)
            dma("pool", V["R"].ap[:, 0:nch * 128].rearrange("p (c d) -> p c d", d=128), vR, h_tiles, V["R"].tiles)

        def alloc_kv(nh, hw):
            K = {"own": work.alloc(4096), "L": work.alloc(nh * hw * 2), "R": work.alloc(nh * hw * 2), "nh": nh, "hw": hw}
            V = {"own": work.alloc(4096), "L": work.alloc(nh * hw * 2), "R": work.alloc(nh * hw * 2)}
            return K, V

        def kva(rank, off):
            return rank * NKV + off

        def attention_a(l):
            st = attn_setup()
            sets = Cyc([alloc_kv(1, 128) for _ in range(2)])
            qb = Cyc([work.alloc(4096) for _ in range(2)])
            kvs = {}

            def group(kh, g):
                if g == 0:
                    K, V = sets.next()
                    va3 = lambda base: base.rearrange("(c p) n -> p c n", p=128)[:, :, kh * 128:(kh + 1) * 128]
                    load_kv(K, V,
                            (kv_fm(kv_own, O_KA + kh * 128 * T), [kvt((("ka", kh), tg)) for tg in range(4)],
                             v_hd(O_VA, kh), [kvt(("va", kh, t_)) for t_ in range(16)]),
                            (md("a", 0, (1 * 2 + kh) * 16384, 128, 128), md("a", 1, (0 * 2 + kh) * 16384, 128, 128),
                             va3(md("a", 0, 65536 + 32768, 128, 256)), va3(md("a", 1, 65536, 128, 256)), [msgd_t["a"]]))
                    kvs[kh] = (K, V)
                K, V = kvs[kh]
                h = kh * 4 + g
                q = qb.next()
                dma("pool", q.ap, qsc[h][:, :], qsc_t[h], q.tiles)
                yield
                kvtiles = K["own"].tiles + K["L"].tiles + K["R"].tiles + V["own"].tiles + V["L"].tiles + V["R"].tiles

                def kch(c):
                    if c < 0:
                        return K["L"].ap[:, 0:128], V["L"].ap[:, 0:128]
                    if c > 15:
                        return K["R"].ap[:, 0:128], V["R"].ap[:, 0:128]
                    return K["own"].ap[:, c * 128:(c + 1) * 128], V["own"].ap[:, c * 128:(c + 1) * 128]
                units = [dict(q_ap=q.ap[:, j * 128:(j + 1) * 128], q_tiles=q.tiles, chunks=[kch(j + dl) for dl in (-1, 0, 1)],
                              kvt=kvtiles, mask=("ab", sigA[j]), col=j % 4) for j in range(16)]

                def fin_a(jb):
                    ob, db = st["cur_ot"], st["cur_d"]
                    rd = st["rd"].next()
                    recip_act(rd.ap, rd.tiles, ps[db][:], [ps_t[db]], bias=sinke[:, l * 8 + h:l * 8 + h + 1])
                    os_ = st["ost"].next()
                    tt("dve", os_.ap, ps[ob][:], rd.ap, ALU.mult, [ps_t[ob]] + rd.tiles, os_.tiles)
                    dma("sp", osc[h][:, jb * 512:(jb + 1) * 512], os_.ap, os_.tiles, [osc_t[h][jb]])
                run_units(st, units, fin_a, 3)
            drive([group(kh, g) for kh in range(2) for g in range(4)])

        def attention_b(l):
            st = attn_setup(1024, 4, 1)
            num = work.alloc(8192, F32)
            den = work.alloc(8192, F32)
            sets = Cyc([alloc_kv(16, 128) for _ in range(2)])
            qb = Cyc([work.alloc(4096) for _ in range(2)])
            def group(h, g):
                    dil = (1, 4, 16)[g]
                    ncls = dil
                    cl = T // dil
                    K, V = sets.next()
                    K = dict(K)
                    K["nh"] = ncls
                    ncc = cl // 128
                    hc = slice(h * 128, (h + 1) * 128)
                    tm3 = lambda base: base.rearrange("(c p) n -> p c n", p=128)[:, :, hc]
                    if g == 2:
                        own = (ms("b", h * 128 * T, 128, T), [mt("b", (h, tg)) for tg in range(4)],
                               tm3(ms("d", 0, T, 512)), [mt("d", t_) for t_ in range(16)])
                        halos = (md("b", 0, h * 128 * T, 128, T), md("b", 1, h * 128 * T, 128, T),
                                 tm3(md("d", 0, 0, T, 512)), tm3(md("d", 1, 0, T, 512)), [msgd_t["b"], msgd_t["d"]])
                    else:
                        own = (kv_fm(kv_own, O_KB + (g * 4 + h) * 128 * T), [kvt((("kb", g, h), tg)) for tg in range(4)],
                               v_hd(O_VB + g * T * 512, h), [kvt(("vb", g, h, t_)) for t_ in range(16)])
                        if g == 0:
                            halos = (md("a", 0, 131072 + (1 * 4 + h) * 16384, 128, 128), md("a", 1, 131072 + (0 * 4 + h) * 16384, 128, 128),
                                     tm3(md("c", 0, 65536, 128, 512)), tm3(md("c", 1, 0, 128, 512)), [msgd_t["a"], msgd_t["c"]])
                        else:
                            halos = (md("a", 0, 262144 + (1 * 4 + h) * 65536, 128, 512), md("a", 1, 262144 + (0 * 4 + h) * 65536, 128, 512),
                                     tm3(md("c", 0, 131072 + 4 * 65536, 512, 512)), tm3(md("c", 1, 131072, 512, 512)),
                                     [msgd_t["a"], msgd_t["c"]])
                    load_kv(K, V, own, halos)
                    kvtiles = K["own"].tiles + K["L"].tiles + K["R"].tiles + V["own"].tiles + V["L"].tiles + V["R"].tiles
                    q = qb.next()
                    qi = 8 + g * 4 + h
                    dma("pool", q.ap, qsc[qi][:, :], qsc_t[qi], q.tiles)
                    yield

                    def kch(r, c):
                        if c < 0:
                            return K["L"].ap[:, r * 128:(r + 1) * 128], V["L"].ap[:, r * 128:(r + 1) * 128]
                        if c >= ncc:
                            return K["R"].ap[:, r * 128:(r + 1) * 128], V["R"].ap[:, r * 128:(r + 1) * 128]
                        o_ = (r * ncc + c) * 128
                        return K["own"].ap[:, o_:o_ + 128], V["own"].ap[:, o_:o_ + 128]
                    units = []
                    for bt in range(4):
                        for jj in range(4):
                            if g == 0:
                                r, j = 0, bt * 4 + jj
                            elif g == 1:
                                r, j = bt, jj
                            else:
                                r, j = bt * 4 + jj, 0
                            qo = (r * ncc + j) * 128
                            units.append(dict(q_ap=q.ap[:, qo:qo + 128], q_tiles=q.tiles, chunks=[kch(r, j + dl) for dl in (-1, 0, 1)],
                                              kvt=kvtiles, mask=("ab", sigB[g][j]), col=jj))

                    def fin_b(bt):
                        ob, db = st["cur_ot"], st["cur_d"]
                        if g == 0:
                            nv = num.ap[:, bt * 512:(bt + 1) * 512]
                            dv = den.ap[:, bt * 512:(bt + 1) * 512]
                            act(nv, ps[ob][:], AF.Copy, [ps_t[ob]], num.tiles)
                            act(dv, ps[db][:], AF.Copy, [ps_t[db]], den.tiles)
                        else:
                            if g == 1:
                                nv = num.ap.rearrange("p (m r) -> p r m", r=4)[:, bt, :]
                                dv = den.ap.rearrange("p (m r) -> p r m", r=4)[:, bt, :]
                                po, pd = ps[ob][:], ps[db][:]
                            else:
                                nv = num.ap.rearrange("p (m r) -> p r m", r=16)[:, bt * 4:bt * 4 + 4, :]
                                dv = den.ap.rearrange("p (m r) -> p r m", r=16)[:, bt * 4:bt * 4 + 4, :]
                                po = ps[ob][:].rearrange("p (r m) -> p r m", r=4)
                                pd = ps[db][:].rearrange("p (r m) -> p r m", r=4)
                            tt("dve", nv, nv, po, ALU.add, [ps_t[ob]] + num.tiles, num.tiles)
                            tt("dve", dv, dv, pd, ALU.add, [ps_t[db]] + den.tiles, den.tiles)
                    run_units(st, units, fin_b, 3)
                    if g < 2:
                        return
                    for bt in range(4):
                        dv = den.ap[:, bt * 512:(bt + 1) * 512]
                        recip_act(dv, den.tiles, dv, den.tiles)
                        os_ = st["ost"].next()
                        tt("dve", os_.ap, num.ap[:, bt * 512:(bt + 1) * 512], dv, ALU.mult, num.tiles + den.tiles, os_.tiles)
                        dma("sp", osc[8 + h][:, bt * 512:(bt + 1) * 512], os_.ap, os_.tiles, [osc_t[8 + h][bt]])
            drive([group(h, g) for h in range(4) for g in range(3)])

        def attention_c(l):
            st = attn_setup(2048, 3)
            tbc = Cyc([work.alloc(2048, F32) for _ in range(4)])
            cbb = Cyc([work.alloc(n_ct * 512, F32) for _ in range(2)])
            sets = Cyc([alloc_kv(1, 256) for _ in range(2)])
            qb = Cyc([work.alloc(4096) for _ in range(2)])
            def group(h):
                K, V = sets.next()
                hc = slice(h * 128, (h + 1) * 128)
                tm3 = lambda base: base.rearrange("(c p) n -> p c n", p=128)[:, :, hc]
                load_kv(K, V,
                        (kv_fm(kv_own, O_KC + h * 128 * T), [kvt((("kc", h), tg)) for tg in range(4)],
                         v_hd(O_VC, h), [kvt(("vc", h, t_)) for t_ in range(16)]),
                        (md("e", 0, (1 * 8 + h) * 32768, 128, 256), md("e", 1, (0 * 8 + h) * 32768, 128, 256),
                         tm3(md("e", 0, 524288 + 262144, 256, 1024)), tm3(md("e", 1, 524288, 256, 1024)), [msgd_t["e"]]))
                kvtiles = K["own"].tiles + K["L"].tiles + K["R"].tiles + V["own"].tiles + V["L"].tiles + V["R"].tiles
                cb = cbb.next()
                dma("pool", cb.ap, cb_in[l][h][:, :], [], cb.tiles)
                q = qb.next()
                dma("pool", q.ap, qsc[20 + h][:, :], qsc_t[20 + h], q.tiles)
                yield

                def kch(c):
                    if c < 0:
                        return K["L"].ap[:, (c + 2) * 128:(c + 3) * 128], V["L"].ap[:, (c + 2) * 128:(c + 3) * 128]
                    if c > 15:
                        return K["R"].ap[:, (c - 16) * 128:(c - 15) * 128], V["R"].ap[:, (c - 16) * 128:(c - 15) * 128]
                    return K["own"].ap[:, c * 128:(c + 1) * 128], V["own"].ap[:, c * 128:(c + 1) * 128]
                units = []
                for j in range(16):
                    off, dls = sigC[j]
                    units.append(dict(q_ap=q.ap[:, j * 128:(j + 1) * 128], q_tiles=q.tiles, chunks=[kch(j + dl) for dl in dls],
                                      kvt=kvtiles, mask=("c", cb, off, tbc), col=j % 4))

                def fin_c(jb):
                    ob, db = st["cur_ot"], st["cur_d"]
                    rd = st["rd"].next()
                    recip_act(rd.ap, rd.tiles, ps[db][:], [ps_t[db]])
                    os_ = st["ost"].next()
                    tt("dve", os_.ap, ps[ob][:], rd.ap, ALU.mult, [ps_t[ob]] + rd.tiles, os_.tiles)
                    dma("sp", osc[12 + h][:, jb * 512:(jb + 1) * 512], os_.ap, os_.tiles, [osc_t[12 + h][jb]])
                run_units(st, units, fin_c, 1)
            drive([group(h) for h in range(8)])

        def branch(l, i, o0, nk):
            work.reset()
            ob_ = work.alloc(nk * 4096)
            ov = ob_.ap.rearrange("p (k t) -> p k t", t=T)
            gstb = Cyc([work.alloc(1024, BF16) for _ in range(3)])
            tmpb = Cyc([work.alloc(2048, F32) for _ in range(2)])
            obt = [ob_.tiles[4 * k:4 * k + 4] for k in range(nk)]
            for k in range(nk):
                dma("act", ov[:, k, :], osc[o0 + k][:, :], osc_t[o0 + k], obt[k])
            for cg in range(4):
                slot = ws_next("br")
                wt = ring[slot][:, 0:nk * 512].rearrange("p (k n) -> p k n", n=512)
                for blk in range(4):
                    fc = cg * 4 + blk
                    for tgp in range(2):
                        banks = [acc_pool.next(), acc_pool.next()]
                        for kc in range(nk):
                            for t_ in range(2):
                                tg = tgp * 2 + t_
                                mm(ps[banks[t_]][:], wt[:, kc, blk * 128:(blk + 1) * 128], ov[:, kc, tg * 512:(tg + 1) * 512],
                                   kc == 0, kc == nk - 1, [ring_t[slot]] + obt[kc], [ps_t[banks[t_]]])
                        for t_ in range(2):
                            tg = tgp * 2 + t_
                            bank = banks[t_]
                            gs_ = gstb.next()
                            dma("act", gs_.ap, gsc[fc][:, i, tg * 512:(tg + 1) * 512], [gsc_t[fc][i][tg]], gs_.tiles)
                            hv = Hs[:, fc, tg * 512:(tg + 1) * 512]
                            if i == 0:
                                tt("dve", hv, ps[bank][:], gs_.ap, ALU.mult, [ps_t[bank]] + gs_.tiles, [H_t[fc][tg]])
                            else:
                                tm = tmpb.next()
                                tt("dve", tm.ap, ps[bank][:], gs_.ap, ALU.mult, [ps_t[bank]] + gs_.tiles, tm.tiles)
                                tt("pool", hv, hv, tm.ap, ALU.add, [H_t[fc][tg]] + tm.tiles, [H_t[fc][tg]])

        def resid_setup(nxs=8, nxn=5, nsq=4):
            return {"xs": Cyc([work.alloc(2048, F32) for _ in range(nxs)]),
                    "xn": Cyc([work.alloc(2048, F32) for _ in range(nxn)]),
                    "sq": Cyc([work.alloc(1024, BF16) for _ in range(nsq)]), "pend": [], "ld": {}, "plan": [], "nld": 0}

        def resid_plan(rs_, pairs_):
            rs_["plan"] = list(pairs_)
            rs_["nld"] = 0
            rs_["ld"] = {}

        def resid_prefetch(rs_, upto):
            while rs_["nld"] < min(upto + 1, len(rs_["plan"])):
                fc, tgp = rs_["plan"][rs_["nld"]]
                for t_ in range(2):
                    tg = tgp * 2 + t_
                    xb = rs_["xs"].next()
                    dma("act", xb.ap, xres[fc][:, tg * 512:(tg + 1) * 512], [xres_t[fc][tg]], xb.tiles)
                    rs_["ld"][(fc, tg)] = xb
                rs_["nld"] += 1

        def resid_update(rs_, bank, fc, tg, stats):
            xb = rs_["ld"].pop((fc, tg))
            xn = rs_["xn"].next()
            tt("dve", xn.ap, ps[bank][:], xb.ap, ALU.add, [ps_t[bank]] + xb.tiles, xn.tiles)
            dma("sp", xres[fc][:, tg * 512:(tg + 1) * 512], xn.ap, xn.tiles, [xres_t[fc][tg]])
            if stats:
                sq = rs_["sq"].next()
                act(sq.ap, xn.ap, AF.Square, xn.tiles, sq.tiles)
                rs_["pend"].append(lambda: mm(ps[SS[tg]][:], ones[:], sq.ap, fc == 0, fc == 15, sq.tiles + [T_const], [ps_t[SS[tg]]]))

        def resid_flush(rs_, keep=0):
            while len(rs_["pend"]) > keep:
                rs_["pend"].pop(0)()

        def phase_wo(l):
            work.reset()
            rs_ = resid_setup()
            resid_plan(rs_, [(fc, tgp) for fc in range(16) for tgp in range(2)])
            pi = 0
            resid_prefetch(rs_, 1)
            for cg in range(4):
                slot = ws_next("wo")
                wt = ring[slot][:, :].rearrange("p (k n) -> p k n", n=512)
                for blk in range(4):
                    fc = cg * 4 + blk
                    for tgp in range(2):
                        banks = [acc_pool.next(), acc_pool.next()]
                        for kc in range(16):
                            for t_ in range(2):
                                tg = tgp * 2 + t_
                                mm(ps[banks[t_]][:], wt[:, kc, blk * 128:(blk + 1) * 128], Hs[:, kc, tg * 512:(tg + 1) * 512],
                                   kc == 0, kc == 15, [ring_t[slot], H_t[kc][tg]], [ps_t[banks[t_]]])
                        resid_flush(rs_)
                        resid_prefetch(rs_, pi + 2)
                        for t_ in range(2):
                            resid_update(rs_, banks[t_], fc, tgp * 2 + t_, True)
                        pi += 1
            resid_flush(rs_)

        def phase_ffn(l, last):
            for qi, (j0, nq) in enumerate(QUARTERS):
                work.reset()
                actb = work.alloc(nq * 4096)
                av = actb.ap.rearrange("p (j t) -> p j t", t=T)
                sgb = Cyc([work.alloc(2048, F32) for _ in range(3)])
                rs_ = resid_setup(8, 5, 4)
                gu_banks = Cyc([0, 1, 2, 3, 4, 5, 6, 7])
                for jj in range(0, nq, 2):
                    slot = ws_next("gu")
                    wg = ring[slot][:, 0:4096].rearrange("p (k n) -> p k n", n=256)
                    wu = ring[slot][:, 4096:8192].rearrange("p (k n) -> p k n", n=256)
                    for b2 in range(2):
                        jq = jj + b2
                        for tgp in range(2):
                            bg = [gu_banks.next(), gu_banks.next()]
                            bu = [gu_banks.next(), gu_banks.next()]
                            for (wt, bb) in ((wg, bg), (wu, bu)):
                                for kc in range(16):
                                    for t_ in range(2):
                                        tg = tgp * 2 + t_
                                        mm(ps[bb[t_]][:], wt[:, kc, b2 * 128:(b2 + 1) * 128], Hs[:, kc, tg * 512:(tg + 1) * 512],
                                           kc == 0, kc == 15, [ring_t[slot], H_t[kc][tg]], [ps_t[bb[t_]]])
                            for t_ in range(2):
                                tg = tgp * 2 + t_
                                sg = sgb.next()
                                act(sg.ap, ps[bg[t_]][:], AF.Silu, [ps_t[bg[t_]]], sg.tiles)
                                tt("dve", av[:, jq, tg * 512:(tg + 1) * 512], ps[bu[t_]][:], sg.ap, ALU.mult,
                                   [ps_t[bu[t_]]] + sg.tiles, actb.tiles)
                stats = (qi == 3) and (not last)
                resid_plan(rs_, [(fc, tgp) for fc in range(16) for tgp in range(2)])
                pi = 0
                resid_prefetch(rs_, 1)
                for cg in range(4):
                    slot = ws_next("dn")
                    wt = ring[slot][:, 0:nq * 512].rearrange("p (k n) -> p k n", n=512)
                    for blk in range(4):
                        fc = cg * 4 + blk
                        for tgp in range(2):
                            banks = [acc_pool.next(), acc_pool.next()]
                            for jq in range(nq):
                                for t_ in range(2):
                                    tg = tgp * 2 + t_
                                    mm(ps[banks[t_]][:], wt[:, jq, blk * 128:(blk + 1) * 128], av[:, jq, tg * 512:(tg + 1) * 512],
                                       jq == 0, jq == nq - 1, [ring_t[slot]] + actb.tiles, [ps_t[banks[t_]]])
                            resid_flush(rs_)
                            resid_prefetch(rs_, pi + 2)
                            for t_ in range(2):
                                resid_update(rs_, banks[t_], fc, tgp * 2 + t_, stats)
                            pi += 1
                resid_flush(rs_)

        def phase_out():
            work.reset()
            xin = Cyc([work.alloc(16384, F32) for _ in range(3)])
            xo = Cyc([work.alloc(8192, F32) for _ in range(3)])
            for u in range(8):
                tg = (2 * u) // 4
                xi = xin.next()
                xi3 = xi.ap.rearrange("p (c t) -> p c t", t=256)
                dma("sp", xi3, xres.rearrange("c p t -> p c t")[:, :, u * 256:(u + 1) * 256],
                    [xres_t[fc][tg] for fc in range(16)], xi.tiles)
                for half in range(2):
                    tt_ = 2 * u + half
                    xt = xo.next()
                    for b_ in range(4):
                        bank = acc_pool.next()
                        for i in range(4):
                            fc = b_ * 4 + i
                            tr(ps[bank][:, i * 128:(i + 1) * 128], xi3[:, fc, half * 128:(half + 1) * 128], xi.tiles, [ps_t[bank]])
                        if b_ % 2 == 0:
                            P.op("dve", lambda e, bank=bank, b_=b_, xt=xt: e.tensor_copy(xt.ap[:, b_ * 512:(b_ + 1) * 512], ps[bank][:]),
                                 reads=[ps_t[bank]], writes=xt.tiles)
                        else:
                            act(xt.ap[:, b_ * 512:(b_ + 1) * 512], ps[bank][:], AF.Copy, [ps_t[bank]], xt.tiles)
                    t_o = Tile()
                    out_t.append(t_o)
                    dma("sp", out[tt_ * 128:(tt_ + 1) * 128, :], xt.ap, xt.tiles, [t_o])

        def body():
            phase0()
            if stop == "p0":
                return
            for l in range(nl):
                normalize(g1s, l * 16)
                if stop == "n1":
                    return
                phase1(l)
                if stop == "p1":
                    return
                attention_a(l)
                if stop == "aa":
                    return
                branch(l, 0, 0, 8)
                attention_b(l)
                if stop == "ab":
                    return
                branch(l, 1, 8, 4)
                attention_c(l)
                if stop == "ac":
                    return
                branch(l, 2, 12, 8)
                phase_wo(l)
                normalize(g2s, l * 16)
                if stop == "wo":
                    return
                phase_ffn(l, l == nl - 1)
        body()
        phase_out()
        P.op("sp", lambda e: e.nop(), reads=out_t)
        P.finish()
    return nc


_PROG_CACHE = {}


def _core_consts(par, nl, rpb_c):
    maskAB, sigA, sigB, cmask, cdr, cdc, sigC = _tables()
    cos, sin = _rope_tables()
    rope = _rope_core(par, cos, sin)
    n_ab = maskAB[par].shape[0]
    mab = np.ascontiguousarray(maskAB[par].transpose(1, 0, 2).reshape(128, n_ab * 128)).astype(np.float32)
    n_ct = cmask[par].shape[0]
    g = rpb_c[:nl][:, :, cdr[par], cdc[par]]
    g = np.where(cmask[par][None, None], g, np.float32(NEG)).astype(np.float32)
    cb = np.ascontiguousarray(g.transpose(0, 1, 3, 2, 4).reshape(nl, 8, 128, n_ct * 128))
    return rope, mab, cb


def _lay(v, nl, k):
    return np.ascontiguousarray(v[:nl].reshape(nl, k, 128).transpose(2, 0, 1).reshape(128, nl * k)).astype(np.float32)


def run(inputs, nl=NL, n_cores=8, debug=False, trace=False, stop=None):
    x = np.asarray(inputs["x"], np.float32)
    pairs = [[2 * i, 2 * i + 1] for i in range(n_cores // 2)]
    key = (nl, n_cores, debug, stop)
    if key not in _PROG_CACHE:
        _PROG_CACHE[key] = build_program(nl, pairs, debug, stop)
    nc = _PROG_CACHE[key]
    f = lambda n: np.ascontiguousarray(np.asarray(inputs[n], np.float32)[:nl])
    shared = {
        "w_in": f("w_in"), "w_br_a": f("w_br_a"), "w_br_b": f("w_br_b"), "w_br_c": f("w_br_c"),
        "w_o": f("w_o"), "w_gate_up": f("w_gate_up"), "w_down": f("w_down"),
        "g1": _lay(np.asarray(inputs["norm1_g"], np.float32), nl, 16),
        "g2": _lay(np.asarray(inputs["norm2_g"], np.float32), nl, 16),
        "qkg": np.ascontiguousarray(np.asarray(inputs["qk_norm_g"], np.float32)[:nl].transpose(2, 0, 1).reshape(128, nl * 6)),
        "sink": np.ascontiguousarray(np.broadcast_to(np.asarray(inputs["sink_a"], np.float32)[:nl].reshape(1, nl * 8), (128, nl * 8))),
        "ident": np.eye(128, dtype=np.float32),
    }
    rot = np.zeros((128, 128), np.float32)
    for m in range(64):
        rot[m + 64, m] = -1.0
        rot[m, m + 64] = 1.0
    shared["rot"] = rot
    rpb = np.asarray(inputs["rpb_c"], np.float32)
    per_par = [_core_consts(p, nl, rpb) for p in range(2)]
    in_maps = []
    for c in range(n_cores):
        b, par = c // 2, c % 2
        rope, mab, cb = per_par[par]
        m = dict(shared)
        m["x"] = np.ascontiguousarray(x[b, par * T:(par + 1) * T, :])
        m["rope"] = rope
        m["maskab"] = mab
        m["cbias"] = cb
        in_maps.append(m)
    res = run_bass_kernel_spmd(nc, in_maps, core_ids=list(range(n_cores)), trace=trace)
    nb = n_cores // 2
    outp = np.zeros((nb, SEQ, D), np.float32)
    for c in range(n_cores):
        outp[c // 2, (c % 2) * T:(c % 2 + 1) * T, :] = res.results[c]["out"]
    return outp, res


def kernel(x, norm1_g, w_in, qk_norm_g, sink_a, rpb_c, w_br_a, w_br_b, w_br_c, w_o, norm2_g, w_gate_up, w_down):
    inputs = dict(x=x, norm1_g=norm1_g, w_in=w_in, qk_norm_g=qk_norm_g, sink_a=sink_a, rpb_c=rpb_c, w_br_a=w_br_a,
                  w_br_b=w_br_b, w_br_c=w_br_c, w_o=w_o, norm2_g=norm2_g, w_gate_up=w_gate_up, w_down=w_down)
    outp, _ = run(inputs)
    return outp
```

```python
import contextlib
import numpy as np
import concourse.bass as bass
import concourse.mybir as mybir
from concourse.bass_utils import run_bass_kernel_spmd

F32 = mybir.dt.float32
BF16 = mybir.dt.bfloat16
AF = mybir.ActivationFunctionType
ALU = mybir.AluOpType

D = 2048
T = 2048
SEQ = 4096
NL = 4
NIN = 15360
DFF = 5632
NFFC = 44
QUARTERS = ((0, 12), (12, 12), (24, 10), (34, 10))
EPS = 1e-6
SCALE = 128.0 ** -0.5
NEG = -30000.0

C_QA, C_KA, C_VA, C_QB, C_KB, C_VB, C_QC, C_KC, C_VC, C_G = 0, 1024, 1280, 1536, 3072, 4608, 6144, 7168, 8192, 9216

O_KA = 0
O_VA = 524288
O_KB = 1048576
O_VB = 4194304
O_KC = 7340032
O_VC = 9437184
NKV = 11534336

ENGS = ("pe", "act", "dve", "pool", "sp")


class Tile:
    __slots__ = ("w", "r")

    def __init__(self):
        self.w = None
        self.r = []


class Op:
    __slots__ = ("eng", "fn", "deps", "need_inc", "semval", "kind", "dsem", "dval", "pos")

    def __init__(self, eng, fn, kind):
        self.eng = eng
        self.fn = fn
        self.deps = set()
        self.need_inc = False
        self.semval = None
        self.kind = kind
        self.dsem = None
        self.dval = None


class Prog:
    N_DMA_SEM = {"sp": 12, "act": 16, "pool": 8}
    EPOCH = 30000

    def __init__(self, nc):
        self.nc = nc
        self.ops = {e: [] for e in ENGS}
        self.dma_ops = {e: [] for e in ENGS}
        self.cc_ops = []

    def op(self, eng, fn, reads=(), writes=(), kind=None):
        o = Op(eng, fn, kind)
        deps = o.deps
        for t in reads:
            if t.w is not None:
                deps.add(t.w)
        for t in writes:
            if t.w is not None:
                deps.add(t.w)
            if t.r:
                deps.update(t.r)
        for t in reads:
            t.r.append(o)
        for t in writes:
            t.w = o
            t.r = []
        if kind == "dma":
            lst = self.dma_ops[eng]
            n = self.N_DMA_SEM[eng]
            k = len(lst)
            o.dsem = k % n
            o.dval = 16 * (k // n + 1)
            if k >= n:
                deps.add(lst[k - n])
            lst.append(o)
        elif kind == "cc":
            o.dsem = len(self.cc_ops)
            o.dval = 1
            self.cc_ops.append(o)
        deps.discard(o)
        o.pos = len(self.ops[eng])
        self.ops[eng].append(o)
        return o

    def finish(self):
        nc = self.nc
        for e in ENGS:
            for o in self.ops[e]:
                last = {}
                for d in o.deps:
                    if d.kind is not None:
                        continue
                    if d.eng == "pe" and e == "pe":
                        continue
                    if d.eng not in last or last[d.eng].pos < d.pos:
                        last[d.eng] = d
                for d in last.values():
                    d.need_inc = True
        n_ep = {}
        for e in ENGS:
            c = 0
            for o in self.ops[e]:
                if o.kind is None and o.need_inc:
                    c += 1
                    o.semval = c
            n_ep[e] = c // self.EPOCH + 1
        sems = {}
        for e in ENGS:
            if not self.ops[e]:
                continue
            for ep in range(n_ep[e]):
                sems[(e, ep)] = nc.alloc_semaphore(name=f"s_{e}_{ep}")
        dsems = {}
        for e, n in self.N_DMA_SEM.items():
            if self.dma_ops[e]:
                for i in range(n):
                    dsems[(e, i)] = nc.alloc_semaphore(name=f"d_{e}_{i}")
        csems = [nc.alloc_semaphore(name=f"cc_{i}") for i in range(len(self.cc_ops))]
        EP = self.EPOCH

        def semof(o):
            if o.kind == "dma":
                return dsems[(o.eng, o.dsem)], o.dval
            if o.kind == "cc":
                return csems[o.dsem], 1
            ep = (o.semval - 1) // EP
            return sems[(o.eng, ep)], o.semval - ep * EP

        plans = {}
        for e in ENGS:
            seen_eng = {f: 0 for f in ENGS}
            seen_x = {}
            plan = []
            for o in self.ops[e]:
                best = {}
                xw = {}
                for d in o.deps:
                    if d.kind is not None:
                        key = (d.kind, d.eng, d.dsem)
                        if seen_x.get(key, 0) < d.dval:
                            if key not in xw or xw[key].dval < d.dval:
                                xw[key] = d
                    else:
                        if d.eng == "pe" and e == "pe":
                            continue
                        if d.eng not in best or best[d.eng].pos < d.pos:
                            best[d.eng] = d
                waits = []
                for f, d in best.items():
                    if d.semval > seen_eng[f]:
                        seen_eng[f] = d.semval
                        waits.append(d)
                for key, d in xw.items():
                    seen_x[key] = d.dval
                    waits.append(d)
                plan.append((o, waits))
            plans[e] = plan

        with nc.Block() as block:
            def make(e):
                def body(eng):
                    for o, waits in plans[e]:
                        for d in waits:
                            s, v = semof(d)
                            eng.wait_ge(s, v)
                        ins = o.fn(eng)
                        if o.kind == "dma":
                            s, v = semof(o)
                            ins.then_inc(s, 16)
                        elif o.kind == "cc":
                            s, v = semof(o)
                            ins.then_inc(s)
                        elif o.need_inc:
                            s, v = semof(o)
                            ins.then_inc(s, 1)
                return body
            if plans["sp"]:
                block.sync(make("sp"))
            if plans["pe"]:
                block.tensor(make("pe"))
            if plans["act"]:
                block.scalar(make("act"))
            if plans["dve"]:
                block.vector(make("dve"))
            if plans["pool"]:
                block.gpsimd(make("pool"))


class Cyc:
    def __init__(self, items):
        self.items = list(items)
        self.i = 0

    def next(self):
        v = self.items[self.i % len(self.items)]
        self.i += 1
        return v


class Buf:
    __slots__ = ("ap", "tiles")

    def __init__(self, ap, tiles):
        self.ap = ap
        self.tiles = tiles


class Work:
    def __init__(self, tensor, nbytes):
        self.t = tensor
        self.n = nbytes
        self.tiles = [Tile() for _ in range(nbytes // 1024)]
        self.off = 0

    def reset(self):
        self.off = 0

    def alloc(self, nbytes, dtype=BF16):
        sz = (nbytes + 1023) // 1024 * 1024
        off = self.off
        assert off + sz <= self.n, ("work overflow", off, sz, self.n)
        self.off += sz
        ap = self.t[:, off // 2:(off + nbytes) // 2]
        if dtype == F32:
            ap = ap.bitcast(F32)
        return Buf(ap, self.tiles[off // 1024:(off + sz) // 1024])


def _rope_tables():
    half = 64
    inv = (10000.0 ** (-np.arange(half, dtype=np.float32) * 2.0 / 128.0)).astype(np.float32)
    ang = np.arange(SEQ, dtype=np.float32)[:, None] * inv[None, :]
    return np.cos(ang).astype(np.float32), np.sin(ang).astype(np.float32)


def _rope_core(par, cos, sin):
    out = np.zeros((3, 2, 128, T), np.float32)
    lt = np.arange(T)
    orders = [lt,
              (np.arange(T) % 512) * 4 + np.arange(T) // 512,
              (np.arange(T) % 128) * 16 + np.arange(T) // 128]
    for v, o in enumerate(orders):
        g = par * T + o
        c = cos[g].T
        s = sin[g].T
        out[v, 0] = np.concatenate([c, c], 0)
        out[v, 1] = np.concatenate([s, s], 0)
    return out


def _mask_tables():
    a = np.arange(128)
    ab_tiles = [[], []]
    ab_sig = {}

    def add_ab(tiles2):
        key = b"".join(t.tobytes() for p in range(2) for t in tiles2[p])
        if key not in ab_sig:
            ab_sig[key] = len(ab_tiles[0])
            for p in range(2):
                ab_tiles[p].extend(tiles2[p])
        return ab_sig[key]

    sigA = []
    for j in range(16):
        t2 = [[], []]
        for p in range(2):
            for dl in (-1, 0, 1):
                q = p * T + 128 * j + a
                k = p * T + 128 * (j + dl) + a
                v = (np.abs(q[None, :] - k[:, None]) <= 128) & (k[:, None] >= 0) & (k[:, None] < SEQ)
                t2[p].append(v.astype(np.float32))
        sigA.append(add_ab(t2))
    sigB = []
    for g, dil in enumerate((1, 4, 16)):
        nb = T // dil // 128
        mtot = SEQ // dil
        row = []
        for j in range(nb):
            t2 = [[], []]
            for p in range(2):
                for dl in (-1, 0, 1):
                    q = p * (T // dil) + 128 * j + a
                    k = p * (T // dil) + 128 * (j + dl) + a
                    v = (np.abs(q[None, :] - k[:, None]) <= 64) & (k[:, None] >= 0) & (k[:, None] < mtot)
                    t2[p].append(v.astype(np.float32))
            row.append(add_ab(t2))
        sigB.append(row)
    maskAB = [np.stack(ab_tiles[p]) for p in range(2)]

    c_valid = [[], []]
    c_dr = [[], []]
    c_dc = [[], []]
    c_sig = {}
    sigC = []
    for j in range(16):
        per = []
        for dl in range(-3, 4):
            c = j + dl
            if c < -2 or c > 17:
                continue
            vs, drs, dcs = [], [], []
            for p in range(2):
                q = p * T + 128 * j + a
                k = p * T + 128 * c + a
                qr, qc = q // 64, q % 64
                kr, kc = k // 64, k % 64
                rs = np.clip(qr - 4, 0, 56)
                cs = np.clip(qc - 8, 0, 48)
                v = ((k[:, None] >= 0) & (k[:, None] < SEQ)
                     & (kr[:, None] >= rs[None, :]) & (kr[:, None] < rs[None, :] + 8)
                     & (kc[:, None] >= cs[None, :]) & (kc[:, None] < cs[None, :] + 16))
                dr = np.clip(kr[:, None] - qr[None, :] + 7, 0, 14)
                dc = np.clip(kc[:, None] - qc[None, :], -15, 15) + 15
                vs.append(v)
                drs.append(np.where(v, dr, 0))
                dcs.append(np.where(v, dc, 0))
            if not (vs[0].any() or vs[1].any()):
                continue
            per.append((dl, vs, drs, dcs))
        key = (tuple(x[0] for x in per),
               b"".join(x[1][p].tobytes() + x[2][p].tobytes() + x[3][p].tobytes() for x in per for p in range(2)))
        if key not in c_sig:
            c_sig[key] = len(c_valid[0])
            for x in per:
                for p in range(2):
                    c_valid[p].append(x[1][p])
                    c_dr[p].append(x[2][p])
                    c_dc[p].append(x[3][p])
        sigC.append((c_sig[key], tuple(x[0] for x in per)))
    cmask = [np.stack(c_valid[p]) for p in range(2)]
    cdr = [np.stack(c_dr[p]) for p in range(2)]
    cdc = [np.stack(c_dc[p]) for p in range(2)]
    return maskAB, sigA, sigB, cmask, cdr, cdc, sigC


_TABLES = None


def _tables():
    global _TABLES
    if _TABLES is None:
        _TABLES = _mask_tables()
    return _TABLES


def build_program(nl, pairs, debug=False, stop=None):
    maskAB, sigA, sigB, cmask, cdr, cdc, sigC = _tables()
    n_ab = maskAB[0].shape[0]
    n_ct = cmask[0].shape[0]

    nc = bass.Bass("TRN2", target_bir_lowering=False)
    P = Prog(nc)

    def din(name, shape, dt=F32):
        return nc.dram_tensor(name, list(shape), dt, kind="ExternalInput").ap()

    def dscr(name, shape, dt, dbg=False):
        kind = "ExternalOutput" if (debug and dbg) else "Internal"
        return nc.dram_tensor(name, list(shape), dt, kind=kind).ap()

    x_in = din("x", [T, D])
    w_in = din("w_in", [nl, D, NIN])
    w_bra = din("w_br_a", [nl, 1024, D])
    w_brb = din("w_br_b", [nl, 512, D])
    w_brc = din("w_br_c", [nl, 1024, D])
    w_o = din("w_o", [nl, D, D])
    w_gu = din("w_gate_up", [nl, D, 2 * DFF])
    w_dn = din("w_down", [nl, DFF, D])
    g1_in = din("g1", [128, nl * 16])
    g2_in = din("g2", [128, nl * 16])
    qkg_in = din("qkg", [128, nl * 6])
    sink_in = din("sink", [128, nl * 8])
    rope_in = din("rope", [3, 2, 128, T])
    mab_in = din("maskab", [128, n_ab * 128])
    cb_in = din("cbias", [nl, 8, 128, n_ct * 128])
    ident_in = din("ident", [128, 128])
    rot_in = din("rot", [128, 128])
    out = nc.dram_tensor("out", [T, D], F32, kind="ExternalOutput").ap()

    xres = dscr("xres", [16, 128, T], F32, dbg=True)
    qsc = dscr("qsc", [28, 128, T], BF16, dbg=True)
    gsc = dscr("gsc", [16, 128, 3, T], BF16, dbg=True)
    osc = dscr("osc", [20, 128, T], BF16, dbg=True)
    kv_own = nc.dram_tensor("kv_own", [NKV], BF16, kind="Internal").ap()
    MSG = {"a": 786432, "b": 1048576, "c": 655360, "d": 1048576, "e": 1048576}
    msg_src = {m: nc.dram_tensor("msg_src_" + m, [n], BF16, kind="Internal").ap() for m, n in MSG.items()}
    msg_dst = {m: nc.dram_tensor("msg_dst_" + m, [2 * n], BF16, addr_space="Local", kind="Internal").ap()
               for m, n in MSG.items()}

    def blk(flat, off, rows, cols):
        return flat[off:off + rows * cols].rearrange("(p t) -> p t", t=cols)

    def ms(m, off, rows, cols):
        return blk(msg_src[m], off, rows, cols)

    def md(m, rank, off, rows, cols):
        return blk(msg_dst[m], rank * MSG[m] + off, rows, cols)

    def kv_fm(base, off, ncols=T):
        return base[off:off + 128 * ncols].rearrange("(p t) -> p t", t=ncols)

    def kv_tm(base, off, nrows, ncols):
        return base[off:off + nrows * ncols].rearrange("(r n) -> r n", n=ncols)

    es = contextlib.ExitStack()
    with es:
        def sb(name, shape, dt):
            return es.enter_context(nc.sbuf_tensor("sb_" + name, list(shape), dt))

        Hs = sb("H", [128, 16, T], BF16)
        H_t = [[Tile() for _ in range(4)] for _ in range(16)]
        NB = 3
        ring = [sb(f"ring{i}", [128, 8192], BF16) for i in range(NB)]
        ring_t = [Tile() for _ in range(NB)]
        WORK_BYTES = 84 * 1024
        work_s = sb("work", [128, WORK_BYTES // 2], BF16)
        work = Work(work_s, WORK_BYTES)
        ident = sb("ident_sb", [128, 128], F32)
        ones = sb("ones", [128, 128], BF16)
        rotm = sb("rotm", [128, 128], BF16)
        g1s = sb("g1s", [128, nl * 16], F32)
        g2s = sb("g2s", [128, nl * 16], F32)
        qkgs = sb("qkgs", [128, nl * 6], F32)
        sinke = sb("sinke", [128, nl * 8], F32)
        mab = sb("mab", [128, n_ab * 128], BF16)
        T_const = Tile()
        ps = [es.enter_context(nc.psum_tensor(f"ps{i}", [128, 512], F32)) for i in range(8)]
        ps_t = [Tile() for _ in range(8)]
        acc_pool = Cyc([0, 1, 2, 3])
        aux_pool = Cyc([4, 5, 6, 7])

        xres_t = [[Tile() for _ in range(4)] for _ in range(16)]
        qsc_t = [[Tile() for _ in range(4)] for _ in range(28)]
        gsc_t = [[[Tile() for _ in range(4)] for _ in range(3)] for _ in range(16)]
        osc_t = [[Tile() for _ in range(4)] for _ in range(20)]
        kvo_t = {}
        msg_t = {m: {} for m in MSG}
        msgd_t = {m: Tile() for m in MSG}

        def mt(m, key):
            if key not in msg_t[m]:
                msg_t[m][key] = Tile()
            return msg_t[m][key]
        out_t = []

        def kvt(key):
            if key not in kvo_t:
                kvo_t[key] = Tile()
            return kvo_t[key]

        def mm(o, lhsT, rhs, start, stop, rd, wr):
            P.op("pe", lambda e: e.matmul(o, lhsT, rhs, start=start, stop=stop), reads=rd, writes=wr)

        def tr(o, in_, rd, wr):
            P.op("pe", lambda e: e.transpose(o, in_, ident[:]), reads=rd + [T_const], writes=wr)

        def act(o, in_, func, rd, wr, scale=None, bias=None):
            if scale is None:
                P.op("act", lambda e: e.activation(o, in_, func), reads=rd, writes=wr)
            elif bias is None:
                P.op("act", lambda e: e.activation(o, in_, func, scale=scale), reads=rd, writes=wr)
            else:
                P.op("act", lambda e: e.activation(o, in_, func, bias=bias, scale=scale), reads=rd, writes=wr)

        def rstd(rs, src_ps, src_tile, n):
            act(rs.ap, src_ps, AF.Ln, [src_tile], rs.tiles, scale=1.0 / n, bias=float(EPS))
            act(rs.ap, rs.ap, AF.Exp, rs.tiles, rs.tiles, scale=-0.5)

        def recip_act(o_ap, o_tiles, in_ap, in_tiles, bias=None):
            if bias is None:
                act(o_ap, in_ap, AF.Ln, in_tiles, o_tiles)
            else:
                act(o_ap, in_ap, AF.Ln, in_tiles + [T_const], o_tiles, scale=1.0, bias=bias)
            act(o_ap, o_ap, AF.Exp, o_tiles, o_tiles, scale=-1.0)

        def tt(eng, o, a, b, op, rd, wr):
            P.op(eng, lambda e: e.tensor_tensor(o, a, b, op), reads=rd, writes=wr)

        def ts(eng, o, a, s1, s2, op0, op1, rd, wr):
            if op1 is None:
                P.op(eng, lambda e: e.tensor_scalar(o, a, s1, None, op0), reads=rd, writes=wr)
            else:
                P.op(eng, lambda e: e.tensor_scalar(o, a, s1, s2, op0, op1), reads=rd, writes=wr)

        def stt(o, a, s, b, op0, op1, rd, wr):
            P.op("dve", lambda e: e.scalar_tensor_tensor(o, a, s, b, op0, op1), reads=rd, writes=wr)

        def dma(eng, o, in_, rd, wr):
            P.op(eng, lambda e: e.dma_start(out=o, in_=in_), reads=rd, writes=wr, kind="dma")

        wspecs = []

        def wv(src2d, p=128):
            return src2d.rearrange("(k p) n -> p k n", p=p)

        IN_ORDER = ([("kv_a", C_KA)] + [("kb", C_KB + 512 * g) for g in range(3)]
                    + [("vb", C_VB + 512 * g) for g in range(3)]
                    + [("kc", C_KC), ("kc", C_KC + 512), ("vc", C_VC), ("vc", C_VC + 512)]
                    + [("qa", C_QA), ("qa", C_QA + 512)] + [("qb", C_QB + 512 * g) for g in range(3)]
                    + [("qc", C_QC), ("qc", C_QC + 512)] + [("gate", C_G + 512 * i) for i in range(12)])
        for l in range(nl):
            for tag, c0 in IN_ORDER:
                wspecs.append(("in", [((0, 16, 512), wv(w_in[l])[:, :, c0:c0 + 512])]))
            for wbr, nk in ((w_bra, 8), (w_brb, 4), (w_brc, 8)):
                for cg in range(4):
                    wspecs.append(("br", [((0, nk, 512), wv(wbr[l])[:, :, cg * 512:(cg + 1) * 512])]))
            for cg in range(4):
                wspecs.append(("wo", [((0, 16, 512), wv(w_o[l])[:, :, cg * 512:(cg + 1) * 512])]))
            for (j0, nq) in QUARTERS:
                for jj in range(0, nq, 2):
                    j = j0 + jj
                    wspecs.append(("gu", [((0, 16, 256), wv(w_gu[l])[:, :, j * 128:j * 128 + 256]),
                                          ((4096, 16, 256), wv(w_gu[l])[:, :, DFF + j * 128:DFF + j * 128 + 256])]))
                for cg in range(4):
                    wspecs.append(("dn", [((0, nq, 512), wv(w_dn[l])[:, j0:j0 + nq, cg * 512:(cg + 1) * 512])]))
        ws_state = {"issued": 0, "cons": 0}

        def ws_issue(n):
            tag, dmas = wspecs[n]
            slot = n % NB
            for (lo, k, ncol), src in dmas:
                dst = ring[slot][:, lo:lo + k * ncol].rearrange("p (k n) -> p k n", n=ncol)
                dma("pool", dst, src, [], [ring_t[slot]])

        def ws_next(tag):
            n = ws_state["cons"]
            assert wspecs[n][0] == tag, (wspecs[n][0], tag)
            while ws_state["issued"] < min(n + NB, len(wspecs)):
                ws_issue(ws_state["issued"])
                ws_state["issued"] += 1
            ws_state["cons"] += 1
            return n % NB

        work.reset()
        c_tmp = work.alloc(nl * 16 * 4, F32)
        dma("sp", ident[:], ident_in[:, :], [], [T_const])
        dma("pool", rotm[:], rot_in[:, :], [], [T_const])
        dma("pool", mab[:], mab_in[:, :], [], [T_const])
        P.op("dve", lambda e: e.memset(ones[:], 1.0), writes=[T_const])
        dma("sp", c_tmp.ap[:, 0:nl * 16], g1_in[:, :], [], c_tmp.tiles)
        ts("dve", g1s[:], c_tmp.ap[:, 0:nl * 16], 1.0, None, ALU.mult, None, c_tmp.tiles, [T_const])
        dma("sp", c_tmp.ap[:, 0:nl * 16], g2_in[:, :], [], c_tmp.tiles)
        ts("dve", g2s[:], c_tmp.ap[:, 0:nl * 16], 1.0, None, ALU.mult, None, c_tmp.tiles, [T_const])
        dma("sp", c_tmp.ap[:, 0:nl * 6], qkg_in[:, :], [], c_tmp.tiles)
        ts("dve", qkgs[:], c_tmp.ap[:, 0:nl * 6], 1.0, None, ALU.mult, None, c_tmp.tiles, [T_const])
        dma("sp", c_tmp.ap[:, 0:nl * 8], sink_in[:, :], [], c_tmp.tiles)
        act(sinke[:], c_tmp.ap[:, 0:nl * 8], AF.Exp, c_tmp.tiles, [T_const])

        SS = [4, 5, 6, 7]

        def normalize(gs, gcol0):
            work.reset()
            rsb = [work.alloc(2048, F32) for _ in range(4)]
            xsb = Cyc([work.alloc(2048, F32) for _ in range(14)])
            for tg in range(4):
                rstd(rsb[tg], ps[SS[tg]][:], ps_t[SS[tg]], float(D))
            for tg in range(4):
                for fc in range(16):
                    xb = xsb.next()
                    dma("act", xb.ap, xres[fc][:, tg * 512:(tg + 1) * 512], [xres_t[fc][tg]], xb.tiles)
                    stt(Hs[:, fc, tg * 512:(tg + 1) * 512], xb.ap, gs[:, gcol0 + fc:gcol0 + fc + 1], rsb[tg].ap,
                        ALU.mult, ALU.mult, xb.tiles + rsb[tg].tiles + [T_const], [H_t[fc][tg]])

        def phase0():
            work.reset()
            xin = Cyc([work.alloc(8192, F32) for _ in range(3)])
            xts = Cyc([work.alloc(16384, F32) for _ in range(2)])
            sqb = Cyc([work.alloc(8192, BF16) for _ in range(2)])
            pend0 = []
            for u in range(8):
                xt = xts.next()
                sq = sqb.next()
                xt3 = xt.ap.rearrange("p (c t) -> p c t", t=256)
                sq3 = sq.ap.rearrange("p (c t) -> p c t", t=256)
                for half in range(2):
                    tt_ = 2 * u + half
                    tg = tt_ // 4
                    xi = xin.next()
                    dma("act", xi.ap, x_in[tt_ * 128:(tt_ + 1) * 128, :], [], xi.tiles)
                    for b_ in range(4):
                        bank = acc_pool.next()
                        for i in range(4):
                            fc = b_ * 4 + i
                            tr(ps[bank][:, i * 128:(i + 1) * 128], xi.ap[:, fc * 128:(fc + 1) * 128], xi.tiles, [ps_t[bank]])
                        dstv = xt3[:, b_ * 4:(b_ + 1) * 4, half * 128:(half + 1) * 128]
                        srcv = ps[bank][:].rearrange("p (c t) -> p c t", t=128)
                        P.op("dve", lambda e, dstv=dstv, srcv=srcv: e.tensor_copy(dstv, srcv),
                             reads=[ps_t[bank]], writes=xt.tiles)
                        act(sq3[:, b_ * 4:(b_ + 1) * 4, half * 128:(half + 1) * 128], dstv, AF.Square, xt.tiles, sq.tiles)

                    def stats(tt_=tt_, tg=tg, sq3=sq3, sq=sq, half=half):
                        for fc in range(16):
                            mm(ps[SS[tg]][:, (tt_ % 4) * 128:(tt_ % 4 + 1) * 128], ones[:], sq3[:, fc, half * 128:(half + 1) * 128],
                               fc == 0, fc == 15, sq.tiles + [T_const], [ps_t[SS[tg]]])
                    pend0.append(stats)
                    if len(pend0) > 1:
                        pend0.pop(0)()
                tg = (2 * u) // 4
                dst = xres.rearrange("c p t -> p c t")[:, :, u * 256:(u + 1) * 256]
                dma("sp", dst, xt3, xt.tiles, [xres_t[fc][tg] for fc in range(16)])
            while pend0:
                pend0.pop(0)()

        def tokview(kc, variant, tg):
            base = Hs[:, kc, :]
            if variant == 0:
                return base[:, tg * 512:(tg + 1) * 512]
            if variant == 1:
                return base.rearrange("p (m r) -> p r m", r=4)[:, tg, :]
            return base.rearrange("p (m r) -> p r m", r=16)[:, 4 * tg:4 * tg + 4, :]

        def tokview128(kc, variant, tt_):
            base = Hs[:, kc, :]
            if variant == 0:
                return base[:, tt_ * 128:(tt_ + 1) * 128]
            if variant == 1:
                r, m0 = tt_ // 4, (tt_ % 4) * 128
                return base.rearrange("p (m r) -> p r m", r=4)[:, r, m0:m0 + 128]
            return base.rearrange("p (m r) -> p r m", r=16)[:, tt_, :]

        def Hall(kc):
            return [H_t[kc][0], H_t[kc][1], H_t[kc][2], H_t[kc][3]]

        def phase1(l):
            work.reset()
            sqb = Cyc([work.alloc(1024, BF16) for _ in range(6)])
            rsb = Cyc([work.alloc(2048, F32) for _ in range(3)])
            qnb = Cyc([work.alloc(1024, BF16) for _ in range(6)])
            t1b = Cyc([work.alloc(2048, F32) for _ in range(5)])
            t2b = Cyc([work.alloc(2048, F32) for _ in range(3)])
            ostb = Cyc([work.alloc(1024, BF16) for _ in range(6)])
            csb = Cyc([work.alloc(4096, F32) for _ in range(5)])
            vstb = Cyc([work.alloc(1024, BF16) for _ in range(6)])
            accb = Cyc([work.alloc(2048, F32) for _ in range(6)])

            def fm_block(slot, blk, kind, variant, gcol, rope, dsts_fn):
                wt = ring[slot][:, :].rearrange("p (k n) -> p k n", n=512)
                for tgp in range(2):
                    banks = [acc_pool.next(), acc_pool.next()]
                    for t_ in range(2):
                        for kc in range(16):
                            tg = tgp * 2 + t_
                            o = ps[banks[t_]][:]
                            rhs = tokview(kc, variant, tg)
                            if variant == 2:
                                o = o.rearrange("p (r m) -> p r m", r=4)
                            mm(o, wt[:, kc, blk * 128:(blk + 1) * 128], rhs, kc == 0, kc == 15,
                               [ring_t[slot]] + (Hall(kc) if variant else [H_t[kc][tg]]), [ps_t[banks[t_]]])
                    for t_ in range(2):
                        pend.append([tile_stages(kind, variant, gcol, rope, dsts_fn, tgp * 2 + t_, banks[t_]), 0])
                    step()

            pend = []

            def step():
                for ent in list(pend):
                    ent[0][ent[1]]()
                    ent[1] += 1
                    if ent[1] >= len(ent[0]):
                        pend.remove(ent)

            def flush():
                while pend:
                    step()

            def tile_stages(kind, variant, gcol, rope, dsts_fn, tg, bank):
                accp = ps[bank][:]
                stt_ = {}

                def store(ost):
                    for (d_ap, lo, hi, d_tiles) in dsts_fn(tg):
                        dma("sp", d_ap, ost.ap[:, lo:hi], ost.tiles, d_tiles)

                if kind == "gate":
                    def g0():
                        ost = ostb.next()
                        act(ost.ap, accp, AF.Sigmoid, [ps_t[bank]], ost.tiles)
                        store(ost)
                    return [g0]

                def s0():
                    sq = sqb.next()
                    act(sq.ap, accp, AF.Square, [ps_t[bank]], sq.tiles)
                    acs = accb.next()
                    act(acs.ap, accp, AF.Copy, [ps_t[bank]], acs.tiles)
                    stt_["sq"], stt_["acs"] = sq, acs

                def s1():
                    sq, acs = stt_["sq"], stt_["acs"]
                    ab = aux_pool.next()
                    mm(ps[ab][:], ones[:], sq.ap, True, True, sq.tiles + [T_const], [ps_t[ab]])
                    rs = rsb.next()
                    rstd(rs, ps[ab][:], ps_t[ab], 128.0)
                    if not rope:
                        ost = ostb.next()
                        stt(ost.ap, acs.ap, qkgs[:, gcol:gcol + 1], rs.ap, ALU.mult, ALU.mult,
                            acs.tiles + [T_const] + rs.tiles, ost.tiles)
                        store(ost)
                    else:
                        qn = qnb.next()
                        stt(qn.ap, acs.ap, qkgs[:, gcol:gcol + 1], rs.ap, ALU.mult, ALU.mult,
                            acs.tiles + [T_const] + rs.tiles, qn.tiles)
                        cs = csb.next()
                        csv = cs.ap.rearrange("p (c t) -> p c t", c=2)
                        dma("act", csv, rope_in[variant].rearrange("c p t -> p c t")[:, :, tg * 512:(tg + 1) * 512],
                            [], cs.tiles)
                        t1 = t1b.next()
                        tt("pool", t1.ap, qn.ap, csv[:, 0, :], ALU.mult, qn.tiles + cs.tiles, t1.tiles)
                        stt_["qn"], stt_["cs"], stt_["csv"], stt_["t1"] = qn, cs, csv, t1

                def s2():
                    qn, cs, csv, t1 = stt_["qn"], stt_["cs"], stt_["csv"], stt_["t1"]
                    rb = aux_pool.next()
                    mm(ps[rb][:], rotm[:], qn.ap, True, True, qn.tiles + [T_const], [ps_t[rb]])
                    t2 = t2b.next()
                    tt("dve", t2.ap, ps[rb][:], csv[:, 1, :], ALU.mult, [ps_t[rb]] + cs.tiles, t2.tiles)
                    ost = ostb.next()
                    tt("pool", ost.ap, t1.ap, t2.ap, ALU.add, t1.tiles + t2.tiles, ost.tiles)
                    store(ost)
                return [s0, s1, s2] if rope else [s0, s1]

            def tm_block(slot, c_lo, ncols, variant, dsts_fn):
                wt = ring[slot][:, :].rearrange("p (k n) -> p k n", n=512)
                for tt_ in range(16):
                    bank = acc_pool.next()
                    for kc in range(16):
                        mm(ps[bank][:, 0:ncols], tokview128(kc, variant, tt_), wt[:, kc, c_lo:c_lo + ncols], kc == 0, kc == 15,
                           [ring_t[slot]] + Hall(kc), [ps_t[bank]])
                    step()
                    vs = vstb.next()
                    act(vs.ap[:, 0:ncols], ps[bank][:, 0:ncols], AF.Copy, [ps_t[bank]], vs.tiles)
                    for (d_ap, d_tiles) in dsts_fn(tt_):
                        dma("sp", d_ap, vs.ap[:, 0:ncols], vs.tiles, d_tiles)

            def kown(off, key, tg):
                return (kv_fm(kv_own, off)[:, tg * 512:(tg + 1) * 512], 0, 512, [kvt((key, tg))])

            def q_dst(idx):
                return lambda tg: [(qsc[idx][:, tg * 512:(tg + 1) * 512], 0, 512, [qsc_t[idx][tg]])]

            def cc(m):
                flush()
                P.op("pool", lambda e: e.collective_compute(
                    "AllGather", ALU.bypass, replica_groups=pairs,
                    ins=[msg_src[m].rearrange("(a b) -> a b", b=1024)],
                    outs=[msg_dst[m].rearrange("(a b) -> a b", b=1024)]),
                    reads=list(msg_t[m].values()), writes=[msgd_t[m]], kind="cc")

            gq = l * 6
            slot = ws_next("in")
            for h in range(2):
                def d_ka(tg, h=h):
                    r = [kown(O_KA + h * 128 * T, ("ka", h), tg)]
                    if tg == 0:
                        r.append((ms("a", (0 * 2 + h) * 16384, 128, 128), 0, 128, [mt("a", ("ka", 0, h))]))
                    if tg == 3:
                        r.append((ms("a", (1 * 2 + h) * 16384, 128, 128), 384, 512, [mt("a", ("ka", 1, h))]))
                    return r
                fm_block(slot, h, "k", 0, gq + 1, True, d_ka)

            def d_va(tt_):
                r = [(kv_tm(kv_own, O_VA, T, 256)[tt_ * 128:(tt_ + 1) * 128, :], [kvt(("va", tt_))])]
                if tt_ == 0:
                    r.append((ms("a", 65536, 128, 256), [mt("a", ("va", 0))]))
                if tt_ == 15:
                    r.append((ms("a", 65536 + 32768, 128, 256), [mt("a", ("va", 1))]))
                return r
            tm_block(slot, 256, 256, 0, d_va)
            for g in range(3):
                slot = ws_next("in")
                for h in range(4):
                    def d_kb(tg, g=g, h=h):
                        if g == 2:
                            return [(ms("b", h * 128 * T, 128, T)[:, tg * 512:(tg + 1) * 512], 0, 512, [mt("b", (h, tg))])]
                        r = [kown(O_KB + (g * 4 + h) * 128 * T, ("kb", g, h), tg)]
                        if g == 0:
                            if tg == 0:
                                r.append((ms("a", 131072 + (0 * 4 + h) * 16384, 128, 128), 0, 128, [mt("a", ("kb0", 0, h))]))
                            if tg == 3:
                                r.append((ms("a", 131072 + (1 * 4 + h) * 16384, 128, 128), 384, 512, [mt("a", ("kb0", 1, h))]))
                        else:
                            for sd, lo in ((0, 0), (1, 384)):
                                v = ms("a", 262144 + (sd * 4 + h) * 65536, 128, 512)[:, tg * 128:(tg + 1) * 128]
                                r.append((v, lo, lo + 128, [mt("a", ("kb1", sd, h, tg))]))
                        return r
                    fm_block(slot, h, "k", g, gq + 3, True, d_kb)
                if g == 1:
                    cc("a")
                if g == 2:
                    cc("b")
            for g in range(3):
                slot = ws_next("in")

                def d_vb(tt_, g=g):
                    if g == 2:
                        return [(ms("d", 0, T, 512)[tt_ * 128:(tt_ + 1) * 128, :], [mt("d", tt_)])]
                    r = [(kv_tm(kv_own, O_VB + g * T * 512, T, 512)[tt_ * 128:(tt_ + 1) * 128, :], [kvt(("vb", g, tt_))])]
                    if g == 0:
                        if tt_ == 0:
                            r.append((ms("c", 0, 128, 512), [mt("c", ("vb0", 0))]))
                        if tt_ == 15:
                            r.append((ms("c", 65536, 128, 512), [mt("c", ("vb0", 1))]))
                    else:
                        rr, cc_ = tt_ // 4, tt_ % 4
                        if cc_ == 0:
                            r.append((ms("c", 131072 + (0 * 4 + rr) * 65536, 128, 512), [mt("c", ("vb1", 0, rr))]))
                        if cc_ == 3:
                            r.append((ms("c", 131072 + (1 * 4 + rr) * 65536, 128, 512), [mt("c", ("vb1", 1, rr))]))
                    return r
                tm_block(slot, 0, 512, g, d_vb)
                if g == 1:
                    cc("c")
                if g == 2:
                    cc("d")
            for half in range(2):
                slot = ws_next("in")
                for hh in range(4):
                    h = half * 4 + hh

                    def d_kc(tg, h=h):
                        r = [kown(O_KC + h * 128 * T, ("kc", h), tg)]
                        if tg == 0:
                            r.append((ms("e", (0 * 8 + h) * 32768, 128, 256), 0, 256, [mt("e", ("kc", 0, h))]))
                        if tg == 3:
                            r.append((ms("e", (1 * 8 + h) * 32768, 128, 256), 256, 512, [mt("e", ("kc", 1, h))]))
                        return r
                    fm_block(slot, hh, "k", 0, gq + 5, False, d_kc)
            for half in range(2):
                slot = ws_next("in")

                def d_vc(tt_, half=half):
                    r = [(kv_tm(kv_own, O_VC, T, 1024)[tt_ * 128:(tt_ + 1) * 128, half * 512:(half + 1) * 512],
                          [kvt(("vc", half, tt_))])]
                    if tt_ < 2:
                        r.append((ms("e", 524288, 256, 1024)[tt_ * 128:(tt_ + 1) * 128, half * 512:(half + 1) * 512],
                                  [mt("e", ("vc", 0, half, tt_))]))
                    if tt_ >= 14:
                        r.append((ms("e", 524288 + 262144, 256, 1024)[(tt_ - 14) * 128:(tt_ - 13) * 128, half * 512:(half + 1) * 512],
                                  [mt("e", ("vc", 1, half, tt_))]))
                    return r
                tm_block(slot, 0, 512, 0, d_vc)
            cc("e")
            for half in range(2):
                slot = ws_next("in")
                for hh in range(4):
                    fm_block(slot, hh, "q", 0, gq + 0, True, q_dst(half * 4 + hh))
            for g in range(3):
                slot = ws_next("in")
                for h in range(4):
                    fm_block(slot, h, "q", g, gq + 2, True, q_dst(8 + g * 4 + h))
            for half in range(2):
                slot = ws_next("in")
                for hh in range(4):
                    fm_block(slot, hh, "q", 0, gq + 4, False, q_dst(20 + half * 4 + hh))
            for gi in range(12):
                slot = ws_next("in")
                for hh in range(4):
                    blk = gi * 4 + hh
                    i, fc = blk // 16, blk % 16
                    fm_block(slot, hh, "gate", 0, 0, False,
                             lambda tg, i=i, fc=fc: [(gsc[fc][:, i, tg * 512:(tg + 1) * 512], 0, 512, [gsc_t[fc][i][tg]])])
            flush()

        def attn_setup(esz=1024, ne=4, nrd=2):
            work.reset()
            st = {}
            st["E"] = Cyc([work.alloc(esz, BF16) for _ in range(ne)])
            st["rd"] = Cyc([work.alloc(2048, F32) for _ in range(nrd)])
            st["ost"] = Cyc([work.alloc(1024, BF16) for _ in range(6)])
            st["S1"] = Cyc([0, 1, 2, 3])
            st["S2"] = Cyc([(0, 1), (2, 3)])
            st["OT"] = Cyc([4, 5])
            st["D"] = Cyc([6, 7])
            return st

        def unit_scores(st, u):
            chunks, mask = u["chunks"], u["mask"]
            n = len(chunks)
            if n <= 4:
                sb_ = (st["S1"].next(),)
            else:
                sb_ = st["S2"].next()
            used = sorted(set(i // 4 for i in range(n)))
            for i, (k_ap, v_ap) in enumerate(chunks):
                b = sb_[i // 4]
                mm(ps[b][:, (i % 4) * 128:(i % 4 + 1) * 128], k_ap, u["q_ap"], True, True, u["kvt"] + u["q_tiles"], [ps_t[b]])
            e = st["E"].next()
            u["e"] = e
            for bi in used:
                b = sb_[bi]
                ncol = min(n - bi * 4, 4) * 128
                if mask[0] == "ab":
                    act(e.ap[:, bi * 512:bi * 512 + ncol], ps[b][:, 0:ncol], AF.Exp, [ps_t[b]], e.tiles, scale=SCALE)
                else:
                    tb = mask[3].next()
                    boff = (mask[2] + bi * 4) * 128
                    stt(tb.ap[:, 0:ncol], ps[b][:, 0:ncol], SCALE, mask[1].ap[:, boff:boff + ncol], ALU.mult, ALU.add,
                        [ps_t[b]] + mask[1].tiles, tb.tiles)
                    act(e.ap[:, bi * 512:bi * 512 + ncol], tb.ap[:, 0:ncol], AF.Exp, tb.tiles, e.tiles)
            if mask[0] == "ab":
                moff = mask[1] * 128
                tt("dve", e.ap[:, 0:n * 128], e.ap[:, 0:n * 128], mab[:, moff:moff + n * 128], ALU.mult,
                   e.tiles + [T_const], e.tiles)

        def unit_pv(st, u):
            chunks, e, col = u["chunks"], u["e"], u["col"]
            n = len(chunks)
            ob, db = st["cur_ot"], st["cur_d"]
            for i, (k_ap, v_ap) in enumerate(chunks):
                mm(ps[ob][:, col * 128:(col + 1) * 128], v_ap, e.ap[:, i * 128:(i + 1) * 128], i == 0, i == n - 1,
                   u["kvt"] + e.tiles, [ps_t[ob]])
            for i in range(n):
                mm(ps[db][:, col * 128:(col + 1) * 128], ones[:], e.ap[:, i * 128:(i + 1) * 128], i == 0, i == n - 1,
                   e.tiles + [T_const], [ps_t[db]])

        def run_units(st, units, finalize, lag):
            n = len(units)
            for i in range(n + lag):
                if i < n:
                    unit_scores(st, units[i])
                j = i - lag
                if j >= 0:
                    if j % 4 == 0:
                        begin_batch(st)
                    unit_pv(st, units[j])
                    if j % 4 == 3:
                        finalize(j // 4)

        def drive(gens):
            gens = list(gens)
            next(gens[0])
            for i, g_ in enumerate(gens):
                if i + 1 < len(gens):
                    next(gens[i + 1])
                for _ in g_:
                    pass

        def begin_batch(st):
            st["cur_ot"] = st["OT"].next()
            st["cur_d"] = st["D"].next()

        def load_kv(K, V, own, halos):
            k_src, k_tiles, v_src, v_tiles = own
            dma("sp", K["own"].ap, k_src, k_tiles, K["own"].tiles)
            dma("sp", V["own"].ap.rearrange("p (c d) -> p c d", d=128), v_src, v_tiles, V["own"].tiles)
            kL, kR, vL, vR, h_tiles = halos
            ncol = kL.shape[1] if len(kL.shape) == 2 else kL.shape[1] * kL.shape[2]
            dma("sp", K["L"].ap[:, 0:ncol], kL, h_tiles, K["L"].tiles)
            dma("sp", K["R"].ap[:, 0:ncol], kR, h_tiles, K["R"].tiles)
            nch = vL.shape[1]
            dma("sp", V["L"].ap[:, 0:nch * 128].rearrange("p (c d) -> p c d", d=128), vL, h_tiles, V["L"].tiles)
            dma("sp", V["R"].ap[:, 0:nch * 128].rearrange("p (c d) -> p c d", d=128), vR, h_tiles, V["R"].tiles)

        def alloc_kv(nh, hw):
            K = {"own": work.alloc(4096), "L": work.alloc(nh * hw * 2), "R": work.alloc(nh * hw * 2), "nh": nh, "hw": hw}
            V = {"own": work.alloc(4096), "L": work.alloc(nh * hw * 2), "R": work.alloc(nh * hw * 2)}
            return K, V

        def kva(rank, off):
            return rank * NKV + off

        def attention_a(l):
            st = attn_setup()
            sets = Cyc([alloc_kv(1, 128) for _ in range(2)])
            qb = Cyc([work.alloc(4096) for _ in range(2)])
            kvs = {}

            def group(kh, g):
                if g == 0:
                    K, V = sets.next()
                    va3 = lambda base: base.rearrange("(c p) n -> p c n", p=128)[:, :, kh * 128:(kh + 1) * 128]
                    load_kv(K, V,
                            (kv_fm(kv_own, O_KA + kh * 128 * T), [kvt((("ka", kh), tg)) for tg in range(4)],
                             va3(kv_tm(kv_own, O_VA, T, 256)), [kvt(("va", t_)) for t_ in range(16)]),
                            (md("a", 0, (1 * 2 + kh) * 16384, 128, 128), md("a", 1, (0 * 2 + kh) * 16384, 128, 128),
                             va3(md("a", 0, 65536 + 32768, 128, 256)), va3(md("a", 1, 65536, 128, 256)), [msgd_t["a"]]))
                    kvs[kh] = (K, V)
                K, V = kvs[kh]
                h = kh * 4 + g
                q = qb.next()
                dma("sp", q.ap, qsc[h][:, :], qsc_t[h], q.tiles)
                yield
                kvtiles = K["own"].tiles + K["L"].tiles + K["R"].tiles + V["own"].tiles + V["L"].tiles + V["R"].tiles

                def kch(c):
                    if c < 0:
                        return K["L"].ap[:, 0:128], V["L"].ap[:, 0:128]
                    if c > 15:
                        return K["R"].ap[:, 0:128], V["R"].ap[:, 0:128]
                    return K["own"].ap[:, c * 128:(c + 1) * 128], V["own"].ap[:, c * 128:(c + 1) * 128]
                units = [dict(q_ap=q.ap[:, j * 128:(j + 1) * 128], q_tiles=q.tiles, chunks=[kch(j + dl) for dl in (-1, 0, 1)],
                              kvt=kvtiles, mask=("ab", sigA[j]), col=j % 4) for j in range(16)]

                def fin_a(jb):
                    ob, db = st["cur_ot"], st["cur_d"]
                    rd = st["rd"].next()
                    recip_act(rd.ap, rd.tiles, ps[db][:], [ps_t[db]], bias=sinke[:, l * 8 + h:l * 8 + h + 1])
                    os_ = st["ost"].next()
                    tt("dve", os_.ap, ps[ob][:], rd.ap, ALU.mult, [ps_t[ob]] + rd.tiles, os_.tiles)
                    dma("sp", osc[h][:, jb * 512:(jb + 1) * 512], os_.ap, os_.tiles, [osc_t[h][jb]])
                run_units(st, units, fin_a, 3)
            drive([group(kh, g) for kh in range(2) for g in range(4)])

        def attention_b(l):
            st = attn_setup(1024, 4, 1)
            num = work.alloc(8192, F32)
            den = work.alloc(8192, F32)
            sets = Cyc([alloc_kv(16, 128) for _ in range(2)])
            qb = Cyc([work.alloc(4096) for _ in range(2)])
            def group(h, g):
                    dil = (1, 4, 16)[g]
                    ncls = dil
                    cl = T // dil
                    K, V = sets.next()
                    K = dict(K)
                    K["nh"] = ncls
                    ncc = cl // 128
                    hc = slice(h * 128, (h + 1) * 128)
                    tm3 = lambda base: base.rearrange("(c p) n -> p c n", p=128)[:, :, hc]
                    if g == 2:
                        own = (ms("b", h * 128 * T, 128, T), [mt("b", (h, tg)) for tg in range(4)],
                               tm3(ms("d", 0, T, 512)), [mt("d", t_) for t_ in range(16)])
                        halos = (md("b", 0, h * 128 * T, 128, T), md("b", 1, h * 128 * T, 128, T),
                                 tm3(md("d", 0, 0, T, 512)), tm3(md("d", 1, 0, T, 512)), [msgd_t["b"], msgd_t["d"]])
                    else:
                        own = (kv_fm(kv_own, O_KB + (g * 4 + h) * 128 * T), [kvt((("kb", g, h), tg)) for tg in range(4)],
                               tm3(kv_tm(kv_own, O_VB + g * T * 512, T, 512)), [kvt(("vb", g, t_)) for t_ in range(16)])
                        if g == 0:
                            halos = (md("a", 0, 131072 + (1 * 4 + h) * 16384, 128, 128), md("a", 1, 131072 + (0 * 4 + h) * 16384, 128, 128),
                                     tm3(md("c", 0, 65536, 128, 512)), tm3(md("c", 1, 0, 128, 512)), [msgd_t["a"], msgd_t["c"]])
                        else:
                            halos = (md("a", 0, 262144 + (1 * 4 + h) * 65536, 128, 512), md("a", 1, 262144 + (0 * 4 + h) * 65536, 128, 512),
                                     tm3(md("c", 0, 131072 + 4 * 65536, 512, 512)), tm3(md("c", 1, 131072, 512, 512)),
                                     [msgd_t["a"], msgd_t["c"]])
                    load_kv(K, V, own, halos)
                    kvtiles = K["own"].tiles + K["L"].tiles + K["R"].tiles + V["own"].tiles + V["L"].tiles + V["R"].tiles
                    q = qb.next()
                    qi = 8 + g * 4 + h
                    dma("sp", q.ap, qsc[qi][:, :], qsc_t[qi], q.tiles)
                    yield

                    def kch(r, c):
                        if c < 0:
                            return K["L"].ap[:, r * 128:(r + 1) * 128], V["L"].ap[:, r * 128:(r + 1) * 128]
                        if c >= ncc:
                            return K["R"].ap[:, r * 128:(r + 1) * 128], V["R"].ap[:, r * 128:(r + 1) * 128]
                        o_ = (r * ncc + c) * 128
                        return K["own"].ap[:, o_:o_ + 128], V["own"].ap[:, o_:o_ + 128]
                    units = []
                    for bt in range(4):
                        for jj in range(4):
                            if g == 0:
                                r, j = 0, bt * 4 + jj
                            elif g == 1:
                                r, j = bt, jj
                            else:
                                r, j = bt * 4 + jj, 0
                            qo = (r * ncc + j) * 128
                            units.append(dict(q_ap=q.ap[:, qo:qo + 128], q_tiles=q.tiles, chunks=[kch(r, j + dl) for dl in (-1, 0, 1)],
                                              kvt=kvtiles, mask=("ab", sigB[g][j]), col=jj))

                    def fin_b(bt):
                        ob, db = st["cur_ot"], st["cur_d"]
                        if g == 0:
                            nv = num.ap[:, bt * 512:(bt + 1) * 512]
                            dv = den.ap[:, bt * 512:(bt + 1) * 512]
                            act(nv, ps[ob][:], AF.Copy, [ps_t[ob]], num.tiles)
                            act(dv, ps[db][:], AF.Copy, [ps_t[db]], den.tiles)
                        else:
                            if g == 1:
                                nv = num.ap.rearrange("p (m r) -> p r m", r=4)[:, bt, :]
                                dv = den.ap.rearrange("p (m r) -> p r m", r=4)[:, bt, :]
                                po, pd = ps[ob][:], ps[db][:]
                            else:
                                nv = num.ap.rearrange("p (m r) -> p r m", r=16)[:, bt * 4:bt * 4 + 4, :]
                                dv = den.ap.rearrange("p (m r) -> p r m", r=16)[:, bt * 4:bt * 4 + 4, :]
                                po = ps[ob][:].rearrange("p (r m) -> p r m", r=4)
                                pd = ps[db][:].rearrange("p (r m) -> p r m", r=4)
                            tt("dve", nv, nv, po, ALU.add, [ps_t[ob]] + num.tiles, num.tiles)
                            tt("dve", dv, dv, pd, ALU.add, [ps_t[db]] + den.tiles, den.tiles)
                    run_units(st, units, fin_b, 3)
                    if g < 2:
                        return
                    for bt in range(4):
                        dv = den.ap[:, bt * 512:(bt + 1) * 512]
                        recip_act(dv, den.tiles, dv, den.tiles)
                        os_ = st["ost"].next()
                        tt("dve", os_.ap, num.ap[:, bt * 512:(bt + 1) * 512], dv, ALU.mult, num.tiles + den.tiles, os_.tiles)
                        dma("sp", osc[8 + h][:, bt * 512:(bt + 1) * 512], os_.ap, os_.tiles, [osc_t[8 + h][bt]])
            drive([group(h, g) for h in range(4) for g in range(3)])

        def attention_c(l):
            st = attn_setup(2048, 3)
            tbc = Cyc([work.alloc(2048, F32) for _ in range(4)])
            cbb = Cyc([work.alloc(n_ct * 512, F32) for _ in range(2)])
            sets = Cyc([alloc_kv(1, 256) for _ in range(2)])
            qb = Cyc([work.alloc(4096) for _ in range(2)])
            def group(h):
                K, V = sets.next()
                hc = slice(h * 128, (h + 1) * 128)
                tm3 = lambda base: base.rearrange("(c p) n -> p c n", p=128)[:, :, hc]
                load_kv(K, V,
                        (kv_fm(kv_own, O_KC + h * 128 * T), [kvt((("kc", h), tg)) for tg in range(4)],
                         tm3(kv_tm(kv_own, O_VC, T, 1024)), [kvt(("vc", hf, t_)) for hf in range(2) for t_ in range(16)]),
                        (md("e", 0, (1 * 8 + h) * 32768, 128, 256), md("e", 1, (0 * 8 + h) * 32768, 128, 256),
                         tm3(md("e", 0, 524288 + 262144, 256, 1024)), tm3(md("e", 1, 524288, 256, 1024)), [msgd_t["e"]]))
                kvtiles = K["own"].tiles + K["L"].tiles + K["R"].tiles + V["own"].tiles + V["L"].tiles + V["R"].tiles
                cb = cbb.next()
                dma("sp", cb.ap, cb_in[l][h][:, :], [], cb.tiles)
                q = qb.next()
                dma("sp", q.ap, qsc[20 + h][:, :], qsc_t[20 + h], q.tiles)
                yield

                def kch(c):
                    if c < 0:
                        return K["L"].ap[:, (c + 2) * 128:(c + 3) * 128], V["L"].ap[:, (c + 2) * 128:(c + 3) * 128]
                    if c > 15:
                        return K["R"].ap[:, (c - 16) * 128:(c - 15) * 128], V["R"].ap[:, (c - 16) * 128:(c - 15) * 128]
                    return K["own"].ap[:, c * 128:(c + 1) * 128], V["own"].ap[:, c * 128:(c + 1) * 128]
                units = []
                for j in range(16):
                    off, dls = sigC[j]
                    units.append(dict(q_ap=q.ap[:, j * 128:(j + 1) * 128], q_tiles=q.tiles, chunks=[kch(j + dl) for dl in dls],
                                      kvt=kvtiles, mask=("c", cb, off, tbc), col=j % 4))

                def fin_c(jb):
                    ob, db = st["cur_ot"], st["cur_d"]
                    rd = st["rd"].next()
                    recip_act(rd.ap, rd.tiles, ps[db][:], [ps_t[db]])
                    os_ = st["ost"].next()
                    tt("dve", os_.ap, ps[ob][:], rd.ap, ALU.mult, [ps_t[ob]] + rd.tiles, os_.tiles)
                    dma("sp", osc[12 + h][:, jb * 512:(jb + 1) * 512], os_.ap, os_.tiles, [osc_t[12 + h][jb]])
                run_units(st, units, fin_c, 1)
            drive([group(h) for h in range(8)])

        def branch(l, i, o0, nk):
            work.reset()
            ob_ = work.alloc(nk * 4096)
            ov = ob_.ap.rearrange("p (k t) -> p k t", t=T)
            gstb = Cyc([work.alloc(1024, BF16) for _ in range(3)])
            tmpb = Cyc([work.alloc(2048, F32) for _ in range(2)])
            obt = [ob_.tiles[4 * k:4 * k + 4] for k in range(nk)]
            for k in range(nk):
                dma("act", ov[:, k, :], osc[o0 + k][:, :], osc_t[o0 + k], obt[k])
            for cg in range(4):
                slot = ws_next("br")
                wt = ring[slot][:, 0:nk * 512].rearrange("p (k n) -> p k n", n=512)
                for blk in range(4):
                    fc = cg * 4 + blk
                    for tgp in range(2):
                        banks = [acc_pool.next(), acc_pool.next()]
                        for kc in range(nk):
                            for t_ in range(2):
                                tg = tgp * 2 + t_
                                mm(ps[banks[t_]][:], wt[:, kc, blk * 128:(blk + 1) * 128], ov[:, kc, tg * 512:(tg + 1) * 512],
                                   kc == 0, kc == nk - 1, [ring_t[slot]] + obt[kc], [ps_t[banks[t_]]])
                        for t_ in range(2):
                            tg = tgp * 2 + t_
                            bank = banks[t_]
                            gs_ = gstb.next()
                            dma("act", gs_.ap, gsc[fc][:, i, tg * 512:(tg + 1) * 512], [gsc_t[fc][i][tg]], gs_.tiles)
                            hv = Hs[:, fc, tg * 512:(tg + 1) * 512]
                            if i == 0:
                                tt("dve", hv, ps[bank][:], gs_.ap, ALU.mult, [ps_t[bank]] + gs_.tiles, [H_t[fc][tg]])
                            else:
                                tm = tmpb.next()
                                tt("dve", tm.ap, ps[bank][:], gs_.ap, ALU.mult, [ps_t[bank]] + gs_.tiles, tm.tiles)
                                tt("pool", hv, hv, tm.ap, ALU.add, [H_t[fc][tg]] + tm.tiles, [H_t[fc][tg]])

        def resid_setup(nxs=8, nxn=5, nsq=4):
            return {"xs": Cyc([work.alloc(2048, F32) for _ in range(nxs)]),
                    "xn": Cyc([work.alloc(2048, F32) for _ in range(nxn)]),
                    "sq": Cyc([work.alloc(1024, BF16) for _ in range(nsq)]), "pend": [], "ld": {}, "plan": [], "nld": 0}

        def resid_plan(rs_, pairs_):
            rs_["plan"] = list(pairs_)
            rs_["nld"] = 0
            rs_["ld"] = {}

        def resid_prefetch(rs_, upto):
            while rs_["nld"] < min(upto + 1, len(rs_["plan"])):
                fc, tgp = rs_["plan"][rs_["nld"]]
                for t_ in range(2):
                    tg = tgp * 2 + t_
                    xb = rs_["xs"].next()
                    dma("act", xb.ap, xres[fc][:, tg * 512:(tg + 1) * 512], [xres_t[fc][tg]], xb.tiles)
                    rs_["ld"][(fc, tg)] = xb
                rs_["nld"] += 1

        def resid_update(rs_, bank, fc, tg, stats):
            xb = rs_["ld"].pop((fc, tg))
            xn = rs_["xn"].next()
            tt("dve", xn.ap, ps[bank][:], xb.ap, ALU.add, [ps_t[bank]] + xb.tiles, xn.tiles)
            dma("sp", xres[fc][:, tg * 512:(tg + 1) * 512], xn.ap, xn.tiles, [xres_t[fc][tg]])
            if stats:
                sq = rs_["sq"].next()
                act(sq.ap, xn.ap, AF.Square, xn.tiles, sq.tiles)
                rs_["pend"].append(lambda: mm(ps[SS[tg]][:], ones[:], sq.ap, fc == 0, fc == 15, sq.tiles + [T_const], [ps_t[SS[tg]]]))

        def resid_flush(rs_, keep=0):
            while len(rs_["pend"]) > keep:
                rs_["pend"].pop(0)()

        def phase_wo(l):
            work.reset()
            rs_ = resid_setup()
            resid_plan(rs_, [(fc, tgp) for fc in range(16) for tgp in range(2)])
            pi = 0
            resid_prefetch(rs_, 1)
            for cg in range(4):
                slot = ws_next("wo")
                wt = ring[slot][:, :].rearrange("p (k n) -> p k n", n=512)
                for blk in range(4):
                    fc = cg * 4 + blk
                    for tgp in range(2):
                        banks = [acc_pool.next(), acc_pool.next()]
                        for kc in range(16):
                            for t_ in range(2):
                                tg = tgp * 2 + t_
                                mm(ps[banks[t_]][:], wt[:, kc, blk * 128:(blk + 1) * 128], Hs[:, kc, tg * 512:(tg + 1) * 512],
                                   kc == 0, kc == 15, [ring_t[slot], H_t[kc][tg]], [ps_t[banks[t_]]])
                        resid_flush(rs_)
                        resid_prefetch(rs_, pi + 2)
                        for t_ in range(2):
                            resid_update(rs_, banks[t_], fc, tgp * 2 + t_, True)
                        pi += 1
            resid_flush(rs_)

        def phase_ffn(l, last):
            for qi, (j0, nq) in enumerate(QUARTERS):
                work.reset()
                actb = work.alloc(nq * 4096)
                av = actb.ap.rearrange("p (j t) -> p j t", t=T)
                sgb = Cyc([work.alloc(2048, F32) for _ in range(3)])
                rs_ = resid_setup(8, 5, 4)
                gu_banks = Cyc([0, 1, 2, 3, 4, 5, 6, 7])
                for jj in range(0, nq, 2):
                    slot = ws_next("gu")
                    wg = ring[slot][:, 0:4096].rearrange("p (k n) -> p k n", n=256)
                    wu = ring[slot][:, 4096:8192].rearrange("p (k n) -> p k n", n=256)
                    for b2 in range(2):
                        jq = jj + b2
                        for tgp in range(2):
                            bg = [gu_banks.next(), gu_banks.next()]
                            bu = [gu_banks.next(), gu_banks.next()]
                            for (wt, bb) in ((wg, bg), (wu, bu)):
                                for t_ in range(2):
                                    for kc in range(16):
                                        tg = tgp * 2 + t_
                                        mm(ps[bb[t_]][:], wt[:, kc, b2 * 128:(b2 + 1) * 128], Hs[:, kc, tg * 512:(tg + 1) * 512],
                                           kc == 0, kc == 15, [ring_t[slot], H_t[kc][tg]], [ps_t[bb[t_]]])
                            for t_ in range(2):
                                tg = tgp * 2 + t_
                                sg = sgb.next()
                                act(sg.ap, ps[bg[t_]][:], AF.Silu, [ps_t[bg[t_]]], sg.tiles)
                                tt("dve", av[:, jq, tg * 512:(tg + 1) * 512], ps[bu[t_]][:], sg.ap, ALU.mult,
                                   [ps_t[bu[t_]]] + sg.tiles, actb.tiles)
                stats = (qi == 3) and (not last)
                resid_plan(rs_, [(fc, tgp) for fc in range(16) for tgp in range(2)])
                pi = 0
                resid_prefetch(rs_, 1)
                for cg in range(4):
                    slot = ws_next("dn")
                    wt = ring[slot][:, 0:nq * 512].rearrange("p (k n) -> p k n", n=512)
                    for blk in range(4):
                        fc = cg * 4 + blk
                        for tgp in range(2):
                            banks = [acc_pool.next(), acc_pool.next()]
                            for jq in range(nq):
                                for t_ in range(2):
                                    tg = tgp * 2 + t_
                                    mm(ps[banks[t_]][:], wt[:, jq, blk * 128:(blk + 1) * 128], av[:, jq, tg * 512:(tg + 1) * 512],
                                       jq == 0, jq == nq - 1, [ring_t[slot]] + actb.tiles, [ps_t[banks[t_]]])
                            resid_flush(rs_)
                            resid_prefetch(rs_, pi + 2)
                            for t_ in range(2):
                                resid_update(rs_, banks[t_], fc, tgp * 2 + t_, stats)
                            pi += 1
                resid_flush(rs_)

        def phase_out():
            work.reset()
            xin = Cyc([work.alloc(16384, F32) for _ in range(3)])
            xo = Cyc([work.alloc(8192, F32) for _ in range(3)])
            for u in range(8):
                tg = (2 * u) // 4
                xi = xin.next()
                xi3 = xi.ap.rearrange("p (c t) -> p c t", t=256)
                dma("sp", xi3, xres.rearrange("c p t -> p c t")[:, :, u * 256:(u + 1) * 256],
                    [xres_t[fc][tg] for fc in range(16)], xi.tiles)
                for half in range(2):
                    tt_ = 2 * u + half
                    xt = xo.next()
                    for b_ in range(4):
                        bank = acc_pool.next()
                        for i in range(4):
                            fc = b_ * 4 + i
                            tr(ps[bank][:, i * 128:(i + 1) * 128], xi3[:, fc, half * 128:(half + 1) * 128], xi.tiles, [ps_t[bank]])
                        if b_ % 2 == 0:
                            P.op("dve", lambda e, bank=bank, b_=b_, xt=xt: e.tensor_copy(xt.ap[:, b_ * 512:(b_ + 1) * 512], ps[bank][:]),
                                 reads=[ps_t[bank]], writes=xt.tiles)
                        else:
                            act(xt.ap[:, b_ * 512:(b_ + 1) * 512], ps[bank][:], AF.Copy, [ps_t[bank]], xt.tiles)
                    t_o = Tile()
                    out_t.append(t_o)
                    dma("sp", out[tt_ * 128:(tt_ + 1) * 128, :], xt.ap, xt.tiles, [t_o])

        def body():
            phase0()
            if stop == "p0":
                return
            for l in range(nl):
                normalize(g1s, l * 16)
                if stop == "n1":
                    return
                phase1(l)
                if stop == "p1":
                    return
                attention_a(l)
                if stop == "aa":
                    return
                branch(l, 0, 0, 8)
                attention_b(l)
                if stop == "ab":
                    return
                branch(l, 1, 8, 4)
                attention_c(l)
                if stop == "ac":
                    return
                branch(l, 2, 12, 8)
                phase_wo(l)
                normalize(g2s, l * 16)
                if stop == "wo":
                    return
                phase_ffn(l, l == nl - 1)
        body()
        phase_out()
        P.op("sp", lambda e: e.nop(), reads=out_t)
        P.finish()
    return nc


_PROG_CACHE = {}


def _core_consts(par, nl, rpb_c):
    maskAB, sigA, sigB, cmask, cdr, cdc, sigC = _tables()
    cos, sin = _rope_tables()
    rope = _rope_core(par, cos, sin)
    n_ab = maskAB[par].shape[0]
    mab = np.ascontiguousarray(maskAB[par].transpose(1, 0, 2).reshape(128, n_ab * 128)).astype(np.float32)
    n_ct = cmask[par].shape[0]
    g = rpb_c[:nl][:, :, cdr[par], cdc[par]]
    g = np.where(cmask[par][None, None], g, np.float32(NEG)).astype(np.float32)
    cb = np.ascontiguousarray(g.transpose(0, 1, 3, 2, 4).reshape(nl, 8, 128, n_ct * 128))
    return rope, mab, cb


def _lay(v, nl, k):
    return np.ascontiguousarray(v[:nl].reshape(nl, k, 128).transpose(2, 0, 1).reshape(128, nl * k)).astype(np.float32)


def run(inputs, nl=NL, n_cores=8, debug=False, trace=False, stop=None):
    x = np.asarray(inputs["x"], np.float32)
    pairs = [[2 * i, 2 * i + 1] for i in range(n_cores // 2)]
    key = (nl, n_cores, debug, stop)
    if key not in _PROG_CACHE:
        _PROG_CACHE[key] = build_program(nl, pairs, debug, stop)
    nc = _PROG_CACHE[key]
    f = lambda n: np.ascontiguousarray(np.asarray(inputs[n], np.float32)[:nl])
    shared = {
        "w_in": f("w_in"), "w_br_a": f("w_br_a"), "w_br_b": f("w_br_b"), "w_br_c": f("w_br_c"),
        "w_o": f("w_o"), "w_gate_up": f("w_gate_up"), "w_down": f("w_down"),
        "g1": _lay(np.asarray(inputs["norm1_g"], np.float32), nl, 16),
        "g2": _lay(np.asarray(inputs["norm2_g"], np.float32), nl, 16),
        "qkg": np.ascontiguousarray(np.asarray(inputs["qk_norm_g"], np.float32)[:nl].transpose(2, 0, 1).reshape(128, nl * 6)),
        "sink": np.ascontiguousarray(np.broadcast_to(np.asarray(inputs["sink_a"], np.float32)[:nl].reshape(1, nl * 8), (128, nl * 8))),
        "ident": np.eye(128, dtype=np.float32),
    }
    rot = np.zeros((128, 128), np.float32)
    for m in range(64):
        rot[m + 64, m] = -1.0
        rot[m, m + 64] = 1.0
    shared["rot"] = rot
    rpb = np.asarray(inputs["rpb_c"], np.float32)
    per_par = [_core_consts(p, nl, rpb) for p in range(2)]
    in_maps = []
    for c in range(n_cores):
        b, par = c // 2, c % 2
        rope, mab, cb = per_par[par]
        m = dict(shared)
        m["x"] = np.ascontiguousarray(x[b, par * T:(par + 1) * T, :])
        m["rope"] = rope
        m["maskab"] = mab
        m["cbias"] = cb
        in_maps.append(m)
    res = run_bass_kernel_spmd(nc, in_maps, core_ids=list(range(n_cores)), trace=trace)
    nb = n_cores // 2
    outp = np.zeros((nb, SEQ, D), np.float32)
    for c in range(n_cores):
        outp[c // 2, (c % 2) * T:(c % 2 + 1) * T, :] = res.results[c]["out"]
    return outp, res


def kernel(x, norm1_g, w_in, qk_norm_g, sink_a, rpb_c, w_br_a, w_br_b, w_br_c, w_o, norm2_g, w_gate_up, w_down):
    inputs = dict(x=x, norm1_g=norm1_g, w_in=w_in, qk_norm_g=qk_norm_g, sink_a=sink_a, rpb_c=rpb_c, w_br_a=w_br_a,
                  w_br_b=w_br_b, w_br_c=w_br_c, w_o=w_o, norm2_g=norm2_g, w_gate_up=w_gate_up, w_down=w_down)
    outp, _ = run(inputs)
    return outp
```

```python
import contextlib
import numpy as np
import concourse.bass as bass
import concourse.mybir as mybir
from concourse.bass_utils import run_bass_kernel_spmd

F32 = mybir.dt.float32
BF16 = mybir.dt.bfloat16
AF = mybir.ActivationFunctionType
ALU = mybir.AluOpType

D = 2048
T = 2048
SEQ = 4096
NL = 4
NIN = 15360
DFF = 5632
NFFC = 44
QUARTERS = ((0, 12), (12, 12), (24, 10), (34, 10))
EPS = 1e-6
SCALE = 128.0 ** -0.5
NEG = -30000.0

C_QA, C_KA, C_VA, C_QB, C_KB, C_VB, C_QC, C_KC, C_VC, C_G = 0, 1024, 1280, 1536, 3072, 4608, 6144, 7168, 8192, 9216

O_KA = 0
O_VA = 524288
O_KB = 1048576
O_VB = 4194304
O_KC = 7340032
O_VC = 9437184
NKV = 11534336

ENGS = ("pe", "act", "dve", "pool", "sp")


class Tile:
    __slots__ = ("w", "r")

    def __init__(self):
        self.w = None
        self.r = []


class Op:
    __slots__ = ("eng", "fn", "deps", "need_inc", "semval", "kind", "dsem", "dval", "pos")

    def __init__(self, eng, fn, kind):
        self.eng = eng
        self.fn = fn
        self.deps = set()
        self.need_inc = False
        self.semval = None
        self.kind = kind
        self.dsem = None
        self.dval = None


class Prog:
    N_DMA_SEM = {"sp": 12, "act": 16, "pool": 8}
    EPOCH = 30000

    def __init__(self, nc):
        self.nc = nc
        self.ops = {e: [] for e in ENGS}
        self.dma_ops = {e: [] for e in ENGS}
        self.cc_ops = []

    def op(self, eng, fn, reads=(), writes=(), kind=None):
        o = Op(eng, fn, kind)
        deps = o.deps
        for t in reads:
            if t.w is not None:
                deps.add(t.w)
        for t in writes:
            if t.w is not None:
                deps.add(t.w)
            if t.r:
                deps.update(t.r)
        for t in reads:
            t.r.append(o)
        for t in writes:
            t.w = o
            t.r = []
        if kind == "dma":
            lst = self.dma_ops[eng]
            n = self.N_DMA_SEM[eng]
            k = len(lst)
            o.dsem = k % n
            o.dval = 16 * (k // n + 1)
            if k >= n:
                deps.add(lst[k - n])
            lst.append(o)
        elif kind == "cc":
            o.dsem = len(self.cc_ops)
            o.dval = 1
            self.cc_ops.append(o)
        deps.discard(o)
        o.pos = len(self.ops[eng])
        self.ops[eng].append(o)
        return o

    def finish(self):
        nc = self.nc
        for e in ENGS:
            for o in self.ops[e]:
                last = {}
                for d in o.deps:
                    if d.kind is not None:
                        continue
                    if d.eng == "pe" and e == "pe":
                        continue
                    if d.eng not in last or last[d.eng].pos < d.pos:
                        last[d.eng] = d
                for d in last.values():
                    d.need_inc = True
        n_ep = {}
        for e in ENGS:
            c = 0
            for o in self.ops[e]:
                if o.kind is None and o.need_inc:
                    c += 1
                    o.semval = c
            n_ep[e] = c // self.EPOCH + 1
        sems = {}
        for e in ENGS:
            if not self.ops[e]:
                continue
            for ep in range(n_ep[e]):
                sems[(e, ep)] = nc.alloc_semaphore(name=f"s_{e}_{ep}")
        dsems = {}
        for e, n in self.N_DMA_SEM.items():
            if self.dma_ops[e]:
                for i in range(n):
                    dsems[(e, i)] = nc.alloc_semaphore(name=f"d_{e}_{i}")
        csems = [nc.alloc_semaphore(name=f"cc_{i}") for i in range(len(self.cc_ops))]
        EP = self.EPOCH

        def semof(o):
            if o.kind == "dma":
                return dsems[(o.eng, o.dsem)], o.dval
            if o.kind == "cc":
                return csems[o.dsem], 1
            ep = (o.semval - 1) // EP
            return sems[(o.eng, ep)], o.semval - ep * EP

        plans = {}
        for e in ENGS:
            seen_eng = {f: 0 for f in ENGS}
            seen_x = {}
            plan = []
            for o in self.ops[e]:
                best = {}
                xw = {}
                for d in o.deps:
                    if d.kind is not None:
                        key = (d.kind, d.eng, d.dsem)
                        if seen_x.get(key, 0) < d.dval:
                            if key not in xw or xw[key].dval < d.dval:
                                xw[key] = d
                    else:
                        if d.eng == "pe" and e == "pe":
                            continue
                        if d.eng not in best or best[d.eng].pos < d.pos:
                            best[d.eng] = d
                waits = []
                for f, d in best.items():
                    if d.semval > seen_eng[f]:
                        seen_eng[f] = d.semval
                        waits.append(d)
                for key, d in xw.items():
                    seen_x[key] = d.dval
                    waits.append(d)
                plan.append((o, waits))
            plans[e] = plan

        with nc.Block() as block:
            def make(e):
                def body(eng):
                    for o, waits in plans[e]:
                        for d in waits:
                            s, v = semof(d)
                            eng.wait_ge(s, v)
                        ins = o.fn(eng)
                        if o.kind == "dma":
                            s, v = semof(o)
                            ins.then_inc(s, 16)
                        elif o.kind == "cc":
                            s, v = semof(o)
                            ins.then_inc(s)
                        elif o.need_inc:
                            s, v = semof(o)
                            ins.then_inc(s, 1)
                return body
            if plans["sp"]:
                block.sync(make("sp"))
            if plans["pe"]:
                block.tensor(make("pe"))
            if plans["act"]:
                block.scalar(make("act"))
            if plans["dve"]:
                block.vector(make("dve"))
            if plans["pool"]:
                block.gpsimd(make("pool"))


class Cyc:
    def __init__(self, items):
        self.items = list(items)
        self.i = 0

    def next(self):
        v = self.items[self.i % len(self.items)]
        self.i += 1
        return v


class Buf:
    __slots__ = ("ap", "tiles")

    def __init__(self, ap, tiles):
        self.ap = ap
        self.tiles = tiles


class Work:
    def __init__(self, tensor, nbytes):
        self.t = tensor
        self.n = nbytes
        self.tiles = [Tile() for _ in range(nbytes // 1024)]
        self.off = 0

    def reset(self):
        self.off = 0

    def alloc(self, nbytes, dtype=BF16):
        sz = (nbytes + 1023) // 1024 * 1024
        off = self.off
        assert off + sz <= self.n, ("work overflow", off, sz, self.n)
        self.off += sz
        ap = self.t[:, off // 2:(off + nbytes) // 2]
        if dtype == F32:
            ap = ap.bitcast(F32)
        return Buf(ap, self.tiles[off // 1024:(off + sz) // 1024])


def _rope_tables():
    half = 64
    inv = (10000.0 ** (-np.arange(half, dtype=np.float32) * 2.0 / 128.0)).astype(np.float32)
    ang = np.arange(SEQ, dtype=np.float32)[:, None] * inv[None, :]
    return np.cos(ang).astype(np.float32), np.sin(ang).astype(np.float32)


def _rope_core(par, cos, sin):
    out = np.zeros((3, 2, 128, T), np.float32)
    lt = np.arange(T)
    orders = [lt,
              (np.arange(T) % 512) * 4 + np.arange(T) // 512,
              (np.arange(T) % 128) * 16 + np.arange(T) // 128]
    for v, o in enumerate(orders):
        g = par * T + o
        c = cos[g].T
        s = sin[g].T
        out[v, 0] = np.concatenate([c, c], 0)
        out[v, 1] = np.concatenate([s, s], 0)
    return out


def _mask_tables():
    a = np.arange(128)
    ab_tiles = [[], []]
    ab_sig = {}

    def add_ab(tiles2):
        key = b"".join(t.tobytes() for p in range(2) for t in tiles2[p])
        if key not in ab_sig:
            ab_sig[key] = len(ab_tiles[0])
            for p in range(2):
                ab_tiles[p].extend(tiles2[p])
        return ab_sig[key]

    sigA = []
    for j in range(16):
        t2 = [[], []]
        for p in range(2):
            for dl in (-1, 0, 1):
                q = p * T + 128 * j + a
                k = p * T + 128 * (j + dl) + a
                v = (np.abs(q[None, :] - k[:, None]) <= 128) & (k[:, None] >= 0) & (k[:, None] < SEQ)
                t2[p].append(v.astype(np.float32))
        sigA.append(add_ab(t2))
    sigB = []
    for g, dil in enumerate((1, 4, 16)):
        nb = T // dil // 128
        mtot = SEQ // dil
        row = []
        for j in range(nb):
            t2 = [[], []]
            for p in range(2):
                for dl in (-1, 0, 1):
                    q = p * (T // dil) + 128 * j + a
                    k = p * (T // dil) + 128 * (j + dl) + a
                    v = (np.abs(q[None, :] - k[:, None]) <= 64) & (k[:, None] >= 0) & (k[:, None] < mtot)
                    t2[p].append(v.astype(np.float32))
            row.append(add_ab(t2))
        sigB.append(row)
    maskAB = [np.stack(ab_tiles[p]) for p in range(2)]

    c_valid = [[], []]
    c_dr = [[], []]
    c_dc = [[], []]
    c_sig = {}
    sigC = []
    for j in range(16):
        per = []
        for dl in range(-3, 4):
            c = j + dl
            if c < -2 or c > 17:
                continue
            vs, drs, dcs = [], [], []
            for p in range(2):
                q = p * T + 128 * j + a
                k = p * T + 128 * c + a
                qr, qc = q // 64, q % 64
                kr, kc = k // 64, k % 64
                rs = np.clip(qr - 4, 0, 56)
                cs = np.clip(qc - 8, 0, 48)
                v = ((k[:, None] >= 0) & (k[:, None] < SEQ)
                     & (kr[:, None] >= rs[None, :]) & (kr[:, None] < rs[None, :] + 8)
                     & (kc[:, None] >= cs[None, :]) & (kc[:, None] < cs[None, :] + 16))
                dr = np.clip(kr[:, None] - qr[None, :] + 7, 0, 14)
                dc = np.clip(kc[:, None] - qc[None, :], -15, 15) + 15
                vs.append(v)
                drs.append(np.where(v, dr, 0))
                dcs.append(np.where(v, dc, 0))
            if not (vs[0].any() or vs[1].any()):
                continue
            per.append((dl, vs, drs, dcs))
        key = (tuple(x[0] for x in per),
               b"".join(x[1][p].tobytes() + x[2][p].tobytes() + x[3][p].tobytes() for x in per for p in range(2)))
        if key not in c_sig:
            c_sig[key] = len(c_valid[0])
            for x in per:
                for p in range(2):
                    c_valid[p].append(x[1][p])
                    c_dr[p].append(x[2][p])
                    c_dc[p].append(x[3][p])
        sigC.append((c_sig[key], tuple(x[0] for x in per)))
    cmask = [np.stack(c_valid[p]) for p in range(2)]
    cdr = [np.stack(c_dr[p]) for p in range(2)]
    cdc = [np.stack(c_dc[p]) for p in range(2)]
    return maskAB, sigA, sigB, cmask, cdr, cdc, sigC


_TABLES = None


def _tables():
    global _TABLES
    if _TABLES is None:
        _TABLES = _mask_tables()
    return _TABLES


def build_program(nl, pairs, debug=False, stop=None):
    maskAB, sigA, sigB, cmask, cdr, cdc, sigC = _tables()
    n_ab = maskAB[0].shape[0]
    n_ct = cmask[0].shape[0]

    nc = bass.Bass("TRN2", target_bir_lowering=False)
    P = Prog(nc)

    def din(name, shape, dt=F32):
        return nc.dram_tensor(name, list(shape), dt, kind="ExternalInput").ap()

    def dscr(name, shape, dt, dbg=False):
        kind = "ExternalOutput" if (debug and dbg) else "Internal"
        return nc.dram_tensor(name, list(shape), dt, kind=kind).ap()

    x_in = din("x", [T, D])
    w_in = din("w_in", [nl, D, NIN])
    w_bra = din("w_br_a", [nl, 1024, D])
    w_brb = din("w_br_b", [nl, 512, D])
    w_brc = din("w_br_c", [nl, 1024, D])
    w_o = din("w_o", [nl, D, D])
    w_gu = din("w_gate_up", [nl, D, 2 * DFF])
    w_dn = din("w_down", [nl, DFF, D])
    g1_in = din("g1", [128, nl * 16])
    g2_in = din("g2", [128, nl * 16])
    qkg_in = din("qkg", [128, nl * 6])
    sink_in = din("sink", [128, nl * 8])
    rope_in = din("rope", [3, 2, 128, T])
    mab_in = din("maskab", [128, n_ab * 128])
    cb_in = din("cbias", [nl, 8, 128, n_ct * 128])
    ident_in = din("ident", [128, 128])
    rot_in = din("rot", [128, 128])
    out = nc.dram_tensor("out", [T, D], F32, kind="ExternalOutput").ap()

    xres = dscr("xres", [16, 128, T], F32, dbg=True)
    qsc = dscr("qsc", [28, 128, T], BF16, dbg=True)
    gsc = dscr("gsc", [16, 128, 3, T], BF16, dbg=True)
    osc = dscr("osc", [20, 128, T], BF16, dbg=True)
    kv_own = nc.dram_tensor("kv_own", [NKV], BF16, kind="Internal").ap()
    MSG = {"a": 786432, "b": 1048576, "c": 655360, "d": 1048576, "e": 1048576}
    msg_src = {m: nc.dram_tensor("msg_src_" + m, [n], BF16, kind="Internal").ap() for m, n in MSG.items()}
    msg_dst = {m: nc.dram_tensor("msg_dst_" + m, [2 * n], BF16, addr_space="Local", kind="Internal").ap()
               for m, n in MSG.items()}

    def blk(flat, off, rows, cols):
        return flat[off:off + rows * cols].rearrange("(p t) -> p t", t=cols)

    def ms(m, off, rows, cols):
        return blk(msg_src[m], off, rows, cols)

    def md(m, rank, off, rows, cols):
        return blk(msg_dst[m], rank * MSG[m] + off, rows, cols)

    def kv_fm(base, off, ncols=T):
        return base[off:off + 128 * ncols].rearrange("(p t) -> p t", t=ncols)

    def kv_tm(base, off, nrows, ncols):
        return base[off:off + nrows * ncols].rearrange("(r n) -> r n", n=ncols)

    es = contextlib.ExitStack()
    with es:
        def sb(name, shape, dt):
            return es.enter_context(nc.sbuf_tensor("sb_" + name, list(shape), dt))

        Hs = sb("H", [128, 16, T], BF16)
        H_t = [[Tile() for _ in range(4)] for _ in range(16)]
        NB = 3
        ring = [sb(f"ring{i}", [128, 8192], BF16) for i in range(NB)]
        ring_t = [Tile() for _ in range(NB)]
        WORK_BYTES = 84 * 1024
        work_s = sb("work", [128, WORK_BYTES // 2], BF16)
        work = Work(work_s, WORK_BYTES)
        ident = sb("ident_sb", [128, 128], F32)
        ones = sb("ones", [128, 128], BF16)
        rotm = sb("rotm", [128, 128], BF16)
        g1s = sb("g1s", [128, nl * 16], F32)
        g2s = sb("g2s", [128, nl * 16], F32)
        qkgs = sb("qkgs", [128, nl * 6], F32)
        sinke = sb("sinke", [128, nl * 8], F32)
        mab = sb("mab", [128, n_ab * 128], BF16)
        T_const = Tile()
        ps = [es.enter_context(nc.psum_tensor(f"ps{i}", [128, 512], F32)) for i in range(8)]
        ps_t = [Tile() for _ in range(8)]
        acc_pool = Cyc([0, 1, 2, 3])
        aux_pool = Cyc([4, 5, 6, 7])

        xres_t = [[Tile() for _ in range(4)] for _ in range(16)]
        qsc_t = [[Tile() for _ in range(4)] for _ in range(28)]
        gsc_t = [[[Tile() for _ in range(4)] for _ in range(3)] for _ in range(16)]
        osc_t = [[Tile() for _ in range(4)] for _ in range(20)]
        kvo_t = {}
        msg_t = {m: {} for m in MSG}
        msgd_t = {m: Tile() for m in MSG}

        def mt(m, key):
            if key not in msg_t[m]:
                msg_t[m][key] = Tile()
            return msg_t[m][key]
        out_t = []

        def kvt(key):
            if key not in kvo_t:
                kvo_t[key] = Tile()
            return kvo_t[key]

        def mm(o, lhsT, rhs, start, stop, rd, wr):
            P.op("pe", lambda e: e.matmul(o, lhsT, rhs, start=start, stop=stop), reads=rd, writes=wr)

        def tr(o, in_, rd, wr):
            P.op("pe", lambda e: e.transpose(o, in_, ident[:]), reads=rd + [T_const], writes=wr)

        def act(o, in_, func, rd, wr, scale=None, bias=None):
            if scale is None:
                P.op("act", lambda e: e.activation(o, in_, func), reads=rd, writes=wr)
            elif bias is None:
                P.op("act", lambda e: e.activation(o, in_, func, scale=scale), reads=rd, writes=wr)
            else:
                P.op("act", lambda e: e.activation(o, in_, func, bias=bias, scale=scale), reads=rd, writes=wr)

        def rstd(rs, src_ps, src_tile, n):
            act(rs.ap, src_ps, AF.Ln, [src_tile], rs.tiles, scale=1.0 / n, bias=float(EPS))
            act(rs.ap, rs.ap, AF.Exp, rs.tiles, rs.tiles, scale=-0.5)

        def recip_act(o_ap, o_tiles, in_ap, in_tiles, bias=None):
            if bias is None:
                act(o_ap, in_ap, AF.Ln, in_tiles, o_tiles)
            else:
                act(o_ap, in_ap, AF.Ln, in_tiles + [T_const], o_tiles, scale=1.0, bias=bias)
            act(o_ap, o_ap, AF.Exp, o_tiles, o_tiles, scale=-1.0)

        def tt(eng, o, a, b, op, rd, wr):
            P.op(eng, lambda e: e.tensor_tensor(o, a, b, op), reads=rd, writes=wr)

        def ts(eng, o, a, s1, s2, op0, op1, rd, wr):
            if op1 is None:
                P.op(eng, lambda e: e.tensor_scalar(o, a, s1, None, op0), reads=rd, writes=wr)
            else:
                P.op(eng, lambda e: e.tensor_scalar(o, a, s1, s2, op0, op1), reads=rd, writes=wr)

        def stt(o, a, s, b, op0, op1, rd, wr):
            P.op("dve", lambda e: e.scalar_tensor_tensor(o, a, s, b, op0, op1), reads=rd, writes=wr)

        def dma(eng, o, in_, rd, wr):
            P.op(eng, lambda e: e.dma_start(out=o, in_=in_), reads=rd, writes=wr, kind="dma")

        wspecs = []

        def wv(src2d, p=128):
            return src2d.rearrange("(k p) n -> p k n", p=p)

        IN_ORDER = ([("kv_a", C_KA)] + [("kb", C_KB + 512 * g) for g in range(3)]
                    + [("vb", C_VB + 512 * g) for g in range(3)]
                    + [("kc", C_KC), ("kc", C_KC + 512), ("vc", C_VC), ("vc", C_VC + 512)]
                    + [("qa", C_QA), ("qa", C_QA + 512)] + [("qb", C_QB + 512 * g) for g in range(3)]
                    + [("qc", C_QC), ("qc", C_QC + 512)] + [("gate", C_G + 512 * i) for i in range(12)])
        for l in range(nl):
            for tag, c0 in IN_ORDER:
                wspecs.append(("in", [((0, 16, 512), wv(w_in[l])[:, :, c0:c0 + 512])]))
            for wbr, nk in ((w_bra, 8), (w_brb, 4), (w_brc, 8)):
                for cg in range(4):
                    wspecs.append(("br", [((0, nk, 512), wv(wbr[l])[:, :, cg * 512:(cg + 1) * 512])]))
            for cg in range(4):
                wspecs.append(("wo", [((0, 16, 512), wv(w_o[l])[:, :, cg * 512:(cg + 1) * 512])]))
            for (j0, nq) in QUARTERS:
                for jj in range(0, nq, 2):
                    j = j0 + jj
                    wspecs.append(("gu", [((0, 16, 256), wv(w_gu[l])[:, :, j * 128:j * 128 + 256]),
                                          ((4096, 16, 256), wv(w_gu[l])[:, :, DFF + j * 128:DFF + j * 128 + 256])]))
                for cg in range(4):
                    wspecs.append(("dn", [((0, nq, 512), wv(w_dn[l])[:, j0:j0 + nq, cg * 512:(cg + 1) * 512])]))
        ws_state = {"issued": 0, "cons": 0}

        def ws_issue(n):
            tag, dmas = wspecs[n]
            slot = n % NB
            for (lo, k, ncol), src in dmas:
                dst = ring[slot][:, lo:lo + k * ncol].rearrange("p (k n) -> p k n", n=ncol)
                dma("pool", dst, src, [], [ring_t[slot]])

        def ws_next(tag):
            n = ws_state["cons"]
            assert wspecs[n][0] == tag, (wspecs[n][0], tag)
            while ws_state["issued"] < min(n + NB, len(wspecs)):
                ws_issue(ws_state["issued"])
                ws_state["issued"] += 1
            ws_state["cons"] += 1
            return n % NB

        work.reset()
        c_tmp = work.alloc(nl * 16 * 4, F32)
        dma("sp", ident[:], ident_in[:, :], [], [T_const])
        dma("pool", rotm[:], rot_in[:, :], [], [T_const])
        dma("pool", mab[:], mab_in[:, :], [], [T_const])
        P.op("dve", lambda e: e.memset(ones[:], 1.0), writes=[T_const])
        dma("sp", c_tmp.ap[:, 0:nl * 16], g1_in[:, :], [], c_tmp.tiles)
        ts("dve", g1s[:], c_tmp.ap[:, 0:nl * 16], 1.0, None, ALU.mult, None, c_tmp.tiles, [T_const])
        dma("sp", c_tmp.ap[:, 0:nl * 16], g2_in[:, :], [], c_tmp.tiles)
        ts("dve", g2s[:], c_tmp.ap[:, 0:nl * 16], 1.0, None, ALU.mult, None, c_tmp.tiles, [T_const])
        dma("sp", c_tmp.ap[:, 0:nl * 6], qkg_in[:, :], [], c_tmp.tiles)
        ts("dve", qkgs[:], c_tmp.ap[:, 0:nl * 6], 1.0, None, ALU.mult, None, c_tmp.tiles, [T_const])
        dma("sp", c_tmp.ap[:, 0:nl * 8], sink_in[:, :], [], c_tmp.tiles)
        act(sinke[:], c_tmp.ap[:, 0:nl * 8], AF.Exp, c_tmp.tiles, [T_const])

        SS = [4, 5, 6, 7]

        def normalize(gs, gcol0):
            work.reset()
            rsb = [work.alloc(2048, F32) for _ in range(4)]
            xsb = Cyc([work.alloc(2048, F32) for _ in range(14)])
            for tg in range(4):
                rstd(rsb[tg], ps[SS[tg]][:], ps_t[SS[tg]], float(D))
            for tg in range(4):
                for fc in range(16):
                    xb = xsb.next()
                    dma("act", xb.ap, xres[fc][:, tg * 512:(tg + 1) * 512], [xres_t[fc][tg]], xb.tiles)
                    stt(Hs[:, fc, tg * 512:(tg + 1) * 512], xb.ap, gs[:, gcol0 + fc:gcol0 + fc + 1], rsb[tg].ap,
                        ALU.mult, ALU.mult, xb.tiles + rsb[tg].tiles + [T_const], [H_t[fc][tg]])

        def phase0():
            work.reset()
            xin = Cyc([work.alloc(8192, F32) for _ in range(3)])
            xts = Cyc([work.alloc(16384, F32) for _ in range(2)])
            sqb = Cyc([work.alloc(8192, BF16) for _ in range(2)])
            pend0 = []
            for u in range(8):
                xt = xts.next()
                sq = sqb.next()
                xt3 = xt.ap.rearrange("p (c t) -> p c t", t=256)
                sq3 = sq.ap.rearrange("p (c t) -> p c t", t=256)
                for half in range(2):
                    tt_ = 2 * u + half
                    tg = tt_ // 4
                    xi = xin.next()
                    dma("act", xi.ap, x_in[tt_ * 128:(tt_ + 1) * 128, :], [], xi.tiles)
                    for b_ in range(4):
                        bank = acc_pool.next()
                        for i in range(4):
                            fc = b_ * 4 + i
                            tr(ps[bank][:, i * 128:(i + 1) * 128], xi.ap[:, fc * 128:(fc + 1) * 128], xi.tiles, [ps_t[bank]])
                        dstv = xt3[:, b_ * 4:(b_ + 1) * 4, half * 128:(half + 1) * 128]
                        srcv = ps[bank][:].rearrange("p (c t) -> p c t", t=128)
                        P.op("dve", lambda e, dstv=dstv, srcv=srcv: e.tensor_copy(dstv, srcv),
                             reads=[ps_t[bank]], writes=xt.tiles)
                        act(sq3[:, b_ * 4:(b_ + 1) * 4, half * 128:(half + 1) * 128], dstv, AF.Square, xt.tiles, sq.tiles)

                    def stats(tt_=tt_, tg=tg, sq3=sq3, sq=sq, half=half):
                        for fc in range(16):
                            mm(ps[SS[tg]][:, (tt_ % 4) * 128:(tt_ % 4 + 1) * 128], ones[:], sq3[:, fc, half * 128:(half + 1) * 128],
                               fc == 0, fc == 15, sq.tiles + [T_const], [ps_t[SS[tg]]])
                    pend0.append(stats)
                    if len(pend0) > 1:
                        pend0.pop(0)()
                tg = (2 * u) // 4
                dst = xres.rearrange("c p t -> p c t")[:, :, u * 256:(u + 1) * 256]
                dma("sp", dst, xt3, xt.tiles, [xres_t[fc][tg] for fc in range(16)])
            while pend0:
                pend0.pop(0)()

        def tokview(kc, variant, tg):
            base = Hs[:, kc, :]
            if variant == 0:
                return base[:, tg * 512:(tg + 1) * 512]
            if variant == 1:
                return base.rearrange("p (m r) -> p r m", r=4)[:, tg, :]
            return base.rearrange("p (m r) -> p r m", r=16)[:, 4 * tg:4 * tg + 4, :]

        def tokview128(kc, variant, tt_):
            base = Hs[:, kc, :]
            if variant == 0:
                return base[:, tt_ * 128:(tt_ + 1) * 128]
            if variant == 1:
                r, m0 = tt_ // 4, (tt_ % 4) * 128
                return base.rearrange("p (m r) -> p r m", r=4)[:, r, m0:m0 + 128]
            return base.rearrange("p (m r) -> p r m", r=16)[:, tt_, :]

        def Hall(kc):
            return [H_t[kc][0], H_t[kc][1], H_t[kc][2], H_t[kc][3]]

        def phase1(l):
            work.reset()
            sqb = Cyc([work.alloc(1024, BF16) for _ in range(6)])
            rsb = Cyc([work.alloc(2048, F32) for _ in range(3)])
            qnb = Cyc([work.alloc(1024, BF16) for _ in range(6)])
            t1b = Cyc([work.alloc(2048, F32) for _ in range(5)])
            t2b = Cyc([work.alloc(2048, F32) for _ in range(3)])
            ostb = Cyc([work.alloc(1024, BF16) for _ in range(6)])
            csb = Cyc([work.alloc(4096, F32) for _ in range(5)])
            vstb = Cyc([work.alloc(1024, BF16) for _ in range(6)])
            accb = Cyc([work.alloc(2048, F32) for _ in range(6)])

            def fm_block(slot, blk, kind, variant, gcol, rope, dsts_fn):
                cc_tick()
                wt = ring[slot][:, :].rearrange("p (k n) -> p k n", n=512)
                for tgp in range(2):
                    banks = [acc_pool.next(), acc_pool.next()]
                    for t_ in range(2):
                        for kc in range(16):
                            tg = tgp * 2 + t_
                            o = ps[banks[t_]][:]
                            rhs = tokview(kc, variant, tg)
                            if variant == 2:
                                o = o.rearrange("p (r m) -> p r m", r=4)
                            mm(o, wt[:, kc, blk * 128:(blk + 1) * 128], rhs, kc == 0, kc == 15,
                               [ring_t[slot]] + (Hall(kc) if variant else [H_t[kc][tg]]), [ps_t[banks[t_]]])
                    for t_ in range(2):
                        pend.append([tile_stages(kind, variant, gcol, rope, dsts_fn, tgp * 2 + t_, banks[t_]), 0])
                    step()

            pend = []

            def step():
                for ent in list(pend):
                    ent[0][ent[1]]()
                    ent[1] += 1
                    if ent[1] >= len(ent[0]):
                        pend.remove(ent)

            def flush():
                while pend:
                    step()

            def tile_stages(kind, variant, gcol, rope, dsts_fn, tg, bank):
                accp = ps[bank][:]
                stt_ = {}

                def store(ost):
                    for (d_ap, lo, hi, d_tiles) in dsts_fn(tg):
                        dma("sp", d_ap, ost.ap[:, lo:hi], ost.tiles, d_tiles)

                if kind == "gate":
                    def g0():
                        ost = ostb.next()
                        act(ost.ap, accp, AF.Sigmoid, [ps_t[bank]], ost.tiles)
                        store(ost)
                    return [g0]

                def s0():
                    sq = sqb.next()
                    act(sq.ap, accp, AF.Square, [ps_t[bank]], sq.tiles)
                    acs = accb.next()
                    act(acs.ap, accp, AF.Copy, [ps_t[bank]], acs.tiles)
                    stt_["sq"], stt_["acs"] = sq, acs

                def s1():
                    sq, acs = stt_["sq"], stt_["acs"]
                    ab = aux_pool.next()
                    mm(ps[ab][:], ones[:], sq.ap, True, True, sq.tiles + [T_const], [ps_t[ab]])
                    rs = rsb.next()
                    rstd(rs, ps[ab][:], ps_t[ab], 128.0)
                    if not rope:
                        ost = ostb.next()
                        stt(ost.ap, acs.ap, qkgs[:, gcol:gcol + 1], rs.ap, ALU.mult, ALU.mult,
                            acs.tiles + [T_const] + rs.tiles, ost.tiles)
                        store(ost)
                    else:
                        qn = qnb.next()
                        stt(qn.ap, acs.ap, qkgs[:, gcol:gcol + 1], rs.ap, ALU.mult, ALU.mult,
                            acs.tiles + [T_const] + rs.tiles, qn.tiles)
                        cs = csb.next()
                        csv = cs.ap.rearrange("p (c t) -> p c t", c=2)
                        dma("act", csv, rope_in[variant].rearrange("c p t -> p c t")[:, :, tg * 512:(tg + 1) * 512],
                            [], cs.tiles)
                        t1 = t1b.next()
                        tt("pool", t1.ap, qn.ap, csv[:, 0, :], ALU.mult, qn.tiles + cs.tiles, t1.tiles)
                        stt_["qn"], stt_["cs"], stt_["csv"], stt_["t1"] = qn, cs, csv, t1

                def s2():
                    qn, cs, csv, t1 = stt_["qn"], stt_["cs"], stt_["csv"], stt_["t1"]
                    rb = aux_pool.next()
                    mm(ps[rb][:], rotm[:], qn.ap, True, True, qn.tiles + [T_const], [ps_t[rb]])
                    t2 = t2b.next()
                    tt("dve", t2.ap, ps[rb][:], csv[:, 1, :], ALU.mult, [ps_t[rb]] + cs.tiles, t2.tiles)
                    ost = ostb.next()
                    tt("pool", ost.ap, t1.ap, t2.ap, ALU.add, t1.tiles + t2.tiles, ost.tiles)
                    store(ost)
                return [s0, s1, s2] if rope else [s0, s1]

            def tm_block(slot, c_lo, ncols, variant, dsts_fn):
                cc_tick()
                wt = ring[slot][:, :].rearrange("p (k n) -> p k n", n=512)
                for tt_ in range(16):
                    bank = acc_pool.next()
                    for kc in range(16):
                        mm(ps[bank][:, 0:ncols], tokview128(kc, variant, tt_), wt[:, kc, c_lo:c_lo + ncols], kc == 0, kc == 15,
                           [ring_t[slot]] + Hall(kc), [ps_t[bank]])
                    step()
                    vs = vstb.next()
                    act(vs.ap[:, 0:ncols], ps[bank][:, 0:ncols], AF.Copy, [ps_t[bank]], vs.tiles)
                    for (d_ap, d_tiles) in dsts_fn(tt_):
                        dma("sp", d_ap, vs.ap[:, 0:ncols], vs.tiles, d_tiles)

            def kown(off, key, tg):
                return (kv_fm(kv_own, off)[:, tg * 512:(tg + 1) * 512], 0, 512, [kvt((key, tg))])

            def q_dst(idx):
                return lambda tg: [(qsc[idx][:, tg * 512:(tg + 1) * 512], 0, 512, [qsc_t[idx][tg]])]

            ccq = []

            def cc(m):
                ccq.append([m, 2])

            def cc_tick():
                for ent in list(ccq):
                    ent[1] -= 1
                    if ent[1] <= 0:
                        ccq.remove(ent)
                        cc_emit(ent[0])

            def cc_emit(m):
                P.op("pool", lambda e: e.collective_compute(
                    "AllGather", ALU.bypass, replica_groups=pairs,
                    ins=[msg_src[m].rearrange("(a b) -> a b", b=1024)],
                    outs=[msg_dst[m].rearrange("(a b) -> a b", b=1024)]),
                    reads=list(msg_t[m].values()), writes=[msgd_t[m]], kind="cc")

            gq = l * 6
            slot = ws_next("in")
            for h in range(2):
                def d_ka(tg, h=h):
                    r = [kown(O_KA + h * 128 * T, ("ka", h), tg)]
                    if tg == 0:
                        r.append((ms("a", (0 * 2 + h) * 16384, 128, 128), 0, 128, [mt("a", ("ka", 0, h))]))
                    if tg == 3:
                        r.append((ms("a", (1 * 2 + h) * 16384, 128, 128), 384, 512, [mt("a", ("ka", 1, h))]))
                    return r
                fm_block(slot, h, "k", 0, gq + 1, True, d_ka)

            def d_va(tt_):
                r = [(kv_tm(kv_own, O_VA, T, 256)[tt_ * 128:(tt_ + 1) * 128, :], [kvt(("va", tt_))])]
                if tt_ == 0:
                    r.append((ms("a", 65536, 128, 256), [mt("a", ("va", 0))]))
                if tt_ == 15:
                    r.append((ms("a", 65536 + 32768, 128, 256), [mt("a", ("va", 1))]))
                return r
            tm_block(slot, 256, 256, 0, d_va)
            for g in range(3):
                slot = ws_next("in")
                for h in range(4):
                    def d_kb(tg, g=g, h=h):
                        if g == 2:
                            return [(ms("b", h * 128 * T, 128, T)[:, tg * 512:(tg + 1) * 512], 0, 512, [mt("b", (h, tg))])]
                        r = [kown(O_KB + (g * 4 + h) * 128 * T, ("kb", g, h), tg)]
                        if g == 0:
                            if tg == 0:
                                r.append((ms("a", 131072 + (0 * 4 + h) * 16384, 128, 128), 0, 128, [mt("a", ("kb0", 0, h))]))
                            if tg == 3:
                                r.append((ms("a", 131072 + (1 * 4 + h) * 16384, 128, 128), 384, 512, [mt("a", ("kb0", 1, h))]))
                        else:
                            for sd, lo in ((0, 0), (1, 384)):
                                v = ms("a", 262144 + (sd * 4 + h) * 65536, 128, 512)[:, tg * 128:(tg + 1) * 128]
                                r.append((v, lo, lo + 128, [mt("a", ("kb1", sd, h, tg))]))
                        return r
                    fm_block(slot, h, "k", g, gq + 3, True, d_kb)
                if g == 1:
                    cc("a")
                if g == 2:
                    cc("b")
            for g in range(3):
                slot = ws_next("in")

                def d_vb(tt_, g=g):
                    if g == 2:
                        return [(ms("d", 0, T, 512)[tt_ * 128:(tt_ + 1) * 128, :], [mt("d", tt_)])]
                    r = [(kv_tm(kv_own, O_VB + g * T * 512, T, 512)[tt_ * 128:(tt_ + 1) * 128, :], [kvt(("vb", g, tt_))])]
                    if g == 0:
                        if tt_ == 0:
                            r.append((ms("c", 0, 128, 512), [mt("c", ("vb0", 0))]))
                        if tt_ == 15:
                            r.append((ms("c", 65536, 128, 512), [mt("c", ("vb0", 1))]))
                    else:
                        rr, cc_ = tt_ // 4, tt_ % 4
                        if cc_ == 0:
                            r.append((ms("c", 131072 + (0 * 4 + rr) * 65536, 128, 512), [mt("c", ("vb1", 0, rr))]))
                        if cc_ == 3:
                            r.append((ms("c", 131072 + (1 * 4 + rr) * 65536, 128, 512), [mt("c", ("vb1", 1, rr))]))
                    return r
                tm_block(slot, 0, 512, g, d_vb)
                if g == 1:
                    cc("c")
                if g == 2:
                    cc("d")
            for half in range(2):
                slot = ws_next("in")
                for hh in range(4):
                    h = half * 4 + hh

                    def d_kc(tg, h=h):
                        r = [kown(O_KC + h * 128 * T, ("kc", h), tg)]
                        if tg == 0:
                            r.append((ms("e", (0 * 8 + h) * 32768, 128, 256), 0, 256, [mt("e", ("kc", 0, h))]))
                        if tg == 3:
                            r.append((ms("e", (1 * 8 + h) * 32768, 128, 256), 256, 512, [mt("e", ("kc", 1, h))]))
                        return r
                    fm_block(slot, hh, "k", 0, gq + 5, False, d_kc)
            for half in range(2):
                slot = ws_next("in")

                def d_vc(tt_, half=half):
                    r = [(kv_tm(kv_own, O_VC, T, 1024)[tt_ * 128:(tt_ + 1) * 128, half * 512:(half + 1) * 512],
                          [kvt(("vc", half, tt_))])]
                    if tt_ < 2:
                        r.append((ms("e", 524288, 256, 1024)[tt_ * 128:(tt_ + 1) * 128, half * 512:(half + 1) * 512],
                                  [mt("e", ("vc", 0, half, tt_))]))
                    if tt_ >= 14:
                        r.append((ms("e", 524288 + 262144, 256, 1024)[(tt_ - 14) * 128:(tt_ - 13) * 128, half * 512:(half + 1) * 512],
                                  [mt("e", ("vc", 1, half, tt_))]))
                    return r
                tm_block(slot, 0, 512, 0, d_vc)
            cc("e")
            for half in range(2):
                slot = ws_next("in")
                for hh in range(4):
                    fm_block(slot, hh, "q", 0, gq + 0, True, q_dst(half * 4 + hh))
            for g in range(3):
                slot = ws_next("in")
                for h in range(4):
                    fm_block(slot, h, "q", g, gq + 2, True, q_dst(8 + g * 4 + h))
            for half in range(2):
                slot = ws_next("in")
                for hh in range(4):
                    fm_block(slot, hh, "q", 0, gq + 4, False, q_dst(20 + half * 4 + hh))
            for gi in range(12):
                slot = ws_next("in")
                for hh in range(4):
                    blk = gi * 4 + hh
                    i, fc = blk // 16, blk % 16
                    fm_block(slot, hh, "gate", 0, 0, False,
                             lambda tg, i=i, fc=fc: [(gsc[fc][:, i, tg * 512:(tg + 1) * 512], 0, 512, [gsc_t[fc][i][tg]])])
            flush()
            while ccq:
                cc_emit(ccq.pop(0)[0])

        def attn_setup(esz=1024, ne=4, nrd=2):
            work.reset()
            st = {}
            st["E"] = Cyc([work.alloc(esz, BF16) for _ in range(ne)])
            st["rd"] = Cyc([work.alloc(2048, F32) for _ in range(nrd)])
            st["ost"] = Cyc([work.alloc(1024, BF16) for _ in range(6)])
            st["S1"] = Cyc([0, 1, 2, 3])
            st["S2"] = Cyc([(0, 1), (2, 3)])
            st["OT"] = Cyc([4, 5])
            st["D"] = Cyc([6, 7])
            return st

        def unit_scores(st, u):
            chunks, mask = u["chunks"], u["mask"]
            n = len(chunks)
            if n <= 4:
                sb_ = (st["S1"].next(),)
            else:
                sb_ = st["S2"].next()
            used = sorted(set(i // 4 for i in range(n)))
            for i, (k_ap, v_ap) in enumerate(chunks):
                b = sb_[i // 4]
                mm(ps[b][:, (i % 4) * 128:(i % 4 + 1) * 128], k_ap, u["q_ap"], True, True, u["kvt"] + u["q_tiles"], [ps_t[b]])
            e = st["E"].next()
            u["e"] = e
            for bi in used:
                b = sb_[bi]
                ncol = min(n - bi * 4, 4) * 128
                if mask[0] == "ab":
                    act(e.ap[:, bi * 512:bi * 512 + ncol], ps[b][:, 0:ncol], AF.Exp, [ps_t[b]], e.tiles, scale=SCALE)
                else:
                    tb = mask[3].next()
                    boff = (mask[2] + bi * 4) * 128
                    stt(tb.ap[:, 0:ncol], ps[b][:, 0:ncol], SCALE, mask[1].ap[:, boff:boff + ncol], ALU.mult, ALU.add,
                        [ps_t[b]] + mask[1].tiles, tb.tiles)
                    act(e.ap[:, bi * 512:bi * 512 + ncol], tb.ap[:, 0:ncol], AF.Exp, tb.tiles, e.tiles)
            if mask[0] == "ab":
                moff = mask[1] * 128
                tt("dve", e.ap[:, 0:n * 128], e.ap[:, 0:n * 128], mab[:, moff:moff + n * 128], ALU.mult,
                   e.tiles + [T_const], e.tiles)

        def unit_pv(st, u):
            chunks, e, col = u["chunks"], u["e"], u["col"]
            n = len(chunks)
            ob, db = st["cur_ot"], st["cur_d"]
            for i, (k_ap, v_ap) in enumerate(chunks):
                mm(ps[ob][:, col * 128:(col + 1) * 128], v_ap, e.ap[:, i * 128:(i + 1) * 128], i == 0, i == n - 1,
                   u["kvt"] + e.tiles, [ps_t[ob]])
            for i in range(n):
                mm(ps[db][:, col * 128:(col + 1) * 128], ones[:], e.ap[:, i * 128:(i + 1) * 128], i == 0, i == n - 1,
                   e.tiles + [T_const], [ps_t[db]])

        def run_units(st, units, finalize, lag):
            n = len(units)
            for i in range(n + lag):
                if i < n:
                    unit_scores(st, units[i])
                j = i - lag
                if j >= 0:
                    if j % 4 == 0:
                        begin_batch(st)
                    unit_pv(st, units[j])
                    if j % 4 == 3:
                        finalize(j // 4)

        def drive(gens):
            gens = list(gens)
            next(gens[0])
            for i, g_ in enumerate(gens):
                if i + 1 < len(gens):
                    next(gens[i + 1])
                for _ in g_:
                    pass

        def begin_batch(st):
            st["cur_ot"] = st["OT"].next()
            st["cur_d"] = st["D"].next()

        def load_kv(K, V, own, halos):
            k_src, k_tiles, v_src, v_tiles = own
            dma("sp", K["own"].ap, k_src, k_tiles, K["own"].tiles)
            dma("sp", V["own"].ap.rearrange("p (c d) -> p c d", d=128), v_src, v_tiles, V["own"].tiles)
            kL, kR, vL, vR, h_tiles = halos
            ncol = kL.shape[1] if len(kL.shape) == 2 else kL.shape[1] * kL.shape[2]
            dma("sp", K["L"].ap[:, 0:ncol], kL, h_tiles, K["L"].tiles)
            dma("sp", K["R"].ap[:, 0:ncol], kR, h_tiles, K["R"].tiles)
            nch = vL.shape[1]
            dma("sp", V["L"].ap[:, 0:nch * 128].rearrange("p (c d) -> p c d", d=128), vL, h_tiles, V["L"].tiles)
            dma("sp", V["R"].ap[:, 0:nch * 128].rearrange("p (c d) -> p c d", d=128), vR, h_tiles, V["R"].tiles)

        def alloc_kv(nh, hw):
            K = {"own": work.alloc(4096), "L": work.alloc(nh * hw * 2), "R": work.alloc(nh * hw * 2), "nh": nh, "hw": hw}
            V = {"own": work.alloc(4096), "L": work.alloc(nh * hw * 2), "R": work.alloc(nh * hw * 2)}
            return K, V

        def kva(rank, off):
            return rank * NKV + off

        def attention_a(l):
            st = attn_setup()
            sets = Cyc([alloc_kv(1, 128) for _ in range(2)])
            qb = Cyc([work.alloc(4096) for _ in range(2)])
            kvs = {}

            def group(kh, g):
                if g == 0:
                    K, V = sets.next()
                    va3 = lambda base: base.rearrange("(c p) n -> p c n", p=128)[:, :, kh * 128:(kh + 1) * 128]
                    load_kv(K, V,
                            (kv_fm(kv_own, O_KA + kh * 128 * T), [kvt((("ka", kh), tg)) for tg in range(4)],
                             va3(kv_tm(kv_own, O_VA, T, 256)), [kvt(("va", t_)) for t_ in range(16)]),
                            (md("a", 0, (1 * 2 + kh) * 16384, 128, 128), md("a", 1, (0 * 2 + kh) * 16384, 128, 128),
                             va3(md("a", 0, 65536 + 32768, 128, 256)), va3(md("a", 1, 65536, 128, 256)), [msgd_t["a"]]))
                    kvs[kh] = (K, V)
                K, V = kvs[kh]
                h = kh * 4 + g
                q = qb.next()
                dma("sp", q.ap, qsc[h][:, :], qsc_t[h], q.tiles)
                yield
                kvtiles = K["own"].tiles + K["L"].tiles + K["R"].tiles + V["own"].tiles + V["L"].tiles + V["R"].tiles

                def kch(c):
                    if c < 0:
                        return K["L"].ap[:, 0:128], V["L"].ap[:, 0:128]
                    if c > 15:
                        return K["R"].ap[:, 0:128], V["R"].ap[:, 0:128]
                    return K["own"].ap[:, c * 128:(c + 1) * 128], V["own"].ap[:, c * 128:(c + 1) * 128]
                units = [dict(q_ap=q.ap[:, j * 128:(j + 1) * 128], q_tiles=q.tiles, chunks=[kch(j + dl) for dl in (-1, 0, 1)],
                              kvt=kvtiles, mask=("ab", sigA[j]), col=j % 4) for j in range(16)]

                def fin_a(jb):
                    ob, db = st["cur_ot"], st["cur_d"]
                    rd = st["rd"].next()
                    recip_act(rd.ap, rd.tiles, ps[db][:], [ps_t[db]], bias=sinke[:, l * 8 + h:l * 8 + h + 1])
                    os_ = st["ost"].next()
                    tt("dve", os_.ap, ps[ob][:], rd.ap, ALU.mult, [ps_t[ob]] + rd.tiles, os_.tiles)
                    dma("sp", osc[h][:, jb * 512:(jb + 1) * 512], os_.ap, os_.tiles, [osc_t[h][jb]])
                run_units(st, units, fin_a, 3)
            drive([group(kh, g) for kh in range(2) for g in range(4)])

        def attention_b(l):
            st = attn_setup(1024, 4, 1)
            num = work.alloc(8192, F32)
            den = work.alloc(8192, F32)
            sets = Cyc([alloc_kv(16, 128) for _ in range(2)])
            qb = Cyc([work.alloc(4096) for _ in range(2)])
            def group(h, g):
                    dil = (1, 4, 16)[g]
                    ncls = dil
                    cl = T // dil
                    K, V = sets.next()
                    K = dict(K)
                    K["nh"] = ncls
                    ncc = cl // 128
                    hc = slice(h * 128, (h + 1) * 128)
                    tm3 = lambda base: base.rearrange("(c p) n -> p c n", p=128)[:, :, hc]
                    if g == 2:
                        own = (ms("b", h * 128 * T, 128, T), [mt("b", (h, tg)) for tg in range(4)],
                               tm3(ms("d", 0, T, 512)), [mt("d", t_) for t_ in range(16)])
                        halos = (md("b", 0, h * 128 * T, 128, T), md("b", 1, h * 128 * T, 128, T),
                                 tm3(md("d", 0, 0, T, 512)), tm3(md("d", 1, 0, T, 512)), [msgd_t["b"], msgd_t["d"]])
                    else:
                        own = (kv_fm(kv_own, O_KB + (g * 4 + h) * 128 * T), [kvt((("kb", g, h), tg)) for tg in range(4)],
                               tm3(kv_tm(kv_own, O_VB + g * T * 512, T, 512)), [kvt(("vb", g, t_)) for t_ in range(16)])
                        if g == 0:
                            halos = (md("a", 0, 131072 + (1 * 4 + h) * 16384, 128, 128), md("a", 1, 131072 + (0 * 4 + h) * 16384, 128, 128),
                                     tm3(md("c", 0, 65536, 128, 512)), tm3(md("c", 1, 0, 128, 512)), [msgd_t["a"], msgd_t["c"]])
                        else:
                            halos = (md("a", 0, 262144 + (1 * 4 + h) * 65536, 128, 512), md("a", 1, 262144 + (0 * 4 + h) * 65536, 128, 512),
                                     tm3(md("c", 0, 131072 + 4 * 65536, 512, 512)), tm3(md("c", 1, 131072, 512, 512)),
                                     [msgd_t["a"], msgd_t["c"]])
                    load_kv(K, V, own, halos)
                    kvtiles = K["own"].tiles + K["L"].tiles + K["R"].tiles + V["own"].tiles + V["L"].tiles + V["R"].tiles
                    q = qb.next()
                    qi = 8 + g * 4 + h
                    dma("sp", q.ap, qsc[qi][:, :], qsc_t[qi], q.tiles)
                    yield

                    def kch(r, c):
                        if c < 0:
                            return K["L"].ap[:, r * 128:(r + 1) * 128], V["L"].ap[:, r * 128:(r + 1) * 128]
                        if c >= ncc:
                            return K["R"].ap[:, r * 128:(r + 1) * 128], V["R"].ap[:, r * 128:(r + 1) * 128]
                        o_ = (r * ncc + c) * 128
                        return K["own"].ap[:, o_:o_ + 128], V["own"].ap[:, o_:o_ + 128]
                    units = []
                    for bt in range(4):
                        for jj in range(4):
                            if g == 0:
                                r, j = 0, bt * 4 + jj
                            elif g == 1:
                                r, j = bt, jj
                            else:
                                r, j = bt * 4 + jj, 0
                            qo = (r * ncc + j) * 128
                            units.append(dict(q_ap=q.ap[:, qo:qo + 128], q_tiles=q.tiles, chunks=[kch(r, j + dl) for dl in (-1, 0, 1)],
                                              kvt=kvtiles, mask=("ab", sigB[g][j]), col=jj))

                    def fin_b(bt):
                        ob, db = st["cur_ot"], st["cur_d"]
                        if g == 0:
                            nv = num.ap[:, bt * 512:(bt + 1) * 512]
                            dv = den.ap[:, bt * 512:(bt + 1) * 512]
                            act(nv, ps[ob][:], AF.Copy, [ps_t[ob]], num.tiles)
                            act(dv, ps[db][:], AF.Copy, [ps_t[db]], den.tiles)
                        else:
                            if g == 1:
                                nv = num.ap.rearrange("p (m r) -> p r m", r=4)[:, bt, :]
                                dv = den.ap.rearrange("p (m r) -> p r m", r=4)[:, bt, :]
                                po, pd = ps[ob][:], ps[db][:]
                            else:
                                nv = num.ap.rearrange("p (m r) -> p r m", r=16)[:, bt * 4:bt * 4 + 4, :]
                                dv = den.ap.rearrange("p (m r) -> p r m", r=16)[:, bt * 4:bt * 4 + 4, :]
                                po = ps[ob][:].rearrange("p (r m) -> p r m", r=4)
                                pd = ps[db][:].rearrange("p (r m) -> p r m", r=4)
                            tt("dve", nv, nv, po, ALU.add, [ps_t[ob]] + num.tiles, num.tiles)
                            tt("dve", dv, dv, pd, ALU.add, [ps_t[db]] + den.tiles, den.tiles)
                    run_units(st, units, fin_b, 3)
                    if g < 2:
                        return
                    for bt in range(4):
                        dv = den.ap[:, bt * 512:(bt + 1) * 512]
                        recip_act(dv, den.tiles, dv, den.tiles)
                        os_ = st["ost"].next()
                        tt("dve", os_.ap, num.ap[:, bt * 512:(bt + 1) * 512], dv, ALU.mult, num.tiles + den.tiles, os_.tiles)
                        dma("sp", osc[8 + h][:, bt * 512:(bt + 1) * 512], os_.ap, os_.tiles, [osc_t[8 + h][bt]])
            drive([group(h, g) for h in range(4) for g in range(3)])

        def attention_c(l):
            st = attn_setup(2048, 3)
            tbc = Cyc([work.alloc(2048, F32) for _ in range(4)])
            cbb = Cyc([work.alloc(n_ct * 512, F32) for _ in range(2)])
            sets = Cyc([alloc_kv(1, 256) for _ in range(2)])
            qb = Cyc([work.alloc(4096) for _ in range(2)])
            def group(h):
                K, V = sets.next()
                hc = slice(h * 128, (h + 1) * 128)
                tm3 = lambda base: base.rearrange("(c p) n -> p c n", p=128)[:, :, hc]
                load_kv(K, V,
                        (kv_fm(kv_own, O_KC + h * 128 * T), [kvt((("kc", h), tg)) for tg in range(4)],
                         tm3(kv_tm(kv_own, O_VC, T, 1024)), [kvt(("vc", hf, t_)) for hf in range(2) for t_ in range(16)]),
                        (md("e", 0, (1 * 8 + h) * 32768, 128, 256), md("e", 1, (0 * 8 + h) * 32768, 128, 256),
                         tm3(md("e", 0, 524288 + 262144, 256, 1024)), tm3(md("e", 1, 524288, 256, 1024)), [msgd_t["e"]]))
                kvtiles = K["own"].tiles + K["L"].tiles + K["R"].tiles + V["own"].tiles + V["L"].tiles + V["R"].tiles
                cb = cbb.next()
                dma("sp", cb.ap, cb_in[l][h][:, :], [], cb.tiles)
                q = qb.next()
                dma("sp", q.ap, qsc[20 + h][:, :], qsc_t[20 + h], q.tiles)
                yield

                def kch(c):
                    if c < 0:
                        return K["L"].ap[:, (c + 2) * 128:(c + 3) * 128], V["L"].ap[:, (c + 2) * 128:(c + 3) * 128]
                    if c > 15:
                        return K["R"].ap[:, (c - 16) * 128:(c - 15) * 128], V["R"].ap[:, (c - 16) * 128:(c - 15) * 128]
                    return K["own"].ap[:, c * 128:(c + 1) * 128], V["own"].ap[:, c * 128:(c + 1) * 128]
                units = []
                for j in range(16):
                    off, dls = sigC[j]
                    units.append(dict(q_ap=q.ap[:, j * 128:(j + 1) * 128], q_tiles=q.tiles, chunks=[kch(j + dl) for dl in dls],
                                      kvt=kvtiles, mask=("c", cb, off, tbc), col=j % 4))

                def fin_c(jb):
                    ob, db = st["cur_ot"], st["cur_d"]
                    rd = st["rd"].next()
                    recip_act(rd.ap, rd.tiles, ps[db][:], [ps_t[db]])
                    os_ = st["ost"].next()
                    tt("dve", os_.ap, ps[ob][:], rd.ap, ALU.mult, [ps_t[ob]] + rd.tiles, os_.tiles)
                    dma("sp", osc[12 + h][:, jb * 512:(jb + 1) * 512], os_.ap, os_.tiles, [osc_t[12 + h][jb]])
                run_units(st, units, fin_c, 1)
            drive([group(h) for h in range(8)])

        def branch(l, i, o0, nk):
            work.reset()
            ob_ = work.alloc(nk * 4096)
            ov = ob_.ap.rearrange("p (k t) -> p k t", t=T)
            gstb = Cyc([work.alloc(1024, BF16) for _ in range(3)])
            tmpb = Cyc([work.alloc(2048, F32) for _ in range(2)])
            obt = [ob_.tiles[4 * k:4 * k + 4] for k in range(nk)]
            for k in range(nk):
                dma("act", ov[:, k, :], osc[o0 + k][:, :], osc_t[o0 + k], obt[k])
            for cg in range(4):
                slot = ws_next("br")
                wt = ring[slot][:, 0:nk * 512].rearrange("p (k n) -> p k n", n=512)
                for blk in range(4):
                    fc = cg * 4 + blk
                    for tgp in range(2):
                        banks = [acc_pool.next(), acc_pool.next()]
                        for kc in range(nk):
                            for t_ in range(2):
                                tg = tgp * 2 + t_
                                mm(ps[banks[t_]][:], wt[:, kc, blk * 128:(blk + 1) * 128], ov[:, kc, tg * 512:(tg + 1) * 512],
                                   kc == 0, kc == nk - 1, [ring_t[slot]] + obt[kc], [ps_t[banks[t_]]])
                        for t_ in range(2):
                            tg = tgp * 2 + t_
                            bank = banks[t_]
                            gs_ = gstb.next()
                            dma("act", gs_.ap, gsc[fc][:, i, tg * 512:(tg + 1) * 512], [gsc_t[fc][i][tg]], gs_.tiles)
                            hv = Hs[:, fc, tg * 512:(tg + 1) * 512]
                            if i == 0:
                                tt("dve", hv, ps[bank][:], gs_.ap, ALU.mult, [ps_t[bank]] + gs_.tiles, [H_t[fc][tg]])
                            else:
                                tm = tmpb.next()
                                tt("dve", tm.ap, ps[bank][:], gs_.ap, ALU.mult, [ps_t[bank]] + gs_.tiles, tm.tiles)
                                tt("pool", hv, hv, tm.ap, ALU.add, [H_t[fc][tg]] + tm.tiles, [H_t[fc][tg]])

        def resid_setup(nxs=8, nxn=5, nsq=4):
            return {"xs": Cyc([work.alloc(2048, F32) for _ in range(nxs)]),
                    "xn": Cyc([work.alloc(2048, F32) for _ in range(nxn)]),
                    "sq": Cyc([work.alloc(1024, BF16) for _ in range(nsq)]), "pend": [], "ld": {}, "plan": [], "nld": 0}

        def resid_plan(rs_, pairs_):
            rs_["plan"] = list(pairs_)
            rs_["nld"] = 0
            rs_["ld"] = {}

        def resid_prefetch(rs_, upto):
            while rs_["nld"] < min(upto + 1, len(rs_["plan"])):
                fc, tgp = rs_["plan"][rs_["nld"]]
                for t_ in range(2):
                    tg = tgp * 2 + t_
                    xb = rs_["xs"].next()
                    dma("act", xb.ap, xres[fc][:, tg * 512:(tg + 1) * 512], [xres_t[fc][tg]], xb.tiles)
                    rs_["ld"][(fc, tg)] = xb
                rs_["nld"] += 1

        def resid_update(rs_, bank, fc, tg, stats):
            xb = rs_["ld"].pop((fc, tg))
            xn = rs_["xn"].next()
            tt("dve", xn.ap, ps[bank][:], xb.ap, ALU.add, [ps_t[bank]] + xb.tiles, xn.tiles)
            dma("sp", xres[fc][:, tg * 512:(tg + 1) * 512], xn.ap, xn.tiles, [xres_t[fc][tg]])
            if stats:
                sq = rs_["sq"].next()
                act(sq.ap, xn.ap, AF.Square, xn.tiles, sq.tiles)
                rs_["pend"].append(lambda: mm(ps[SS[tg]][:], ones[:], sq.ap, fc == 0, fc == 15, sq.tiles + [T_const], [ps_t[SS[tg]]]))

        def resid_flush(rs_, keep=0):
            while len(rs_["pend"]) > keep:
                rs_["pend"].pop(0)()

        def phase_wo(l):
            work.reset()
            rs_ = resid_setup()
            resid_plan(rs_, [(fc, tgp) for fc in range(16) for tgp in range(2)])
            pi = 0
            resid_prefetch(rs_, 1)
            for cg in range(4):
                slot = ws_next("wo")
                wt = ring[slot][:, :].rearrange("p (k n) -> p k n", n=512)
                for blk in range(4):
                    fc = cg * 4 + blk
                    for tgp in range(2):
                        banks = [acc_pool.next(), acc_pool.next()]
                        for kc in range(16):
                            for t_ in range(2):
                                tg = tgp * 2 + t_
                                mm(ps[banks[t_]][:], wt[:, kc, blk * 128:(blk + 1) * 128], Hs[:, kc, tg * 512:(tg + 1) * 512],
                                   kc == 0, kc == 15, [ring_t[slot], H_t[kc][tg]], [ps_t[banks[t_]]])
                        resid_flush(rs_)
                        resid_prefetch(rs_, pi + 2)
                        for t_ in range(2):
                            resid_update(rs_, banks[t_], fc, tgp * 2 + t_, True)
                        pi += 1
            resid_flush(rs_)

        def phase_ffn(l, last):
            for qi, (j0, nq) in enumerate(QUARTERS):
                work.reset()
                actb = work.alloc(nq * 4096)
                av = actb.ap.rearrange("p (j t) -> p j t", t=T)
                sgb = Cyc([work.alloc(2048, F32) for _ in range(3)])
                rs_ = resid_setup(8, 5, 4)
                gu_banks = Cyc([0, 1, 2, 3, 4, 5, 6, 7])
                for jj in range(0, nq, 2):
                    slot = ws_next("gu")
                    wg = ring[slot][:, 0:4096].rearrange("p (k n) -> p k n", n=256)
                    wu = ring[slot][:, 4096:8192].rearrange("p (k n) -> p k n", n=256)
                    for b2 in range(2):
                        jq = jj + b2
                        for tgp in range(2):
                            bg = [gu_banks.next(), gu_banks.next()]
                            bu = [gu_banks.next(), gu_banks.next()]
                            for (wt, bb) in ((wg, bg), (wu, bu)):
                                for t_ in range(2):
                                    for kc in range(16):
                                        tg = tgp * 2 + t_
                                        mm(ps[bb[t_]][:], wt[:, kc, b2 * 128:(b2 + 1) * 128], Hs[:, kc, tg * 512:(tg + 1) * 512],
                                           kc == 0, kc == 15, [ring_t[slot], H_t[kc][tg]], [ps_t[bb[t_]]])
                            for t_ in range(2):
                                tg = tgp * 2 + t_
                                sg = sgb.next()
                                act(sg.ap, ps[bg[t_]][:], AF.Silu, [ps_t[bg[t_]]], sg.tiles)
                                tt("dve", av[:, jq, tg * 512:(tg + 1) * 512], ps[bu[t_]][:], sg.ap, ALU.mult,
                                   [ps_t[bu[t_]]] + sg.tiles, actb.tiles)
                stats = (qi == 3) and (not last)
                resid_plan(rs_, [(fc, tgp) for fc in range(16) for tgp in range(2)])
                pi = 0
                resid_prefetch(rs_, 1)
                for cg in range(4):
                    slot = ws_next("dn")
                    wt = ring[slot][:, 0:nq * 512].rearrange("p (k n) -> p k n", n=512)
                    for blk in range(4):
                        fc = cg * 4 + blk
                        for tgp in range(2):
                            banks = [acc_pool.next(), acc_pool.next()]
                            for jq in range(nq):
                                for t_ in range(2):
                                    tg = tgp * 2 + t_
                                    mm(ps[banks[t_]][:], wt[:, jq, blk * 128:(blk + 1) * 128], av[:, jq, tg * 512:(tg + 1) * 512],
                                       jq == 0, jq == nq - 1, [ring_t[slot]] + actb.tiles, [ps_t[banks[t_]]])
                            resid_flush(rs_)
                            resid_prefetch(rs_, pi + 2)
                            for t_ in range(2):
                                resid_update(rs_, banks[t_], fc, tgp * 2 + t_, stats)
                            pi += 1
                resid_flush(rs_)

        def phase_out():
            work.reset()
            xin = Cyc([work.alloc(16384, F32) for _ in range(3)])
            xo = Cyc([work.alloc(8192, F32) for _ in range(3)])
            for u in range(8):
                tg = (2 * u) // 4
                xi = xin.next()
                xi3 = xi.ap.rearrange("p (c t) -> p c t", t=256)
                dma("sp", xi3, xres.rearrange("c p t -> p c t")[:, :, u * 256:(u + 1) * 256],
                    [xres_t[fc][tg] for fc in range(16)], xi.tiles)
                for half in range(2):
                    tt_ = 2 * u + half
                    xt = xo.next()
                    for b_ in range(4):
                        bank = acc_pool.next()
                        for i in range(4):
                            fc = b_ * 4 + i
                            tr(ps[bank][:, i * 128:(i + 1) * 128], xi3[:, fc, half * 128:(half + 1) * 128], xi.tiles, [ps_t[bank]])
                        if b_ % 2 == 0:
                            P.op("dve", lambda e, bank=bank, b_=b_, xt=xt: e.tensor_copy(xt.ap[:, b_ * 512:(b_ + 1) * 512], ps[bank][:]),
                                 reads=[ps_t[bank]], writes=xt.tiles)
                        else:
                            act(xt.ap[:, b_ * 512:(b_ + 1) * 512], ps[bank][:], AF.Copy, [ps_t[bank]], xt.tiles)
                    t_o = Tile()
                    out_t.append(t_o)
                    dma("sp", out[tt_ * 128:(tt_ + 1) * 128, :], xt.ap, xt.tiles, [t_o])

        def body():
            phase0()
            if stop == "p0":
                return
            for l in range(nl):
                normalize(g1s, l * 16)
                if stop == "n1":
                    return
                phase1(l)
                if stop == "p1":
                    return
                attention_a(l)
                if stop == "aa":
                    return
                branch(l, 0, 0, 8)
                attention_b(l)
                if stop == "ab":
                    return
                branch(l, 1, 8, 4)
                attention_c(l)
                if stop == "ac":
                    return
                branch(l, 2, 12, 8)
                phase_wo(l)
                normalize(g2s, l * 16)
                if stop == "wo":
                    return
                phase_ffn(l, l == nl - 1)
        body()
        phase_out()
        P.op("sp", lambda e: e.nop(), reads=out_t)
        P.finish()
    return nc


_PROG_CACHE = {}


def _core_consts(par, nl, rpb_c):
    maskAB, sigA, sigB, cmask, cdr, cdc, sigC = _tables()
    cos, sin = _rope_tables()
    rope = _rope_core(par, cos, sin)
    n_ab = maskAB[par].shape[0]
    mab = np.ascontiguousarray(maskAB[par].transpose(1, 0, 2).reshape(128, n_ab * 128)).astype(np.float32)
    n_ct = cmask[par].shape[0]
    g = rpb_c[:nl][:, :, cdr[par], cdc[par]]
    g = np.where(cmask[par][None, None], g, np.float32(NEG)).astype(np.float32)
    cb = np.ascontiguousarray(g.transpose(0, 1, 3, 2, 4).reshape(nl, 8, 128, n_ct * 128))
    return rope, mab, cb


def _lay(v, nl, k):
    return np.ascontiguousarray(v[:nl].reshape(nl, k, 128).transpose(2, 0, 1).reshape(128, nl * k)).astype(np.float32)


def run(inputs, nl=NL, n_cores=8, debug=False, trace=False, stop=None):
    x = np.asarray(inputs["x"], np.float32)
    pairs = [[2 * i, 2 * i + 1] for i in range(n_cores // 2)]
    key = (nl, n_cores, debug, stop)
    if key not in _PROG_CACHE:
        _PROG_CACHE[key] = build_program(nl, pairs, debug, stop)
    nc = _PROG_CACHE[key]
    f = lambda n: np.ascontiguousarray(np.asarray(inputs[n], np.float32)[:nl])
    shared = {
        "w_in": f("w_in"), "w_br_a": f("w_br_a"), "w_br_b": f("w_br_b"), "w_br_c": f("w_br_c"),
        "w_o": f("w_o"), "w_gate_up": f("w_gate_up"), "w_down": f("w_down"),
        "g1": _lay(np.asarray(inputs["norm1_g"], np.float32), nl, 16),
        "g2": _lay(np.asarray(inputs["norm2_g"], np.float32), nl, 16),
        "qkg": np.ascontiguousarray(np.asarray(inputs["qk_norm_g"], np.float32)[:nl].transpose(2, 0, 1).reshape(128, nl * 6)),
        "sink": np.ascontiguousarray(np.broadcast_to(np.asarray(inputs["sink_a"], np.float32)[:nl].reshape(1, nl * 8), (128, nl * 8))),
        "ident": np.eye(128, dtype=np.float32),
    }
    rot = np.zeros((128, 128), np.float32)
    for m in range(64):
        rot[m + 64, m] = -1.0
        rot[m, m + 64] = 1.0
    shared["rot"] = rot
    rpb = np.asarray(inputs["rpb_c"], np.float32)
    per_par = [_core_consts(p, nl, rpb) for p in range(2)]
    in_maps = []
    for c in range(n_cores):
        b, par = c // 2, c % 2
        rope, mab, cb = per_par[par]
        m = dict(shared)
        m["x"] = np.ascontiguousarray(x[b, par * T:(par + 1) * T, :])
        m["rope"] = rope
        m["maskab"] = mab
        m["cbias"] = cb
        in_maps.append(m)
    res = run_bass_kernel_spmd(nc, in_maps, core_ids=list(range(n_cores)), trace=trace)
    nb = n_cores // 2
    outp = np.zeros((nb, SEQ, D), np.float32)
    for c in range(n_cores):
        outp[c // 2, (c % 2) * T:(c % 2 + 1) * T, :] = res.results[c]["out"]
    return outp, res


def kernel(x, norm1_g, w_in, qk_norm_g, sink_a, rpb_c, w_br_a, w_br_b, w_br_c, w_o, norm2_g, w_gate_up, w_down):
    inputs = dict(x=x, norm1_g=norm1_g, w_in=w_in, qk_norm_g=qk_norm_g, sink_a=sink_a, rpb_c=rpb_c, w_br_a=w_br_a,
                  w_br_b=w_br_b, w_br_c=w_br_c, w_o=w_o, norm2_g=norm2_g, w_gate_up=w_gate_up, w_down=w_down)
    outp, _ = run(inputs)
    return outp
```
